# Optimizing a Trainium2 kernel written in Bass

```python
import math
import jax
import jax.numpy as jnp
from jax import lax
import numpy as np

D_MODEL = 2048
BATCH = 4
SEQ = 4096
DEPTH = 4

N_MIXERS = 4
EPS = 1e-6
ROPE_THETA = 10000.0
NEG_INF = -1e30
SEL_BIG = 1e9

W_TOK = 2 * D_MODEL
MEM_TOKENS = 256
MEM_HEADS = 4
MEM_HEAD_DIM = D_MODEL // 8
W_MEM = MEM_HEADS * MEM_HEAD_DIM

POOL_WINDOWS = (2, 4, 8, 16)
POOL_GROUPS = 4
POOL_GW = W_TOK // POOL_GROUPS

NSA_HEADS = 32
NSA_KV_HEADS = 4
NSA_GROUP = NSA_HEADS // NSA_KV_HEADS
NSA_HEAD_DIM = 128
NSA_KV_W = NSA_KV_HEADS * NSA_HEAD_DIM
CMP_BLOCK = 32
CMP_STRIDE = 16
SEL_BLOCK = 64
SEL_TOPK = 16
WINDOW = 512
NSA_QBLOCK = 32

SSM_HEAD_DIM = 64
SSM_HEADS = W_TOK // SSM_HEAD_DIM
SSM_GROUPS = 8
SSM_STATE = 128
CONV_W = 4
SSD_CHUNK = 256
SSM_XBC_W = W_TOK + 2 * SSM_GROUPS * SSM_STATE

SGU_CHUNK = 128
SGU_GROUPS = 8
SGU_GW = W_TOK // SGU_GROUPS

POS_OFFSET_MAX = 2048

TOK_COLS = (2 * W_TOK,
            2 * W_TOK + 6 * NSA_KV_W + 3 * NSA_HEADS,
            W_TOK + SSM_XBC_W + SSM_HEADS,
            3 * W_TOK)

kernel_name = 'hybrid_interleaved_pool_nsa_ssd_sgu_memxattn'


def rms_norm(x, g):
    xf = x.astype(jnp.float32)
    y = xf * lax.rsqrt(jnp.mean(jnp.square(xf), axis=-1, keepdims=True) + EPS)
    return (y * g.astype(jnp.float32)).astype(x.dtype)


def masked_softmax(scores, mask):
    return jax.nn.softmax(jnp.where(mask, scores.astype(jnp.float32), NEG_INF), axis=-1)


def rope_tables(positions, dim):
    inv = ROPE_THETA ** (-jnp.arange(0, dim, 2, dtype=jnp.float32) / dim)
    ang = positions.astype(jnp.float32)[..., None] * inv
    return jnp.cos(ang), jnp.sin(ang)


def apply_rope(x, cos, sin):
    x1, x2 = jnp.split(x.astype(jnp.float32), 2, axis=-1)
    c = cos[:, :, None, :]
    s = sin[:, :, None, :]
    return jnp.concatenate([x1 * c - x2 * s, x2 * c + x1 * s], axis=-1).astype(x.dtype)


def causal_depthwise_conv(x, w, b):
    c = x.shape[-1]
    y = lax.conv_general_dilated(x, w[:, None, :].astype(x.dtype), window_strides=(1,),
                                 padding=[(CONV_W - 1, 0)],
                                 dimension_numbers=('NWC', 'WIO', 'NWC'),
                                 feature_group_count=c)
    return y + b.astype(x.dtype)


def memory_branch(cols, mem_n, mem_wkv, q_norm, k_norm):
    q, gate = jnp.split(cols, 2, axis=-1)
    B_, S_, _ = q.shape
    k, v = jnp.split(jnp.einsum('bmd,de->bme', mem_n, mem_wkv), 2, axis=-1)
    q = rms_norm(q.reshape(B_, S_, MEM_HEADS, MEM_HEAD_DIM), q_norm)
    k = rms_norm(k.reshape(B_, -1, MEM_HEADS, MEM_HEAD_DIM), k_norm)
    v = v.reshape(B_, -1, MEM_HEADS, MEM_HEAD_DIM)
    s = jnp.einsum('bqhd,bmhd->bhqm', q, k) * (MEM_HEAD_DIM ** -0.5)
    p = jax.nn.softmax(s.astype(jnp.float32), axis=-1).astype(v.dtype)
    o = jnp.einsum('bhqm,bmhd->bqhd', p, v).reshape(B_, S_, W_MEM)
    return o * jax.nn.silu(gate)


def pooling_mixer(cols, pool_w, pool_scale):
    v, gate = jnp.split(cols, 2, axis=-1)
    B_, S_, _ = v.shape
    vf = v.astype(jnp.float32).reshape(B_, S_, POOL_GROUPS, POOL_GW)
    cs = jnp.concatenate([jnp.zeros_like(vf[:, :1]), jnp.cumsum(vf, axis=1)], axis=1)
    t = np.arange(S_)
    win = np.array(POOL_WINDOWS)
    lo = np.maximum(t[:, None] + 1 - win[None, :], 0)
    cnt = (t[:, None] + 1 - lo).astype(np.float32)
    window_sum = cs[:, 1:] - cs[:, lo, np.arange(POOL_GROUPS)[None, :]]
    mix = window_sum / cnt[None, :, :, None] - vf
    out = jnp.einsum('bsgc,gcd->bsgd', mix, pool_w.astype(jnp.float32)).reshape(B_, S_, W_TOK)
    out = out * pool_scale.astype(jnp.float32)
    return out.astype(cols.dtype) * jax.nn.silu(gate)


def nsa_mixer(cols, cos, sin, q_norm, k_norm, cmp_pos, cmp_k_w1, cmp_k_w2, cmp_v_w1, cmp_v_w2):
    B_, S_, _ = cols.shape
    HKV, G, d = NSA_KV_HEADS, NSA_GROUP, NSA_HEAD_DIM
    splits = np.cumsum([W_TOK] + [NSA_KV_W] * 6 + [NSA_HEADS * 3]).tolist()
    q, k_c, v_c, k_s, v_s, k_w, v_w, g_logit, gate = jnp.split(cols, splits, axis=-1)
    q = apply_rope(rms_norm(q.reshape(B_, S_, NSA_HEADS, d), q_norm), cos, sin).reshape(B_, S_, HKV, G, d)

    def keys(a):
        return apply_rope(rms_norm(a.reshape(B_, S_, HKV, d), k_norm), cos, sin)

    k_c, k_s, k_w = keys(k_c), keys(k_s), keys(k_w)
    v_c = v_c.reshape(B_, S_, HKV, d)
    v_s = v_s.reshape(B_, S_, HKV, d)
    v_w = v_w.reshape(B_, S_, HKV, d)
    branch_gate = jax.nn.sigmoid(g_logit.astype(jnp.float32)).astype(cols.dtype).reshape(B_, S_, HKV, G, 3)

    n_cmp = (S_ - CMP_BLOCK) // CMP_STRIDE + 1
    cmp_start = np.arange(n_cmp) * CMP_STRIDE
    cmp_idx = cmp_start[:, None] + np.arange(CMP_BLOCK)[None, :]
    cmp_end = cmp_start + CMP_BLOCK - 1

    def compress(a, w1, w2):
        blk = a[:, cmp_idx] + cmp_pos[:, None, :].astype(a.dtype)
        blk = blk.transpose(0, 1, 3, 2, 4).reshape(B_, n_cmp, HKV, CMP_BLOCK * d)
        return jax.nn.silu(blk @ w1) @ w2

    kc = compress(k_c, cmp_k_w1, cmp_k_w2)
    vc = compress(v_c, cmp_v_w1, cmp_v_w2)

    n_sel = S_ // SEL_BLOCK
    top_k = min(SEL_TOPK, n_sel)
    sel_start = np.arange(n_sel) * SEL_BLOCK
    agg = ((cmp_start[:, None] < sel_start[None, :] + SEL_BLOCK)
           & (cmp_end[:, None] >= sel_start[None, :])).astype(np.float32)
    ks_blk = k_s.reshape(B_, n_sel, SEL_BLOCK, HKV, d).transpose(0, 3, 1, 2, 4)
    vs_blk = v_s.reshape(B_, n_sel, SEL_BLOCK, HKV, d).transpose(0, 3, 1, 2, 4)
    b_ix = jnp.arange(B_)[:, None, None, None]
    h_ix = jnp.arange(HKV)[None, :, None, None]
    sel_ids = np.arange(n_sel)

    pad = ((0, 0), (WINDOW, 0), (0, 0), (0, 0))
    k_w_pad = jnp.pad(k_w, pad)
    v_w_pad = jnp.pad(v_w, pad)
    scale = d ** -0.5

    def query_block(s0):
        t = s0 + jnp.arange(NSA_QBLOCK)
        qb = lax.dynamic_slice_in_dim(q, s0, NSA_QBLOCK, axis=1)
        gb = lax.dynamic_slice_in_dim(branch_gate, s0, NSA_QBLOCK, axis=1)
        c_mask = cmp_end[None, :] <= t[:, None]
        p_cmp = masked_softmax(jnp.einsum('bqhgd,bchd->bhgqc', qb, kc) * scale, c_mask)
        p_cmp = p_cmp * jnp.any(c_mask, axis=-1)[:, None]
        o_cmp = jnp.einsum('bhgqc,bchd->bqhgd', p_cmp.astype(vc.dtype), vc)
        imp = jnp.einsum('bhgqc,cj->bhqj', p_cmp, agg)
        bt = (t // SEL_BLOCK)[:, None]
        causal = sel_ids[None, :] <= bt
        forced = (sel_ids[None, :] == 0) | (sel_ids[None, :] == bt) | (sel_ids[None, :] == bt - 1)
        score = jnp.where(forced, SEL_BIG, jnp.where(causal, imp, -SEL_BIG))
        _, sel = lax.top_k(score, top_k)
        kb = ks_blk[b_ix, h_ix, sel]
        vb = vs_blk[b_ix, h_ix, sel]
        pos = sel[..., None] * SEL_BLOCK + np.arange(SEL_BLOCK)
        s_mask = (pos <= t[None, None, :, None, None]).reshape(B_, HKV, 1, NSA_QBLOCK, top_k * SEL_BLOCK)
        s_sel = jnp.einsum('bqhgd,bhqnld->bhgqnl', qb, kb) * scale
        p_sel = masked_softmax(s_sel.reshape(B_, HKV, G, NSA_QBLOCK, top_k * SEL_BLOCK), s_mask)
        o_sel = jnp.einsum('bhgqk,bhqkd->bqhgd', p_sel.astype(vb.dtype),
                           vb.reshape(B_, HKV, NSA_QBLOCK, top_k * SEL_BLOCK, d))
        kwb = lax.dynamic_slice_in_dim(k_w_pad, s0, WINDOW + NSA_QBLOCK, axis=1)
        vwb = lax.dynamic_slice_in_dim(v_w_pad, s0, WINDOW + NSA_QBLOCK, axis=1)
        kp = s0 - WINDOW + jnp.arange(WINDOW + NSA_QBLOCK)
        w_mask = (kp[None, :] >= 0) & (kp[None, :] <= t[:, None]) & (kp[None, :] > t[:, None] - WINDOW)
        p_win = masked_softmax(jnp.einsum('bqhgd,bkhd->bhgqk', qb, kwb) * scale, w_mask)
        o_win = jnp.einsum('bhgqk,bkhd->bqhgd', p_win.astype(vwb.dtype), vwb)
        return gb[..., 0:1] * o_cmp + gb[..., 1:2] * o_sel + gb[..., 2:3] * o_win

    starts = jnp.arange(S_ // NSA_QBLOCK, dtype=jnp.int32) * NSA_QBLOCK
    o = lax.map(query_block, starts)
    o = o.transpose(1, 0, 2, 3, 4, 5).reshape(B_, S_, W_TOK)
    return o * jax.nn.silu(gate)


def ssd_chunked(x, dt, A, Bm, Cm):
    B_, S_, G_, R_, P_ = x.shape
    L = math.gcd(SSD_CHUNK, S_)
    nc = S_ // L

    def chunks(a):
        a = a.astype(jnp.float32)
        return a.reshape(B_, nc, L, *a.shape[2:]).swapaxes(0, 1)

    tri = np.tril(np.ones((L, L), dtype=bool))[None, :, :, None, None]
    A = A.astype(jnp.float32)

    def step(state, inp):
        xk, dtk, Bk, Ck = inp
        a = jnp.cumsum(dtk * A, axis=1)
        decay = jnp.exp(jnp.where(tri, a[:, :, None] - a[:, None, :], -jnp.inf))
        cb = jnp.einsum('blgn,bsgn->blsg', Ck, Bk)
        m = cb[..., None] * decay * dtk[:, None]
        y_in = jnp.einsum('blsgr,bsgrp->blgrp', m, xk)
        y_st = jnp.einsum('blgn,bgrpn->blgrp', Ck, state) * jnp.exp(a)[..., None]
        a_end = a[:, -1]
        w = jnp.exp(a_end[:, None] - a) * dtk
        state = state * jnp.exp(a_end)[..., None, None] + jnp.einsum('blgn,blgr,blgrp->bgrpn', Bk, w, xk)
        return state, y_in + y_st

    state0 = jnp.zeros((B_, G_, R_, P_, SSM_STATE), jnp.float32)
    _, y = lax.scan(step, state0, (chunks(x), chunks(dt), chunks(Bm), chunks(Cm)))
    return y.swapaxes(0, 1).reshape(B_, S_, G_, R_, P_)


def mamba2_mixer(cols, conv_w, conv_b, dt_bias, A_log, D_skip, gnorm):
    B_, S_, _ = cols.shape
    R = SSM_HEADS // SSM_GROUPS
    z, xbc, dt = jnp.split(cols, [W_TOK, W_TOK + SSM_XBC_W], axis=-1)
    xbc = jax.nn.silu(causal_depthwise_conv(xbc, conv_w, conv_b))
    xs, Bm, Cm = jnp.split(xbc, [W_TOK, W_TOK + SSM_GROUPS * SSM_STATE], axis=-1)
    dt = jax.nn.softplus(dt.astype(jnp.float32) + dt_bias.astype(jnp.float32))
    A = -jnp.exp(A_log.astype(jnp.float32))
    x5 = xs.reshape(B_, S_, SSM_GROUPS, R, SSM_HEAD_DIM)
    y = ssd_chunked(x5, dt.reshape(B_, S_, SSM_GROUPS, R), A.reshape(SSM_GROUPS, R),
                    Bm.reshape(B_, S_, SSM_GROUPS, SSM_STATE), Cm.reshape(B_, S_, SSM_GROUPS, SSM_STATE))
    y = y + D_skip.astype(jnp.float32).reshape(SSM_GROUPS, R, 1) * x5.astype(jnp.float32)
    y = y.reshape(B_, S_, W_TOK).astype(cols.dtype)
    return rms_norm(y * jax.nn.silu(z), gnorm)


def sgu_mixer(cols, v_norm, sgu_w, sgu_b):
    B_, S_, _ = cols.shape
    u, v, gate = jnp.split(cols, 3, axis=-1)
    u = jax.nn.gelu(u, approximate=False)
    v = rms_norm(jax.nn.gelu(v, approximate=False), v_norm)
    vv = v.reshape(B_, S_ // SGU_CHUNK, SGU_CHUNK, SGU_GROUPS, SGU_GW)
    w = (sgu_w * np.tril(np.ones((SGU_CHUNK, SGU_CHUNK), np.float32))).astype(v.dtype)
    mixed = jnp.einsum('gts,bcsgd->bctgd', w, vv) + sgu_b.T[:, :, None].astype(v.dtype)
    return (u * mixed.reshape(B_, S_, W_TOK)) * jax.nn.silu(gate)


def setup_inputs(seed: int = 0) -> dict:
    key = jax.random.key(seed)
    keys = list(jax.random.split(key, 80))

    def nxt():
        return keys.pop()

    def dense(shape, fan_in):
        return jax.random.normal(nxt(), shape, jnp.float32) * (fan_in ** -0.5)

    def gain(n):
        return 1.0 + 0.02 * jax.random.normal(nxt(), (n,), jnp.float32)

    inp = {}
    inp['x'] = jax.random.normal(nxt(), (BATCH, SEQ, D_MODEL), jnp.float32)
    inp['mem'] = jax.random.normal(nxt(), (BATCH, MEM_TOKENS, D_MODEL), jnp.float32)
    offset = jax.random.randint(nxt(), (BATCH, 1), 0, POS_OFFSET_MAX, dtype=jnp.int32)
    inp['positions'] = jnp.arange(SEQ, dtype=jnp.int32)[None, :] + offset
    inp['mem_norm'] = gain(D_MODEL)
    for i in range(DEPTH):
        p = f'l{i}_'
        m = i % N_MIXERS
        inp[p + 'norm'] = gain(D_MODEL)
        inp[p + 'w_in'] = dense((D_MODEL, TOK_COLS[m] + 2 * W_MEM), D_MODEL)
        inp[p + 'w_out'] = dense((W_TOK + W_MEM, D_MODEL), W_TOK + W_MEM)
        inp[p + 'mem_wkv'] = dense((D_MODEL, 2 * W_MEM), D_MODEL)
        inp[p + 'mem_qnorm'] = gain(MEM_HEAD_DIM)
        inp[p + 'mem_knorm'] = gain(MEM_HEAD_DIM)
        if m == 0:
            inp[p + 'pool_w'] = dense((POOL_GROUPS, POOL_GW, POOL_GW), POOL_GW)
            inp[p + 'pool_scale'] = gain(W_TOK)
        elif m == 1:
            inp[p + 'qnorm'] = gain(NSA_HEAD_DIM)
            inp[p + 'knorm'] = gain(NSA_HEAD_DIM)
            inp[p + 'cmp_pos'] = 0.1 * jax.random.normal(nxt(), (CMP_BLOCK, NSA_HEAD_DIM), jnp.float32)
            inp[p + 'cmp_k_w1'] = dense((CMP_BLOCK * NSA_HEAD_DIM, NSA_HEAD_DIM), CMP_BLOCK * NSA_HEAD_DIM)
            inp[p + 'cmp_k_w2'] = dense((NSA_HEAD_DIM, NSA_HEAD_DIM), NSA_HEAD_DIM)
            inp[p + 'cmp_v_w1'] = dense((CMP_BLOCK * NSA_HEAD_DIM, NSA_HEAD_DIM), CMP_BLOCK * NSA_HEAD_DIM)
            inp[p + 'cmp_v_w2'] = dense((NSA_HEAD_DIM, NSA_HEAD_DIM), NSA_HEAD_DIM)
        elif m == 2:
            inp[p + 'conv_w'] = dense((CONV_W, SSM_XBC_W), CONV_W)
            inp[p + 'conv_b'] = 0.01 * jax.random.normal(nxt(), (SSM_XBC_W,), jnp.float32)
            dt0 = jnp.exp(jax.random.uniform(nxt(), (SSM_HEADS,), jnp.float32,
                                             minval=math.log(1e-3), maxval=math.log(1e-1)))
            inp[p + 'dt_bias'] = dt0 + jnp.log(-jnp.expm1(-dt0))
            inp[p + 'A_log'] = jnp.log(jax.random.uniform(nxt(), (SSM_HEADS,), jnp.float32, minval=1.0, maxval=16.0))
            inp[p + 'D'] = gain(SSM_HEADS)
            inp[p + 'gnorm'] = gain(W_TOK)
        else:
            inp[p + 'v_norm'] = gain(W_TOK)
            inp[p + 'sgu_w'] = dense((SGU_GROUPS, SGU_CHUNK, SGU_CHUNK), SGU_CHUNK)
            inp[p + 'sgu_b'] = 1.0 + 0.02 * jax.random.normal(nxt(), (SGU_GROUPS, SGU_CHUNK), jnp.float32)
    return inp


def reference(x, mem, positions, mem_norm,
              l0_norm, l0_w_in, l0_w_out, l0_mem_wkv, l0_mem_qnorm, l0_mem_knorm, l0_pool_w, l0_pool_scale,
              l1_norm, l1_w_in, l1_w_out, l1_mem_wkv, l1_mem_qnorm, l1_mem_knorm,
              l1_qnorm, l1_knorm, l1_cmp_pos, l1_cmp_k_w1, l1_cmp_k_w2, l1_cmp_v_w1, l1_cmp_v_w2,
              l2_norm, l2_w_in, l2_w_out, l2_mem_wkv, l2_mem_qnorm, l2_mem_knorm,
              l2_conv_w, l2_conv_b, l2_dt_bias, l2_A_log, l2_D, l2_gnorm,
              l3_norm, l3_w_in, l3_w_out, l3_mem_wkv, l3_mem_qnorm, l3_mem_knorm,
              l3_v_norm, l3_sgu_w, l3_sgu_b):
    cos, sin = rope_tables(positions, NSA_HEAD_DIM)
    mem_n = rms_norm(mem, mem_norm)
    token_mixers = (
        lambda c: pooling_mixer(c, l0_pool_w, l0_pool_scale),
        lambda c: nsa_mixer(c, cos, sin, l1_qnorm, l1_knorm, l1_cmp_pos,
                            l1_cmp_k_w1, l1_cmp_k_w2, l1_cmp_v_w1, l1_cmp_v_w2),
        lambda c: mamba2_mixer(c, l2_conv_w, l2_conv_b, l2_dt_bias, l2_A_log, l2_D, l2_gnorm),
        lambda c: sgu_mixer(c, l3_v_norm, l3_sgu_w, l3_sgu_b),
    )
    layer_io = (
        (l0_norm, l0_w_in, l0_w_out, l0_mem_wkv, l0_mem_qnorm, l0_mem_knorm),
        (l1_norm, l1_w_in, l1_w_out, l1_mem_wkv, l1_mem_qnorm, l1_mem_knorm),
        (l2_norm, l2_w_in, l2_w_out, l2_mem_wkv, l2_mem_qnorm, l2_mem_knorm),
        (l3_norm, l3_w_in, l3_w_out, l3_mem_wkv, l3_mem_qnorm, l3_mem_knorm),
    )
    for i in range(DEPTH):
        norm, w_in, w_out, mem_wkv, mem_qn, mem_kn = layer_io[i]
        h = rms_norm(x, norm)
        proj = jnp.einsum('bsd,de->bse', h, w_in)
        y_tok = token_mixers[i % N_MIXERS](proj[..., :-2 * W_MEM])
        y_mem = memory_branch(proj[..., -2 * W_MEM:], mem_n, mem_wkv, mem_qn, mem_kn)
        x = x + jnp.einsum('bse,ed->bsd', jnp.concatenate([y_tok, y_mem], axis=-1), w_out)
    return x
```

```python
import numpy as np
import ml_dtypes
from contextlib import ExitStack
import concourse.bass as bass
import concourse.mybir as mybir
from concourse.bass_utils import run_bass_kernel_spmd

F32 = mybir.dt.float32
BF16 = mybir.dt.bfloat16
AF = mybir.ActivationFunctionType
ALU = mybir.AluOpType
AX = mybir.AxisListType

D = 2048
KD = D // 128
WTOK = 4096
WMEM = 1024
MEMT = 256
EPS = 1e-6
TT = 512
NDSEM = 6
NSTG = 4
SEM_MAX = 8000

STRICT_SAME_ENGINE = True
DEBUG_OUT = set()
NSA_BR = (0, 1, 2)


class KB:
    def __init__(self, nc, es):
        self.nc = nc
        self.es = es
        self.streams = {e: [] for e in ('pe', 'act', 'dve', 'pool', 'sp')}
        self.esem = {e: es.enter_context(nc.semaphore('s_' + e)) for e in self.streams}
        self.ecount = {e: 0 for e in self.streams}
        self.dsems = {}
        for q in ('sp', 'pool', 'act'):
            self.dsems[q] = [[es.enter_context(nc.semaphore('d_%s%d' % (q, i))), 0] for i in range(NDSEM)]
        self.drr = {q: 0 for q in self.dsems}
        self.last_w = {}
        self.readers = {}
        self.seen = {e: {} for e in self.streams}
        self.tail_tokens = []
        self.nops = 0
        self.retired = []
        self.nsem = 0

    def _deps(self, eng, reads, writes):
        need = {}

        def add(tok):
            if tok is None:
                return
            sem, val, src = tok
            if (not STRICT_SAME_ENGINE or eng == 'pe') and src == eng:
                return
            k = id(sem)
            if self.seen[eng].get(k, 0) >= val:
                return
            if k not in need or need[k][1] < val:
                need[k] = (sem, val)

        for r in reads:
            add(self.last_w.get(r))
            if isinstance(r, tuple) and r[0] == 'pb':
                for t in self.readers.get(r, ()):
                    if t[2] != eng:
                        add(t)
        for w in writes:
            add(self.last_w.get(w))
            for t in self.readers.get(w, ()):
                add(t)
        waits = list(need.values())
        for sem, val in waits:
            self.seen[eng][id(sem)] = val
        return waits

    def _commit(self, tok, reads, writes):
        for r in reads:
            self.readers.setdefault(r, []).append(tok)
        for w in writes:
            self.last_w[w] = tok
            self.readers[w] = []

    def _rotate(self, eng):
        if self.ecount[eng] >= SEM_MAX:
            self.retired.append((self.esem[eng], self.ecount[eng], eng))
            self.nsem += 1
            self.esem[eng] = self.es.enter_context(self.nc.semaphore('s_%s_%d' % (eng, self.nsem)))
            self.ecount[eng] = 0

    def op(self, eng, fn, reads=(), writes=()):
        self._rotate(eng)
        waits = self._deps(eng, reads, writes)
        self.ecount[eng] += 1
        tok = (self.esem[eng], self.ecount[eng], eng)
        self.streams[eng].append((waits, fn, (self.esem[eng], 1)))
        self._commit(tok, reads, writes)
        self.nops += 1
        return tok

    def dma(self, q, out, in_, reads=(), writes=(), final=False):
        slot = self.dsems[q][self.drr[q]]
        self.drr[q] = (self.drr[q] + 1) % NDSEM
        if slot[1] >= SEM_MAX:
            self.retired.append((slot[0], slot[1], 'dma'))
            self.nsem += 1
            slot[0] = self.es.enter_context(self.nc.semaphore('d_%s_%d' % (q, self.nsem)))
            slot[1] = 0
        sem = slot[0]
        waits = self._deps(q, reads, writes)
        if slot[1] > 0 and self.seen[q].get(id(sem), 0) < slot[1]:
            waits.append((sem, slot[1]))
            self.seen[q][id(sem)] = slot[1]
        slot[1] += 16
        tok = (sem, slot[1], 'dma')
        self.streams[q].append((waits, lambda e: e.dma_start(out=out, in_=in_), (sem, 16)))
        self._commit(tok, reads, writes)
        if final:
            self.tail_tokens.append(tok)
        self.nops += 1
        return tok

    def barrier(self):
        toks = list(self.retired)
        for e2 in self.streams:
            if self.ecount[e2] > 0:
                toks.append((self.esem[e2], self.ecount[e2], e2))
        for q in self.dsems:
            for sem, val in self.dsems[q]:
                if val > 0:
                    toks.append((sem, val, 'dma'))
        for eng in self.streams:
            waits = []
            for sem, val, src in toks:
                if src == eng:
                    continue
                if self.seen[eng].get(id(sem), 0) < val:
                    waits.append((sem, val))
                    self.seen[eng][id(sem)] = val
            if waits:
                self._rotate(eng)
                self.ecount[eng] += 1
                self.streams[eng].append((waits, lambda e: e.nop(), (self.esem[eng], 1)))

    def emit(self):
        nc = self.nc
        fm = {}
        for t in self.tail_tokens:
            if id(t[0]) not in fm or fm[id(t[0])][1] < t[1]:
                fm[id(t[0])] = (t[0], t[1])
        fin = list(fm.values())
        engmap = {'pe': 'tensor', 'act': 'scalar', 'dve': 'vector', 'pool': 'gpsimd', 'sp': 'sync'}
        with nc.Block() as block:
            for ename, bname in engmap.items():
                stream = self.streams[ename]
                extra = fin if ename == 'sp' else []

                def body(e, stream=stream, extra=extra):
                    for waits, fn, inc in stream:
                        for sem, val in waits:
                            e.wait_ge(sem, val)
                        ins = fn(e)
                        ins.then_inc(inc[0], inc[1])
                    for sem, val in extra:
                        e.wait_ge(sem, val)
                getattr(block, bname)(body)


class Ctx:
    pass


def build_program(S, layers, first_layer_input='x'):
    nc = bass.Bass("TRN2", target_bir_lowering=False)
    es = ExitStack()
    kb = KB(nc, es)
    NT = S // TT
    c = Ctx()
    c.nc, c.kb, c.es, c.S, c.NT = nc, kb, es, S, NT

    _dins = {}

    def din(name, shape, dt=F32):
        if name not in _dins:
            _dins[name] = nc.dram_tensor(name, list(shape), dt, kind="ExternalInput").ap()
        return _dins[name]

    def dscr(name, shape, dt=F32):
        kind = "ExternalOutput" if name in DEBUG_OUT else "Internal"
        return nc.dram_tensor('scr_' + name, list(shape), dt, kind=kind).ap()

    c.din, c.dscr = din, dscr
    c._dinc = {}

    def din_cached(name, shape, dt=F32):
        if name not in c._dinc:
            c._dinc[name] = din(name, shape, dt)
        return c._dinc[name]
    c.din_cached = din_cached
    x_in = din('x', [S, D])
    mem_in = din('mem', [MEMT, D])
    ident_in = din('ident', [128, 128])
    out = nc.dram_tensor('out', [S, D], F32, kind="ExternalOutput").ap()
    c.mem_in = mem_in
    xbuf = [(dscr('xs0', [S, D]), 'xs0'), (dscr('xs1', [S, D]), 'xs1')]

    def sb(name, shape, dt=F32):
        return es.enter_context(nc.sbuf_tensor('sb_' + name, list(shape), dt))

    def ps(name, shape, dt=F32):
        return es.enter_context(nc.psum_tensor('ps_' + name, list(shape), dt))
    c.sb, c.ps = sb, ps

    c.ident_f = sb('ident_f', [128, 128], F32)
    c.ident = sb('ident', [128, 128], BF16)
    c.ones_bf = sb('ones_bf', [128, 128], BF16)
    c.xt = [sb('xt0', [128, D], F32)] * 2
    c.hn = [sb('hn0', [128, D], BF16)] * 2
    c.nrb = 1
    c.stat = [sb('stat0', [128, 8], F32)] * 2
    c.hT = sb('hT', [128, KD, TT], BF16)
    c.yT = sb('yT', [128, 40, TT], BF16)
    c.wt = [sb('wt%d' % i, [128, 16, 512], BF16) for i in range(2)]
    c.wrr = 0
    c.pbank = [ps('pb%d' % i, [128, 512], F32) for i in range(8)]
    c.prr = 0
    c.kT = sb('kT', [128, 4, 2, MEMT], BF16)
    c.Vm = sb('Vm', [128, 2, WMEM], BF16)
    c.mq = sb('mq', [128, 2, TT], BF16)
    c.mqs = sb('mqs', [128, 2, TT], BF16)
    c.mrs = sb('mrs', [128, TT], F32)
    c.msc = sb('msc', [128, TT], F32)
    c.mP = sb('mP', [128, 2, TT], BF16)
    c.mrd = sb('mrd', [128, TT], F32)
    c.msg = sb('msg', [128, TT], F32)
    c.mo = sb('mo', [128, TT], F32)
    c.xr = [sb('xr%d' % i, [128, 512], F32) for i in range(2)]
    c.xo = [sb('xo%d' % i, [128, 512], F32) for i in range(2)]
    c.xrr = 0
    c.gpc = sb('gpc', [128, 64], F32)
    ARENA = 74 * 1024
    c.arena = sb('arena', [128, ARENA // 4], F32)
    c.aoff = 0
    c.agen = 0

    def areset():
        kb.barrier()
        c.aoff = 0
        c.agen += 1
        c.nrb = 1
        c.xt[1], c.hn[1], c.stat[1] = c.xt[0], c.hn[0], c.stat[0]

    def rows2():
        c.xt[1] = c.aalloc('xt1', [128, D], F32)
        c.hn[1] = c.aalloc('hn1', [128, D], BF16)
        c.stat[1] = c.aalloc('stat1', [128, 8], F32)
        c.nrb = 2
    c.rows2 = rows2

    def aalloc(name, shape, dt=F32):
        esz = 2 if dt == BF16 else 4
        n = int(np.prod(shape[1:]))
        nbytes = (n * esz + 31) // 32 * 32
        assert c.aoff + nbytes <= ARENA, (name, c.aoff, nbytes)
        v = c.arena[0:shape[0], c.aoff // 4:(c.aoff + nbytes) // 4]
        c.aoff += nbytes
        if dt != F32:
            v = v.bitcast(dt)
        v = v[:, 0:n]
        if len(shape) == 3:
            v = v.rearrange("p (a b) -> p a b", a=shape[1])
        elif len(shape) == 4:
            v = v.rearrange("p (a b c) -> p a b c", a=shape[1], b=shape[2])
        return v
    c.areset, c.aalloc = areset, aalloc

    kb.dma('pool', c.ident_f[:], ident_in, writes=['ident_f'])
    kb.op('dve', lambda e: e.tensor_copy(c.ident[:], c.ident_f[:]), reads=['ident_f'], writes=['ident'])
    kb.op('dve', lambda e: e.memset(c.ones_bf[:], 1.0), writes=['ones_bf'])

    cur_in = (x_in, 'x')
    mem_prep_global(c)
    for li, l in enumerate(layers):
        dst = (out, 'out') if li == len(layers) - 1 else xbuf[li % 2]
        LAYER_FNS[l](c, l, cur_in, dst)
        cur_in = dst
    kb.emit()
    return nc, es


def next_bank(c):
    if not hasattr(c, 'rot'):
        c.rot = list(range(8))
    i = c.rot.pop(0)
    c.rot.append(i)
    return c.pbank[i], ('pb', i)


def reserve_bank(c):
    if not hasattr(c, 'rot'):
        c.rot = list(range(8))
    i = c.rot.pop(0)
    return c.pbank[i], ('pb', i), i


def release_bank(c, i):
    c.rot.append(i)


def norm_rows(c, i, src_ap, srckeys):
    kb = c.kb
    i = i % c.nrb
    xt, hn, st = c.xt[i], c.hn[i], c.stat[i]
    kb.dma('pool', xt[:], src_ap, reads=srckeys, writes=[('xt', i)])
    kb.op('act', lambda e: e.activation(hn[:], xt[:], AF.Square, accum_out=st[:, 0:1]),
          reads=[('xt', i)], writes=[('hn', i), ('st', i)])
    kb.op('dve', lambda e: e.tensor_scalar(st[:, 1:2], st[:, 0:1], 1.0 / D, EPS, ALU.mult, ALU.add),
          reads=[('st', i)], writes=[('st', i)])
    kb.op('act', lambda e: e.activation(st[:, 2:3], st[:, 1:2], AF.Sqrt),
          reads=[('st', i)], writes=[('st', i)])
    kb.op('dve', lambda e: e.reciprocal(st[:, 3:4], st[:, 2:3]),
          reads=[('st', i)], writes=[('st', i)])
    kb.op('dve', lambda e: e.tensor_scalar(hn[:], xt[:], st[:, 3:4], None, ALU.mult),
          reads=[('xt', i), ('st', i)], writes=[('hn', i)])


def transpose_rows(c, i, dst, dkey, s):
    kb = c.kb
    i = i % c.nrb
    hn = c.hn[i]
    for g in range(KD // 4):
        pb, pk = next_bank(c)
        pbv = pb[:].bitcast(BF16)
        for j in range(4):
            k = g * 4 + j
            kb.op('pe', lambda e, k=k, j=j, pbv=pbv: e.transpose(
                pbv[:, j * 128:(j + 1) * 128], hn[:, k * 128:(k + 1) * 128], c.ident[:]),
                reads=[('hn', i), 'ident'], writes=[pk])
        src = pbv[:, 0:512].rearrange("p (j t) -> p j t", j=4)
        o = dst[:, g * 4:(g + 1) * 4, s * 128:(s + 1) * 128]
        if g % 2 == 0:
            kb.op('act', lambda e, o=o, src=src: e.copy(o, src), reads=[pk], writes=[dkey])
        else:
            kb.op('dve', lambda e, o=o, src=src: e.tensor_copy(o, src), reads=[pk], writes=[dkey])


def load_norm_transpose(c, x_src, tt):
    for s in range(TT // 128):
        i = (tt * 4 + s) % 2
        r0 = tt * TT + s * 128
        norm_rows(c, i, x_src[0][r0:r0 + 128, :], [('xd', x_src[1], tt)])
        transpose_rows(c, i, c.hT, ('hT', s), s)


def load_pc(c, col0, name, n):
    c.kb.dma('pool', c.gpc[:, col0:col0 + n], c.din(name, [128, n]), writes=['gpc'])
    return c.gpc[:, col0:col0 + n]


def prep_w(c, name, w_ap, K, segs, rowscale=None):
    kb = c.kb
    KC = K // 128
    wv = w_ap.rearrange("(k p) n -> p k n", p=128)
    res = {}
    for (sname, col0, ncols) in segs:
        nb = (ncols + 511) // 512
        scr = c.dscr('w_%s_%s' % (name, sname), [nb, 128, KC, 512], BF16)
        res[sname] = (scr, ncols)
        for b in range(nb):
            n = min(512, ncols - b * 512)
            for k0 in range(0, KC, 2):
                kn = min(2, KC - k0)
                i = c.srr
                c.srr = (c.srr + 1) % NSTG
                stg, stb = c.stg[i], c.stb[i]
                kb.dma('sp', stg[:, 0:kn, 0:n], wv[:, k0:k0 + kn, col0 + b * 512: col0 + b * 512 + n],
                       writes=[('stg', i)])
                if rowscale is None:
                    if i % 2 == 0:
                        kb.op('dve', lambda e, stg=stg, stb=stb, kn=kn, n=n: e.tensor_copy(stb[:, 0:kn, 0:n], stg[:, 0:kn, 0:n]),
                              reads=[('stg', i)], writes=[('stb', i)])
                    else:
                        kb.op('act', lambda e, stg=stg, stb=stb, kn=kn, n=n: e.copy(stb[:, 0:kn, 0:n], stg[:, 0:kn, 0:n]),
                              reads=[('stg', i)], writes=[('stb', i)])
                else:
                    for kk in range(kn):
                        sc = rowscale[:, k0 + kk:k0 + kk + 1]
                        if kk == 0:
                            kb.op('dve', lambda e, stg=stg, stb=stb, kk=kk, n=n, sc=sc: e.tensor_scalar(
                                stb[:, kk, 0:n], stg[:, kk, 0:n], sc, None, ALU.mult),
                                reads=[('stg', i), 'gpc'], writes=[('stb', i)])
                        else:
                            kb.op('act', lambda e, stg=stg, stb=stb, kk=kk, n=n, sc=sc: e.activation(
                                stb[:, kk, 0:n], stg[:, kk, 0:n], AF.Copy, scale=sc),
                                reads=[('stg', i), 'gpc'], writes=[('stb', i)])
                kb.dma('pool', scr[b, :, k0:k0 + kn, 0:n], stb[:, 0:kn, 0:n],
                       reads=[('stb', i)], writes=[('wscr', name, sname, b)])
    return res


def prep_begin(c):
    c.areset()
    c.stg = [c.aalloc('stg%d' % i, [128, 2, 512], F32) for i in range(NSTG)]
    c.stb = [c.aalloc('stb%d' % i, [128, 2, 512], BF16) for i in range(NSTG)]
    c.srr = 0


def load_w_dep(c, src_ap, srckey):
    i = c.wrr
    c.wrr = (c.wrr + 1) % 2
    wt = c.wt[i]
    shp = src_ap.shape
    c.kb.dma('sp', wt[:, 0:shp[1], 0:shp[2]], src_ap, reads=[srckey], writes=[('wt', i)])
    return wt, ('wt', i)


def proj_fm(c, wseg, name, sname, kc=KD, rhs_of=None, ntok=TT, b0=0, nblk=None):
    kb = c.kb
    scr, ncols = wseg[sname]
    nb = (ncols + 511) // 512
    if nblk is None:
        nblk = nb - b0
    if rhs_of is None:
        rhs_of = lambda k: (c.hT[:, k, :], [('hT', s) for s in range(4)])
    for b in range(b0, b0 + nblk):
        n = min(512, ncols - b * 512)
        wt, wk = load_w_dep(c, scr[b, :, 0:kc, 0:n], ('wscr', name, sname, b))
        for sub in range((n + 127) // 128):
            m = min(128, n - sub * 128)
            pb, pk = next_bank(c)
            for k in range(kc):
                rhs, rk = rhs_of(k)
                kb.op('pe', lambda e, pb=pb, wt=wt, k=k, sub=sub, m=m, rhs=rhs: e.matmul(
                    pb[0:m, 0:ntok], wt[:, k, sub * 128: sub * 128 + m], rhs, start=(k == 0), stop=(k == kc - 1)),
                    reads=[wk] + rk, writes=[pk])
            yield (b - b0) * 4 + sub, m, pb, pk


def wout_phase(c, l, wo, x_src, dst, tt):
    kb = c.kb
    scr, _ = wo['all']
    for cb in range(4):
        banks = [next_bank(c) for _ in range(4)]
        for kg, (k0, kn) in enumerate(((0, 16), (16, 16), (32, 8))):
            wt, wk = load_w_dep(c, scr[cb, :, k0:k0 + kn, :], ('wscr', 'wo%d' % l, 'all', cb))
            for s in range(4):
                pb, pk = banks[s]
                for kk in range(kn):
                    f = k0 + kk
                    kb.op('pe', lambda e, pb=pb, wt=wt, kk=kk, f=f, s=s: e.matmul(
                        pb[:, :], c.yT[:, f, s * 128:(s + 1) * 128], wt[:, kk, :], start=(f == 0), stop=(f == 39)),
                        reads=[wk, ('yT', f)], writes=[pk])
        for s in range(4):
            pb, pk = banks[s]
            i = c.xrr
            c.xrr = (c.xrr + 1) % 2
            xr, xo = c.xr[i], c.xo[i]
            r0 = tt * TT + s * 128
            kb.dma('pool', xr[:], x_src[0][r0:r0 + 128, cb * 512:(cb + 1) * 512], reads=[('xd', x_src[1], tt)],
                   writes=[('xr', i)])
            kb.op('dve', lambda e, xo=xo, xr=xr, pb=pb: e.tensor_tensor(xo[:], pb[:, :], xr[:], ALU.add),
                  reads=[pk, ('xr', i)], writes=[('xo', i)])
            kb.dma('pool', dst[0][r0:r0 + 128, cb * 512:(cb + 1) * 512], xo[:], reads=[('xo', i)],
                   writes=[('xd', dst[1], tt)], final=True)


def mem_prep_global(c):
    kb = c.kb
    c.areset()
    memnT = c.aalloc('memnT', [128, KD, MEMT], BF16)
    c.memnT_scr = c.dscr('memnT', [128, KD, MEMT], BF16)
    for s in range(2):
        norm_rows(c, s, c.mem_in[s * 128:(s + 1) * 128, :], [])
        transpose_rows(c, s, memnT, 'memnT', s)
    kb.dma('pool', c.memnT_scr, memnT, reads=['memnT'], writes=['memnT_scr'])


def mem_prep_layer(c, l):
    kb = c.kb
    gm = load_pc(c, 32, 'mem_norm_pc', KD)
    wkv = prep_w(c, 'wkv%d' % l, c.din('l%d_mem_wkv' % l, [D, 2 * WMEM]), D, [('k', 0, WMEM), ('v', WMEM, WMEM)],
                 rowscale=gm)
    memnT = c.aalloc('memnT', [128, KD, MEMT], BF16)
    kraw = c.aalloc('kraw', [128, 2, MEMT], F32)
    ksq = c.aalloc('ksq', [128, 2, MEMT], BF16)
    krs = c.aalloc('krs', [128, MEMT], F32)
    kb.dma('pool', memnT, c.memnT_scr, reads=['memnT_scr'], writes=['memnT'])
    qkg = c.gpc[:, 48:54]
    load_pc(c, 48, 'l%d_mem_qnorm_pc' % l, 2)
    load_pc(c, 50, 'l%d_mem_knorm_pc' % l, 2)
    kb.op('dve', lambda e: e.tensor_tensor(qkg[:, 4:6], qkg[:, 0:2], qkg[:, 2:4], ALU.mult),
          reads=['gpc'], writes=['gpc2'])
    memrhs = lambda k: (memnT[:, k, :], ['memnT'])
    for ci, m, pb, pk in proj_fm(c, wkv, 'wkv%d' % l, 'k', rhs_of=memrhs, ntok=MEMT):
        hd, dc = ci // 2, ci % 2
        kb.op('act', lambda e, pb=pb, dc=dc: e.copy(kraw[:, dc, :], pb[:, 0:MEMT]),
              reads=[pk], writes=[('kraw', dc)])
        kb.op('act', lambda e, pb=pb, dc=dc: e.activation(ksq[:, dc, :], pb[:, 0:MEMT], AF.Square),
              reads=[pk], writes=[('ksq', dc)])
        if dc == 1:
            pb2, pk2 = next_bank(c)
            for d2 in range(2):
                kb.op('pe', lambda e, pb2=pb2, d2=d2: e.matmul(pb2[:, 0:MEMT], c.ones_bf[:], ksq[:, d2, :],
                                                                start=(d2 == 0), stop=(d2 == 1)),
                      reads=['ones_bf', ('ksq', d2)], writes=[pk2])
            kb.op('dve', lambda e, pb2=pb2: e.tensor_scalar(krs, pb2[:, 0:MEMT], 1.0 / 256, EPS, ALU.mult, ALU.add),
                  reads=[pk2], writes=['krs'])
            kb.op('act', lambda e: e.activation(krs, krs, AF.Sqrt), reads=['krs'], writes=['krs'])
            kb.op('dve', lambda e: e.reciprocal(krs, krs), reads=['krs'], writes=['krs'])
            for d2 in range(2):
                kb.op('dve', lambda e, hd=hd, d2=d2: e.scalar_tensor_tensor(
                    c.kT[:, hd, d2, :], kraw[:, d2, :], qkg[:, 4 + d2:5 + d2], krs, ALU.mult, ALU.mult),
                    reads=[('kraw', d2), 'gpc2', 'krs'], writes=['kT'])
    scr, _ = wkv['v']
    for b in range(2):
        wt, wk = load_w_dep(c, scr[b, :, :, :], ('wscr', 'wkv%d' % l, 'v', b))
        for mt in range(2):
            pb, pk = next_bank(c)
            for k in range(KD):
                kb.op('pe', lambda e, pb=pb, wt=wt, k=k, mt=mt: e.matmul(
                    pb[:, :], memnT[:, k, mt * 128:(mt + 1) * 128], wt[:, k, :], start=(k == 0), stop=(k == KD - 1)),
                    reads=[wk, 'memnT'], writes=[pk])
            kb.op('act', lambda e, pb=pb, mt=mt, b=b: e.copy(c.Vm[:, mt, b * 512:(b + 1) * 512], pb[:, :]),
                  reads=[pk], writes=['Vm'])


def mem_branch(c, l, win):
    kb = c.kb
    gate_it = proj_fm(c, win, 'win%d' % l, 'memg')
    q_it = proj_fm(c, win, 'win%d' % l, 'memq')
    G = [(c.msg, 'msg'), (c.mo, 'mo')]

    def gate_chunk(i):
        gci, gm_, gpb, gpk = next(gate_it)
        kb.op('act', lambda e, gpb=gpb, i=i: e.activation(G[i][0][:], gpb[:, :], AF.Silu), reads=[gpk], writes=[G[i][1]])

    for hd in range(4):
        for dc in range(2):
            ci, m, pb, pk = next(q_it)
            kb.op('act', lambda e, pb=pb, dc=dc: e.copy(c.mq[:, dc, :], pb[:, :]), reads=[pk], writes=[('mq', dc)])
            kb.op('act', lambda e, pb=pb, dc=dc: e.activation(c.mqs[:, dc, :], pb[:, :], AF.Square),
                  reads=[pk], writes=[('mqs', dc)])
        gate_chunk(0)
        pb2, pk2 = next_bank(c)
        for d2 in range(2):
            kb.op('pe', lambda e, pb2=pb2, d2=d2: e.matmul(pb2[:, :], c.ones_bf[:], c.mqs[:, d2, :],
                                                            start=(d2 == 0), stop=(d2 == 1)),
                  reads=['ones_bf', ('mqs', d2)], writes=[pk2])
        kb.op('act', lambda e, pb2=pb2: e.activation(c.mrs[:], pb2[:, :], AF.Ln, scale=1.0 / 256, bias=EPS),
              reads=[pk2], writes=['mrs'])
        kb.op('act', lambda e: e.activation(c.mrs[:], c.mrs[:], AF.Exp, scale=-0.5), reads=['mrs'], writes=['mrs'])
        gate_chunk(1)
        for mt in range(2):
            pb3, pk3 = next_bank(c)
            for d2 in range(2):
                kb.op('pe', lambda e, pb3=pb3, d2=d2, mt=mt, hd=hd: e.matmul(
                    pb3[:, :], c.kT[:, hd, d2, mt * 128:(mt + 1) * 128], c.mq[:, d2, :], start=(d2 == 0), stop=(d2 == 1)),
                    reads=['kT', ('mq', d2)], writes=[pk3])
            kb.op('dve', lambda e, pb3=pb3: e.tensor_tensor(c.msc[:], pb3[:, :], c.mrs[:], ALU.mult),
                  reads=[pk3, 'mrs'], writes=['msc'])
            kb.op('act', lambda e, mt=mt: e.activation(c.mP[:, mt, :], c.msc[:], AF.Exp, scale=1.0 / 16),
                  reads=['msc'], writes=[('mP', mt)])
        pb4, pk4 = next_bank(c)
        for mt in range(2):
            kb.op('pe', lambda e, pb4=pb4, mt=mt: e.matmul(pb4[:, :], c.ones_bf[:], c.mP[:, mt, :],
                                                            start=(mt == 0), stop=(mt == 1)),
                  reads=['ones_bf', ('mP', mt)], writes=[pk4])
        pvb = []
        for d2 in range(2):
            pb5, pk5 = next_bank(c)
            for mt in range(2):
                kb.op('pe', lambda e, pb5=pb5, mt=mt, d2=d2, hd=hd: e.matmul(
                    pb5[:, :], c.Vm[:, mt, hd * 256 + d2 * 128: hd * 256 + (d2 + 1) * 128], c.mP[:, mt, :],
                    start=(mt == 0), stop=(mt == 1)),
                    reads=['Vm', ('mP', mt)], writes=[pk5])
            pvb.append((pb5, pk5))
        kb.op('act', lambda e, pb4=pb4: e.activation(c.mrd[:], pb4[:, :], AF.Ln), reads=[pk4], writes=['mrd'])
        kb.op('act', lambda e: e.activation(c.mrd[:], c.mrd[:], AF.Exp, scale=-1.0), reads=['mrd'], writes=['mrd'])
        for d2 in range(2):
            pb5, pk5 = pvb[d2]
            f = hd * 2 + d2
            kb.op('dve', lambda e, pb5=pb5: e.tensor_tensor(c.msc[:], pb5[:, :], c.mrd[:], ALU.mult),
                  reads=[pk5, 'mrd'], writes=['msc'])
            kb.op('dve', lambda e, f=f, d2=d2: e.tensor_tensor(c.yT[:, 32 + f, :], c.msc[:], G[d2][0][:], ALU.mult),
                  reads=['msc', G[d2][1]], writes=[('yT', 32 + f)])


POOL_W = (2, 4, 8, 16)


def layer0(c, l, x_src, dst):
    kb = c.kb
    prep_begin(c)
    gain = load_pc(c, 0, 'l%d_norm_pc' % l, KD)
    win = prep_w(c, 'win%d' % l, c.din('l%d_w_in' % l, [D, 10240]), D,
                 [('v', 0, 4096), ('gate', 4096, 4096), ('memq', 8192, 1024), ('memg', 9216, 1024)], rowscale=gain)
    wo = prep_w(c, 'wo%d' % l, c.din('l%d_w_out' % l, [5120, D]), 5120, [('all', 0, D)])
    pw_in = c.din('l%d_pool_w' % l, [4, 1024, 1024])
    pws = [prep_w(c, 'pw%d_%d' % (l, g), pw_in[g], 1024, [('all', 0, 1024)]) for g in range(4)]
    mem_prep_layer(c, l)
    c.areset()
    c.rows2()
    psc = c.aalloc('pool_sc', [128, 32], F32)
    kb.dma('pool', psc, c.din('l%d_pool_scale_pc' % l, [128, 32]), writes=['psc'])
    rc0 = c.aalloc('pool_rc0', [128, 4, TT], F32)
    kb.dma('pool', rc0, c.din('pool_rc0', [128, 4, TT]), writes=['rc0'])
    carry = c.aalloc('pool_carry', [128, 32, 16], F32)
    kb.op('dve', lambda e: e.memset(carry, 0.0), writes=['carry'])
    va = c.aalloc('pool_va', [128, 16 + TT], F32)
    vb = c.aalloc('pool_vb', [128, 16 + TT], F32)
    vc = c.aalloc('pool_vc', [128, 16 + TT], F32)
    mixT = c.aalloc('pool_mixT', [128, 8, TT], BF16)
    sg = c.aalloc('pool_sg', [128, TT], F32)
    for tt in range(c.NT):
        load_norm_transpose(c, x_src, tt)
        for g in range(4):
            w = POOL_W[g]
            for ci, m, pb, pk in proj_fm(c, win, 'win%d' % l, 'v', b0=2 * g, nblk=2):
                ch = g * 8 + ci
                kb.op('act', lambda e, pb=pb: e.copy(va[:, 16:16 + TT], pb[:, :]), reads=[pk], writes=['va'])
                kb.op('pool', lambda e, ch=ch: e.tensor_copy(va[:, 0:16], carry[:, ch, :]),
                      reads=['carry'], writes=['va'])
                kb.op('pool', lambda e, ch=ch: e.tensor_copy(carry[:, ch, :], va[:, TT:TT + 16]),
                      reads=['va'], writes=['carry'])
                src, srck = va, 'va'
                bufs = [(vb, 'vb'), (vc, 'vc')]
                sh = 1
                bi = 0
                while sh < w:
                    dstb, dk = bufs[bi]
                    bi ^= 1
                    kb.op('dve', lambda e, src=src, dstb=dstb, sh=sh: e.tensor_tensor(
                        dstb[:, sh:16 + TT], src[:, sh:16 + TT], src[:, 0:16 + TT - sh], ALU.add),
                        reads=[srck], writes=[dk])
                    src, srck = dstb, dk
                    sh *= 2
                if tt == 0:
                    dstb, dk = bufs[bi]
                    kb.op('dve', lambda e, src=src, dstb=dstb, g=g: e.tensor_tensor(
                        dstb[:, 16:16 + TT], src[:, 16:16 + TT], rc0[:, g, :], ALU.mult),
                        reads=[srck, 'rc0'], writes=[dk])
                    kb.op('dve', lambda e, dstb=dstb, ci=ci: e.tensor_tensor(
                        mixT[:, ci, :], dstb[:, 16:16 + TT], va[:, 16:16 + TT], ALU.subtract),
                        reads=[dk, 'va'], writes=[('mixT', ci)])
                else:
                    kb.op('dve', lambda e, src=src, ci=ci, w=w: e.scalar_tensor_tensor(
                        mixT[:, ci, :], src[:, 16:16 + TT], 1.0 / w, va[:, 16:16 + TT], ALU.mult, ALU.subtract),
                        reads=[srck, 'va'], writes=[('mixT', ci)])
            gate_it = proj_fm(c, win, 'win%d' % l, 'gate', b0=2 * g, nblk=2)
            mixrhs = lambda k: (mixT[:, k, :], [('mixT', k)])
            for ci, m, pb, pk in proj_fm(c, pws[g], 'pw%d_%d' % (l, g), 'all', kc=8, rhs_of=mixrhs):
                gci, gm, gpb, gpk = next(gate_it)
                ch = g * 8 + ci
                kb.op('act', lambda e, gpb=gpb: e.activation(sg, gpb[:, :], AF.Silu), reads=[gpk], writes=['sg'])
                kb.op('dve', lambda e, pb=pb, ch=ch: e.scalar_tensor_tensor(
                    c.yT[:, ch, :], pb[:, :], psc[:, ch:ch + 1], sg, ALU.mult, ALU.mult),
                    reads=[pk, 'psc', 'sg'], writes=[('yT', ch)])
        mem_branch(c, l, win)
        wout_phase(c, l, wo, x_src, dst, tt)


def layer3(c, l, x_src, dst):
    kb = c.kb
    prep_begin(c)
    gain = load_pc(c, 0, 'l%d_norm_pc' % l, KD)
    win = prep_w(c, 'win%d' % l, c.din('l%d_w_in' % l, [D, 14336]), D,
                 [('u', 0, 4096), ('v', 4096, 4096), ('gate', 8192, 4096), ('memq', 12288, 1024),
                  ('memg', 13312, 1024)], rowscale=gain)
    wo = prep_w(c, 'wo%d' % l, c.din('l%d_w_out' % l, [5120, D]), 5120, [('all', 0, D)])
    mem_prep_layer(c, l)
    c.areset()
    c.rows2()
    vg = c.aalloc('sgu_vg', [128, 4, 4096], BF16)
    WTf = c.aalloc('sgu_WTf', [128, 8, 128], F32)
    WT = c.aalloc('sgu_WT', [128, 8, 128], BF16)
    tril = c.aalloc('sgu_tril', [128, 128], F32)
    bbc = c.aalloc('sgu_bbc', [128, 8, 128], F32)
    vng = c.aalloc('sgu_vng', [128, 32], F32)
    ssq = c.aalloc('sgu_ssq', [128, 4, 8], F32)
    rst = c.aalloc('sgu_rst', [128, 8], F32)
    t1 = c.aalloc('sgu_t1', [128, TT], F32)
    gu = c.aalloc('sgu_gu', [128, TT], F32)
    sg = c.aalloc('sgu_sg', [128, TT], F32)
    kb.dma('pool', WTf, c.din('l%d_sgu_wT' % l, [128, 8, 128]), writes=['WTf'])
    kb.dma('pool', tril, c.din('trilT', [128, 128]), writes=['tril'])
    kb.dma('pool', bbc, c.din('l%d_sgu_b_bc' % l, [128, 8, 128]), writes=['bbc'])
    kb.dma('pool', vng, c.din('l%d_v_norm_pc' % l, [128, 32]), writes=['vng'])
    for g in range(8):
        kb.op('dve', lambda e, g=g: e.tensor_tensor(WT[:, g, :], WTf[:, g, :], tril, ALU.mult),
              reads=['WTf', 'tril'], writes=['WT'])
    scr_v, _ = win['v']
    for tt in range(c.NT):
        load_norm_transpose(c, x_src, tt)
        for b in range(8):
            wt, wk = load_w_dep(c, scr_v[b, :, :, :], ('wscr', 'win%d' % l, 'v', b))
            for s in range(4):
                pb, pk = next_bank(c)
                for k in range(KD):
                    kb.op('pe', lambda e, pb=pb, wt=wt, k=k, s=s: e.matmul(
                        pb[:, :], c.hT[:, k, s * 128:(s + 1) * 128], wt[:, k, :], start=(k == 0), stop=(k == KD - 1)),
                        reads=[wk, ('hT', s)], writes=[pk])
                kb.op('act', lambda e, pb=pb, s=s, b=b: e.activation(vg[:, s, b * 512:(b + 1) * 512], pb[:, :], AF.Gelu),
                      reads=[pk], writes=[('vg', s)])
                kb.op('act', lambda e, s=s, b=b: e.activation(t1, vg[:, s, b * 512:(b + 1) * 512], AF.Square,
                                                              accum_out=ssq[:, s, b:b + 1]),
                      reads=[('vg', s)], writes=['t1', ('ssq', s)])
        for s in range(4):
            kb.op('dve', lambda e, s=s: e.tensor_reduce(rst[:, s:s + 1], ssq[:, s, :], AX.X, ALU.add),
                  reads=[('ssq', s)], writes=[('rst', s)])
            kb.op('dve', lambda e, s=s: e.tensor_scalar(rst[:, s:s + 1], rst[:, s:s + 1], 1.0 / 4096, EPS, ALU.mult, ALU.add),
                  reads=[('rst', s)], writes=[('rst', s)])
            kb.op('act', lambda e, s=s: e.activation(rst[:, s:s + 1], rst[:, s:s + 1], AF.Sqrt),
                  reads=[('rst', s)], writes=[('rst', s)])
            kb.op('dve', lambda e, s=s: e.reciprocal(rst[:, s:s + 1], rst[:, s:s + 1]),
                  reads=[('rst', s)], writes=[('rst', s)])
            kb.op('dve', lambda e, s=s: e.tensor_scalar(vg[:, s, :], vg[:, s, :], rst[:, s:s + 1], None, ALU.mult),
                  reads=[('rst', s), ('vg', s)], writes=[('vg', s)])
        u_it = proj_fm(c, win, 'win%d' % l, 'u')
        g_it = proj_fm(c, win, 'win%d' % l, 'gate')
        for fc in range(32):
            g = fc // 4
            _, _, upb, upk = next(u_it)
            kb.op('act', lambda e, upb=upb: e.activation(gu, upb[:, :], AF.Gelu), reads=[upk], writes=['gu'])
            _, _, gpb, gpk = next(g_it)
            kb.op('act', lambda e, gpb=gpb: e.activation(sg, gpb[:, :], AF.Silu), reads=[gpk], writes=['sg'])
            pb, pk = next_bank(c)
            for s in range(4):
                kb.op('pe', lambda e, pb=pb, s=s, fc=fc, g=g: e.matmul(
                    pb[:, s * 128:(s + 1) * 128], vg[:, s, fc * 128:(fc + 1) * 128], WT[:, g, :], start=True, stop=True),
                    reads=[('vg', s), 'WT'], writes=[pk])
            for s in range(4):
                kb.op('dve', lambda e, pb=pb, s=s, fc=fc, g=g: e.scalar_tensor_tensor(
                    t1[:, s * 128:(s + 1) * 128], pb[:, s * 128:(s + 1) * 128], vng[:, fc:fc + 1], bbc[:, g, :],
                    ALU.mult, ALU.add), reads=[pk, 'vng', 'bbc'], writes=['t1'])
            kb.op('pool', lambda e: e.tensor_tensor(gu, gu, sg, ALU.mult), reads=['gu', 'sg'], writes=['gu'])
            kb.op('dve', lambda e, fc=fc: e.tensor_tensor(c.yT[:, fc, :], t1, gu, ALU.mult),
                  reads=['t1', 'gu'], writes=[('yT', fc)])
        mem_branch(c, l, win)
        wout_phase(c, l, wo, x_src, dst, tt)


def layer2(c, l, x_src, dst):
    kb = c.kb
    HT = 256
    prep_begin(c)
    gain = load_pc(c, 0, 'l%d_norm_pc' % l, KD)
    win = prep_w(c, 'win%d' % l, c.din('l%d_w_in' % l, [D, 12352]), D,
                 [('z', 0, 4096), ('xbc', 4096, 6144), ('dt', 10240, 64), ('memq', 10304, 1024),
                  ('memg', 11328, 1024)], rowscale=gain)
    wo = prep_w(c, 'wo%d' % l, c.din('l%d_w_out' % l, [5120, D]), 5120, [('all', 0, D)])
    mem_prep_layer(c, l)
    c.areset()
    A = c.aalloc
    xtok = A('xtok', [128, 2, 4096], BF16)
    BT = A('BT', [128, 8, HT], BF16)
    CT = A('CT', [128, 8, HT], BF16)
    Btok = A('Btok', [128, 2, 1024], BF16)
    state = A('state', [128, 4096], F32)
    stbf = A('stbf', [128, 512], BF16)
    xw = A('xw', [128, 512], BF16)
    cvb = [A('cv%d' % i, [128, 3 + HT], F32) for i in range(2)]
    acc = [A('acc%d' % i, [128, HT], F32) for i in range(2)]
    xss = [A('xs%d' % i, [128, HT], BF16) for i in range(3)]
    carry = A('carry', [128, 48, 3], F32)
    cw = A('cw', [128, 48, 4], F32)
    cb = A('cb', [128, 48], F32)
    gn = A('gn', [128, 32], F32)
    wdt = A('wdt', [128, KD, 64], BF16)
    U = A('U', [128, 128], F32)
    onesf = A('onesf', [128, 128], F32)
    tab = A('tab', [128, 4, 64], F32)
    dtt = A('dtt', [128, 2, 64], F32)
    dtA = A('dtA', [128, 2, 64], F32)
    atok = A('atok', [128, 2, 64], F32)
    sel8 = A('sel8', [8, 8, 128], F32)
    aTg = A('aTg', [8, 128], F32)
    eAe = A('eAe', [128, 2, 64], F32)
    wtok = A('wtok', [128, 2, 64], F32)
    Gm = A('Gm', [128, 128], F32)
    szg = A('szg', [128, 4, HT], BF16)
    ty = A('ty', [128, 128], F32)
    ysq = A('ysq', [128, 128], BF16)
    rbc = A('rbc', [128, HT], F32)
    kb.dma('pool', cw, c.din('l%d_conv_w_pc' % l, [128, 48, 4]), writes=['cw'])
    kb.dma('pool', cb, c.din('l%d_conv_b_pc' % l, [128, 48]), writes=['cb'])
    kb.dma('pool', gn, c.din('l%d_gnorm_pc' % l, [128, 32]), writes=['gn'])
    kb.dma('pool', U, c.din('trilT', [128, 128]), writes=['U'])
    kb.dma('pool', sel8, c.din('sel8', [8, 8, 128]), writes=['sel8'])
    kb.dma('pool', tab[:, 0, :], c.din('l%d_dt_bias_bc' % l, [128, 64]), writes=['tab'])
    kb.dma('pool', tab[:, 1, :], c.din('l%d_A_log_bc' % l, [128, 64]), writes=['tab'])
    kb.dma('pool', tab[:, 2, :], c.din('l%d_D_bc' % l, [128, 64]), writes=['tab'])
    kb.op('act', lambda e: e.activation(tab[:, 3, :], tab[:, 1, :], AF.Exp), reads=['tab'], writes=['tab3'])
    kb.op('dve', lambda e: e.tensor_scalar(tab[:, 1, :], tab[:, 3, :], -1.0, None, ALU.mult), reads=['tab3', 'tab'], writes=['tab'])
    kb.op('dve', lambda e: e.memset(onesf, 1.0), writes=['onesf'])
    kb.op('dve', lambda e: e.memset(state, 0.0), writes=['state'])
    kb.op('dve', lambda e: e.memset(carry, 0.0), writes=['carry'])
    kb.dma('sp', wdt, win['dt'][0][0, :, :, 0:64], reads=[('wscr', 'win%d' % l, 'dt', 0)], writes=['wdt'])
    hkeys = [('hT', s) for s in range(4)]
    for tt in range(c.NT):
        load_norm_transpose(c, x_src, tt)
        for half in range(2):
            t0 = half * HT
            rhs_half = lambda k, t0=t0: (c.hT[:, k, t0:t0 + HT], hkeys)
            pend_tr = []
            for ci, m, pb, pk in proj_fm(c, win, 'win%d' % l, 'xbc', rhs_of=rhs_half, ntok=HT):
                i = ci % 2
                cv, ac = cvb[i], acc[i]
                kb.op('pool', lambda e, cv=cv, ci=ci: e.tensor_copy(cv[:, 0:3], carry[:, ci, :]),
                      reads=['carry'], writes=[('cv', i)])
                kb.op('act', lambda e, cv=cv, pb=pb: e.copy(cv[:, 3:3 + HT], pb[:, 0:HT]), reads=[pk], writes=[('cv', i)])
                kb.op('pool', lambda e, cv=cv, ci=ci: e.tensor_copy(carry[:, ci, :], cv[:, HT:HT + 3]),
                      reads=[('cv', i)], writes=['carry'])
                kb.op('dve', lambda e, cv=cv, ac=ac, ci=ci: e.tensor_scalar(
                    ac, cv[:, 3:3 + HT], cw[:, ci, 3:4], cb[:, ci:ci + 1], ALU.mult, ALU.add),
                    reads=[('cv', i), 'cw', 'cb'], writes=[('acc', i)])
                for j in range(3):
                    kb.op('dve', lambda e, cv=cv, ac=ac, ci=ci, j=j: e.scalar_tensor_tensor(
                        ac, cv[:, j:j + HT], cw[:, ci, j:j + 1], ac, ALU.mult, ALU.add),
                        reads=[('cv', i), 'cw', ('acc', i)], writes=[('acc', i)])
                if ci < 32:
                    o, ok = xss[ci % 3], ('xs', ci % 3)
                elif ci < 40:
                    o, ok = BT[:, ci - 32, :], 'BT'
                else:
                    o, ok = CT[:, ci - 40, :], 'CT'
                kb.op('act', lambda e, o=o, ac=ac: e.activation(o, ac, AF.Silu), reads=[('acc', i)], writes=[ok])
                if ci < 40:
                    pend_tr.append((ci, o, ok))
                while pend_tr and (pend_tr[0][0] <= ci - 2 or ci == 47):
                    ci_t, o, ok = pend_tr.pop(0)
                    tb, tk = next_bank(c)
                    tbv = tb[:].bitcast(BF16)
                    for s2 in range(2):
                        kb.op('pe', lambda e, o=o, s2=s2, tbv=tbv: e.transpose(
                            tbv[:, s2 * 128:(s2 + 1) * 128], o[:, s2 * 128:(s2 + 1) * 128], c.ident[:]),
                            reads=[ok, 'ident'], writes=[tk])
                    srcv = tbv[:, 0:256].rearrange("p (s f) -> p s f", s=2)
                    if ci_t < 32:
                        kb.op('dve', lambda e, srcv=srcv, ci_t=ci_t: e.tensor_copy(xtok[:, :, ci_t * 128:(ci_t + 1) * 128], srcv),
                              reads=[tk], writes=['xtok'])
                    else:
                        g = ci_t - 32
                        kb.op('dve', lambda e, srcv=srcv, g=g: e.tensor_copy(Btok[:, :, g * 128:(g + 1) * 128], srcv),
                              reads=[tk], writes=['Btok'])
            for s2 in range(2):
                pb, pk = next_bank(c)
                for k in range(KD):
                    kb.op('pe', lambda e, pb=pb, k=k, s2=s2, t0=t0: e.matmul(
                        pb[:, 0:64], c.hT[:, k, t0 + s2 * 128:t0 + (s2 + 1) * 128], wdt[:, k, :],
                        start=(k == 0), stop=(k == KD - 1)), reads=['wdt'] + hkeys, writes=[pk])
                kb.op('dve', lambda e, pb=pb, s2=s2: e.tensor_tensor(dtt[:, s2, :], pb[:, 0:64], tab[:, 0, :], ALU.add),
                      reads=[pk, 'tab'], writes=[('dtt', s2)])
                kb.op('act', lambda e, s2=s2: e.activation(dtt[:, s2, :], dtt[:, s2, :], AF.Exp),
                      reads=[('dtt', s2)], writes=[('dtt', s2)])
                kb.op('act', lambda e, s2=s2: e.activation(dtt[:, s2, :], dtt[:, s2, :], AF.Ln, bias=1.0),
                      reads=[('dtt', s2)], writes=[('dtt', s2)])
                kb.op('dve', lambda e, s2=s2: e.tensor_tensor(dtA[:, s2, :], dtt[:, s2, :], tab[:, 1, :], ALU.mult),
                      reads=[('dtt', s2), 'tab'], writes=[('dtA', s2)])
                pb1, pk1 = next_bank(c)
                kb.op('pe', lambda e, pb1=pb1, s2=s2: e.matmul(pb1[:, 0:64], U, dtA[:, s2, :], start=True, stop=True),
                      reads=['U', ('dtA', s2)], writes=[pk1])
                kb.op('act', lambda e, pb1=pb1, s2=s2: e.copy(atok[:, s2, :], pb1[:, 0:64]), reads=[pk1], writes=[('atok', s2)])
                pb3, pk3 = next_bank(c)
                kb.op('pe', lambda e, pb3=pb3, s2=s2: e.matmul(pb3[:, 0:64], onesf, dtA[:, s2, :], start=True, stop=True),
                      reads=['onesf', ('dtA', s2)], writes=[pk3])
                kb.op('act', lambda e, pb3=pb3, s2=s2: e.activation(eAe[:, s2, :], pb3[:, 0:64], AF.Exp),
                      reads=[pk3], writes=[('eAe', s2)])
                kb.op('dve', lambda e, pb3=pb3, s2=s2: e.tensor_tensor(wtok[:, s2, :], pb3[:, 0:64], atok[:, s2, :], ALU.subtract),
                      reads=[pk3, ('atok', s2)], writes=[('wtok', s2)])
                kb.op('act', lambda e, s2=s2: e.activation(wtok[:, s2, :], wtok[:, s2, :], AF.Exp),
                      reads=[('wtok', s2)], writes=[('wtok', s2)])
                kb.op('dve', lambda e, s2=s2: e.tensor_tensor(wtok[:, s2, :], wtok[:, s2, :], dtt[:, s2, :], ALU.mult),
                      reads=[('wtok', s2), ('dtt', s2)], writes=[('wtok', s2)])
            sres = [reserve_bank(c), reserve_bank(c)]
            z_it = proj_fm(c, win, 'win%d' % l, 'z', rhs_of=rhs_half, ntok=HT)
            tmpF = [(c.msc, 'msc'), (c.mrs, 'mrs')]
            DIb = [(c.mrd, 'mrd'), (c.msg, 'msg')]
            E4 = [(c.mq[:, j, :], ('mq', j)) for j in range(2)]
            M4 = [(c.mqs[:, j, :], ('mqs', j)) for j in range(2)]
            C4 = [(c.mP[:, j, :], ('mP', j)) for j in range(2)]
            v4 = lambda ap: ap.rearrange("p (h l) -> p h l", h=4)
            for g in range(8):
                for q4 in range(4):
                    _, _, zpb, zpk = next(z_it)
                    kb.op('act', lambda e, zpb=zpb, q4=q4: e.activation(szg[:, q4, :], zpb[:, 0:HT], AF.Silu),
                          reads=[zpk], writes=[('szg', q4)])
                for j in range(2):
                    h0 = g * 8 + j * 4
                    kb.op('dve', lambda e, j=j, h0=h0: e.tensor_tensor(
                        v4(DIb[j][0][:, :]), c.ident_f[:].unsqueeze(1).to_broadcast([128, 4, 128]),
                        tab[:, 2, h0:h0 + 4].unsqueeze(2).to_broadcast([128, 4, 128]), ALU.mult),
                        reads=['ident_f', 'tab'], writes=[DIb[j][1]])
                for s2 in range(2):
                    tc = slice(s2 * 128, (s2 + 1) * 128)
                    gb, gk = next_bank(c)
                    kb.op('pe', lambda e, gb=gb, g=g, tc=tc: e.matmul(gb[:, 0:128], BT[:, g, tc], CT[:, g, tc], start=True, stop=True),
                          reads=['BT', 'CT'], writes=[gk])
                    kb.op('dve', lambda e, gb=gb: e.tensor_tensor(Gm, gb[:, 0:128], U, ALU.mult), reads=[gk, 'U'], writes=['Gm'])
                    kb.op('pool', lambda e, g=g: e.tensor_copy(stbf, state[:, g * 512:(g + 1) * 512]),
                          reads=[('state', g)], writes=['stbf'])
                    atb, atk = next_bank(c)
                    kb.op('pe', lambda e, atb=atb, g=g, s2=s2: e.matmul(atb[0:8, 0:128], dtA[:, s2, g * 8:(g + 1) * 8], U, start=True, stop=True),
                          reads=['U', ('dtA', s2)], writes=[atk])
                    kb.op('act', lambda e, atb=atb: e.copy(aTg, atb[0:8, 0:128]), reads=[atk], writes=['aTg'])
                    abk = []
                    for j in range(2):
                        ab, ak = next_bank(c)
                        for hl in range(4):
                            kb.op('pe', lambda e, ab=ab, hl=hl, j=j: e.matmul(ab[:, hl * 128:(hl + 1) * 128], sel8[:, j * 4 + hl, :], aTg,
                                                                              start=True, stop=True),
                                  reads=['sel8', 'aTg'], writes=[ak])
                        abk.append((ab, ak))
                    for j in range(2):
                        h0 = g * 8 + j * 4
                        ab, ak = abk[j]
                        F, Fk = tmpF[j]
                        kb.op('dve', lambda e, ab=ab, F=F, h0=h0, s2=s2: e.tensor_tensor(
                            v4(F[:, :]), v4(ab[:, :]), atok[:, s2, h0:h0 + 4].unsqueeze(2).to_broadcast([128, 4, 128]), ALU.subtract),
                            reads=[ak, ('atok', s2)], writes=[Fk])
                        kb.op('dve', lambda e, F=F: e.tensor_scalar_min(F[:, :], F[:, :], 0.0), reads=[Fk], writes=[Fk])
                        kb.op('act', lambda e, F=F, j=j: e.activation(E4[j][0], F[:, :], AF.Exp), reads=[Fk], writes=[E4[j][1]])
                        kb.op('act', lambda e, ab=ab, j=j: e.activation(C4[j][0], ab[:, :], AF.Exp), reads=[ak], writes=[C4[j][1]])
                        kb.op('dve', lambda e, F=F, h0=h0, s2=s2: e.tensor_tensor(
                            v4(F[:, :]), Gm.unsqueeze(1).to_broadcast([128, 4, 128]),
                            dtt[:, s2, h0:h0 + 4].unsqueeze(2).to_broadcast([128, 4, 128]), ALU.mult),
                            reads=['Gm', ('dtt', s2), E4[j][1]], writes=[Fk])
                        kb.op('dve', lambda e, F=F, j=j: e.tensor_tensor(F[:, :], F[:, :], DIb[j][0][:, :], ALU.add),
                              reads=[Fk, DIb[j][1]], writes=[Fk])
                        kb.op('dve', lambda e, F=F, j=j: e.tensor_tensor(M4[j][0], E4[j][0], F[:, :], ALU.mult),
                              reads=[Fk, E4[j][1]], writes=[M4[j][1]])
                        kb.op('dve', lambda e, j=j, g=g, tc=tc: e.tensor_tensor(
                            v4(C4[j][0]), v4(C4[j][0]), CT[:, g, tc].unsqueeze(1).to_broadcast([128, 4, 128]), ALU.mult),
                            reads=['CT', C4[j][1]], writes=[C4[j][1]])
                    ybank = [None, None]
                    for hh in range(8):
                        h = g * 8 + hh
                        j, hl = hh // 4, hh % 4
                        yb, yk = next_bank(c)
                        ybank[hh % 2] = (yb, yk)
                        pc = h // 2
                        kb.op('pe', lambda e, yb=yb, s2=s2, pc=pc, j=j, hl=hl: e.matmul(
                            yb[:, 0:128], xtok[:, s2, pc * 128:(pc + 1) * 128], M4[j][0][:, hl * 128:(hl + 1) * 128], start=True, stop=False),
                            reads=['xtok', M4[j][1]], writes=[yk])
                        pl = hh // 2
                        kb.op('pe', lambda e, yb=yb, pl=pl, j=j, hl=hl: e.matmul(
                            yb[:, 0:128], stbf[:, pl * 128:(pl + 1) * 128], C4[j][0][:, hl * 128:(hl + 1) * 128], start=False, stop=True),
                            reads=['stbf', C4[j][1]], writes=[yk])
                        if hh % 2 == 1:
                            (ya, yak), (yb2, ybk) = ybank
                            kb.op('dve', lambda e, ya=ya, pl=pl, tc=tc: e.tensor_tensor(
                                ty[0:64, :], ya[0:64, 0:128], szg[0:64, pl, tc], ALU.mult),
                                reads=[yak, ('szg', pl)], writes=['ty'])
                            kb.op('dve', lambda e, yb2=yb2, pl=pl, tc=tc: e.tensor_tensor(
                                ty[64:128, :], yb2[64:128, 0:128], szg[64:128, pl, tc], ALU.mult),
                                reads=[ybk, ('szg', pl)], writes=['ty'])
                            col = slice(t0 + s2 * 128, t0 + (s2 + 1) * 128)
                            kb.op('act', lambda e, pc=pc, col=col: e.copy(c.yT[:, pc, col], ty), reads=['ty'], writes=[('yT', pc)])
                            kb.op('act', lambda e: e.activation(ysq, ty, AF.Square), reads=['ty'], writes=['ysq'])
                            sb_, sk_, _ = sres[s2]
                            kb.op('pe', lambda e, sb_=sb_, pc=pc: e.matmul(sb_[:, 0:128], c.ones_bf[:], ysq, start=(pc == 0), stop=(pc == 31)),
                                  reads=['ones_bf', 'ysq'], writes=[sk_])
                    kb.op('dve', lambda e, g=g, s2=s2: e.tensor_tensor(
                        xw.rearrange("p (h j) -> p h j", h=8), xtok[:, s2, g * 512:(g + 1) * 512].rearrange("p (h j) -> p h j", h=8),
                        wtok[:, s2, g * 8:(g + 1) * 8].unsqueeze(2).to_broadcast([128, 8, 64]), ALU.mult),
                        reads=['xtok', ('wtok', s2)], writes=['xw'])
                    nb_, nk_ = next_bank(c)
                    kb.op('pe', lambda e, nb_=nb_, s2=s2, g=g: e.matmul(nb_[:, :], Btok[:, s2, g * 128:(g + 1) * 128], xw, start=True, stop=True),
                          reads=['Btok', 'xw'], writes=[nk_])
                    for hh in range(8):
                        h = g * 8 + hh
                        kb.op('dve', lambda e, nb_=nb_, hh=hh, h=h, s2=s2: e.scalar_tensor_tensor(
                            state[:, h * 64:(h + 1) * 64], state[:, h * 64:(h + 1) * 64], eAe[:, s2, h:h + 1],
                            nb_[:, hh * 64:(hh + 1) * 64], ALU.mult, ALU.add),
                            reads=[nk_, ('eAe', s2), ('state', g)], writes=[('state', g)])
            for s2 in range(2):
                sb_, sk_, sidx = sres[s2]
                kb.op('dve', lambda e, sb_=sb_, s2=s2: e.tensor_scalar(rbc[:, s2 * 128:(s2 + 1) * 128], sb_[:, 0:128], 1.0 / 4096, EPS,
                                                                       ALU.mult, ALU.add), reads=[sk_], writes=['rbc'])
                release_bank(c, sidx)
            kb.op('act', lambda e: e.activation(rbc, rbc, AF.Sqrt), reads=['rbc'], writes=['rbc'])
            kb.op('dve', lambda e: e.reciprocal(rbc, rbc), reads=['rbc'], writes=['rbc'])
            for pc in range(32):
                kb.op('dve', lambda e, pc=pc, t0=t0: e.scalar_tensor_tensor(
                    c.yT[:, pc, t0:t0 + HT], c.yT[:, pc, t0:t0 + HT], gn[:, pc:pc + 1], rbc, ALU.mult, ALU.mult),
                    reads=[('yT', pc), 'gn', 'rbc'], writes=[('yT', pc)])
        mem_branch(c, l, win)
        wout_phase(c, l, wo, x_src, dst, tt)


I32 = mybir.dt.int32
NSA_SCALE = 128 ** -0.5


def layer1(c, l, x_src, dst):
    kb = c.kb
    S = c.S
    NQT = S // 128
    NSEL = S // 64
    NC = S // 16 - 1
    NCT = (NC + 127) // 128
    NCP = NCT * 128
    prep_begin(c)
    gain = load_pc(c, 0, 'l%d_norm_pc' % l, KD)
    segs = [('q', 0, 4096), ('kc', 4096, 512), ('vc', 4608, 512), ('ks', 5120, 512), ('vs', 5632, 512),
            ('kw', 6144, 512), ('vw', 6656, 512), ('gl', 7168, 96), ('gate', 7264, 4096),
            ('memq', 11360, 1024), ('memg', 12384, 1024)]
    wname = 'win%d' % l
    win = prep_w(c, wname, c.din('l%d_w_in' % l, [D, 13408]), D, segs, rowscale=gain)
    wo = prep_w(c, 'wo%d' % l, c.din('l%d_w_out' % l, [5120, D]), 5120, [('all', 0, D)])
    w1k = prep_w(c, 'w1k', c.din('l%d_cmp_k_w1' % l, [4096, 128]), 4096, [('all', 0, 128)])
    w1v = prep_w(c, 'w1v', c.din('l%d_cmp_v_w1' % l, [4096, 128]), 4096, [('all', 0, 128)])
    w2k = prep_w(c, 'w2k', c.din('l%d_cmp_k_w2' % l, [128, 128]), 128, [('all', 0, 128)])
    w2v = prep_w(c, 'w2v', c.din('l%d_cmp_v_w2' % l, [128, 128]), 128, [('all', 0, 128)])
    mem_prep_layer(c, l)
    QT = c.dscr('nsa_QT', [32, 128, S], BF16)
    KT = c.dscr('nsa_KT', [3, 4, 128, S], BF16)
    VcT = c.dscr('nsa_VcT', [4, 128, S], BF16)
    Vsw = c.dscr('nsa_Vsw', [2, S, 512], BF16)
    Gs = c.dscr('nsa_Gs', [S, 96], F32)
    SG = c.dscr('nsa_SG', [32, 128, S], BF16)
    Y = c.dscr('nsa_Y', [32, 128, S], BF16)
    cosD = c.dscr('nsa_cos', [128, S], F32)
    sinD = c.dscr('nsa_sin', [128, S], F32)

    c.areset()
    A = c.aalloc
    PW = min(1024, S)
    pi = A('pi', [128, PW], I32)
    ang = A('ang', [128, PW], F32)
    uu = A('uu', [128, PW], F32)
    nn_i = A('nn_i', [128, PW], I32)
    nn_f = A('nn_f', [128, PW], F32)
    inv = A('inv', [128, 1], F32)
    kb.dma('pool', inv, c.din('rope_inv_pc', [128, 1]), writes=['inv'])
    pos_in = c.din('pos_bc', [128, S], I32)
    for p0 in range(0, S, PW):
        kb.dma('pool', pi, pos_in[:, p0:p0 + PW], writes=['pi'])
        kb.op('dve', lambda e: e.tensor_copy(ang, pi), reads=['pi'], writes=['ang'])
        kb.op('dve', lambda e: e.tensor_scalar(ang, ang, inv[:, 0:1], None, ALU.mult), reads=['ang', 'inv'], writes=['ang'])
        for which, off, dstD in (('sin', 0.5, sinD), ('cos', 0.75, cosD)):
            kb.op('dve', lambda e, off=off: e.tensor_scalar(uu, ang, 1.0 / (2 * np.pi), off, ALU.mult, ALU.add),
                  reads=['ang'], writes=['uu'])
            kb.op('dve', lambda e: e.tensor_copy(nn_i, uu), reads=['uu'], writes=['nn_i'])
            kb.op('dve', lambda e: e.tensor_copy(nn_f, nn_i), reads=['nn_i'], writes=['nn_f'])
            kb.op('dve', lambda e: e.tensor_tensor(uu, uu, nn_f, ALU.subtract), reads=['uu', 'nn_f'], writes=['uu'])
            kb.op('dve', lambda e: e.tensor_scalar(nn_f, uu, 0.0, None, ALU.is_lt), reads=['uu'], writes=['nn_f'])
            kb.op('dve', lambda e: e.tensor_tensor(uu, uu, nn_f, ALU.add), reads=['uu', 'nn_f'], writes=['uu'])
            kb.op('act', lambda e: e.activation(uu, uu, AF.Sin, scale=2 * np.pi, bias=-np.pi), reads=['uu'], writes=['uu'])
            kb.dma('pool', dstD[:, p0:p0 + PW], uu, reads=['uu'], writes=['ropeD'])

    c.areset()
    c.rows2()
    cosT = A('cosT', [128, TT], F32)
    sinT = A('sinT', [128, TT], F32)
    NRB = 3
    xsqs = [A('xsq%d' % i, [128, TT], BF16) for i in range(NRB)]
    rsbs = [A('rsb%d' % i, [128, TT], F32) for i in range(NRB)]
    xns = [A('xn%d' % i, [128, TT], BF16) for i in range(NRB)]
    t1s = [A('t1_%d' % i, [128, TT], F32) for i in range(NRB)]
    t2s = [A('t2_%d' % i, [128, TT], F32) for i in range(NRB)]
    ob = [A('ob%d' % i, [128, TT], BF16) for i in range(4)]
    obr = [0]
    gsb = [A('gsb%d' % i, [128, 96], F32) for i in range(2)]
    prot = A('prot', [128, 128], BF16)
    protf = A('protf', [128, 128], F32)
    qkn = A('qkn', [128, 2], F32)
    wgl = A('wgl', [128, KD, 96], BF16)
    kb.dma('pool', protf, c.din('rope_prot', [128, 128]), writes=['protf'])
    kb.op('dve', lambda e: e.tensor_copy(prot, protf), reads=['protf'], writes=['prot'])
    kb.dma('pool', qkn[:, 0:1], c.din('l%d_qnorm_pc' % l, [128, 1]), writes=['qkn'])
    kb.dma('pool', qkn[:, 1:2], c.din('l%d_knorm_pc' % l, [128, 1]), writes=['qkn'])
    kb.dma('sp', wgl, win['gl'][0][0, :, :, 0:96], reads=[('wscr', wname, 'gl', 0)], writes=['wgl'])

    def next_ob():
        i = obr[0]
        obr[0] = (i + 1) % 4
        return ob[i], ('ob', i)

    def nr_stage1(job, bi):
        pb, pk, gcol, dst_ap = job
        xsq = xsqs[bi]
        kb.op('act', lambda e: e.activation(xsq, pb[:, :], AF.Square), reads=[pk], writes=[('xsq', bi)])

    def nr_stage2(job, bi):
        pb, pk, gcol, dst_ap = job
        xsq, rsb, xn = xsqs[bi], rsbs[bi], xns[bi]
        p2, k2 = next_bank(c)
        kb.op('pe', lambda e: e.matmul(p2[:, :], c.ones_bf[:], xsq, start=True, stop=True), reads=['ones_bf', ('xsq', bi)], writes=[k2])
        kb.op('act', lambda e: e.activation(rsb, p2[:, :], AF.Ln, scale=1.0 / 128, bias=EPS), reads=[k2], writes=[('rsb', bi)])
        kb.op('act', lambda e: e.activation(rsb, rsb, AF.Exp, scale=-0.5), reads=[('rsb', bi)], writes=[('rsb', bi)])
        kb.op('dve', lambda e: e.scalar_tensor_tensor(xn, pb[:, :], qkn[:, gcol:gcol + 1], rsb, ALU.mult, ALU.mult),
              reads=[pk, 'qkn', ('rsb', bi)], writes=[('xn', bi)])

    def nr_stage3(job, bi):
        pb, pk, gcol, dst_ap = job
        xn, t1, t2 = xns[bi], t1s[bi], t2s[bi]
        p3, k3 = next_bank(c)
        kb.op('pe', lambda e: e.matmul(p3[:, :], prot, xn, start=True, stop=True), reads=['prot', ('xn', bi)], writes=[k3])
        kb.op('dve', lambda e: e.tensor_tensor(t1, xn, cosT, ALU.mult), reads=[('xn', bi), 'cosT'], writes=[('t1', bi)])
        kb.op('dve', lambda e: e.tensor_tensor(t2, p3[:, :], sinT, ALU.mult), reads=[k3, 'sinT'], writes=[('t2', bi)])
        o, ok = next_ob()
        kb.op('pool', lambda e: e.tensor_tensor(o, t1, t2, ALU.add), reads=[('t1', bi), ('t2', bi)], writes=[ok])
        kb.dma('pool', dst_ap, o, reads=[ok], writes=['nsaD'])

    def norm_rope_pipeline(jobgen):
        jobs = []
        done = False
        i = 0
        while True:
            if not done:
                try:
                    jobs.append(next(jobgen))
                    nr_stage1(jobs[i], i % NRB)
                except StopIteration:
                    done = True
            if 0 <= i - 1 < len(jobs):
                nr_stage2(jobs[i - 1], (i - 1) % NRB)
            if 0 <= i - 2 < len(jobs):
                nr_stage3(jobs[i - 2], (i - 2) % NRB)
            i += 1
            if done and i - 2 >= len(jobs):
                break

    for tt in range(c.NT):
        tsl = slice(tt * TT, (tt + 1) * TT)
        load_norm_transpose(c, x_src, tt)
        kb.dma('pool', cosT, cosD[:, tsl], reads=['ropeD'], writes=['cosT'])
        kb.dma('pool', sinT, sinD[:, tsl], reads=['ropeD'], writes=['sinT'])
        def jobgen(tsl=tsl):
            for ci, m, pb, pk in proj_fm(c, win, wname, 'q'):
                yield (pb, pk, 0, QT[ci, :, tsl])
            for ki, sname in enumerate(('kc', 'ks', 'kw')):
                for ci, m, pb, pk in proj_fm(c, win, wname, sname):
                    yield (pb, pk, 1, KT[ki, ci, :, tsl])
        norm_rope_pipeline(jobgen())
        for ci, m, pb, pk in proj_fm(c, win, wname, 'vc'):
            o, ok = next_ob()
            kb.op('act', lambda e, o=o, pb=pb: e.copy(o, pb[:, :]), reads=[pk], writes=[ok])
            kb.dma('pool', VcT[ci, :, tsl], o, reads=[ok], writes=['nsaD'])
        for ci, m, pb, pk in proj_fm(c, win, wname, 'gate'):
            o, ok = next_ob()
            kb.op('act', lambda e, o=o, pb=pb: e.activation(o, pb[:, :], AF.Silu), reads=[pk], writes=[ok])
            kb.dma('pool', SG[ci, :, tsl], o, reads=[ok], writes=['nsaD'])
        for vi, sname in enumerate(('vs', 'vw')):
            wt, wk = load_w_dep(c, win[sname][0][0, :, :, :], ('wscr', wname, sname, 0))
            for s4 in range(4):
                pb, pk = next_bank(c)
                for k in range(KD):
                    kb.op('pe', lambda e, pb=pb, wt=wt, k=k, s4=s4: e.matmul(
                        pb[:, :], c.hT[:, k, s4 * 128:(s4 + 1) * 128], wt[:, k, :], start=(k == 0), stop=(k == KD - 1)),
                        reads=[wk, ('hT', s4)], writes=[pk])
                o, ok = next_ob()
                kb.op('act', lambda e, o=o, pb=pb: e.copy(o, pb[:, :]), reads=[pk], writes=[ok])
                r0 = tt * TT + s4 * 128
                kb.dma('pool', Vsw[vi, r0:r0 + 128, :], o, reads=[ok], writes=['nsaD'])
        for s4 in range(4):
            pb, pk = next_bank(c)
            for k in range(KD):
                kb.op('pe', lambda e, pb=pb, k=k, s4=s4: e.matmul(
                    pb[:, 0:96], c.hT[:, k, s4 * 128:(s4 + 1) * 128], wgl[:, k, :], start=(k == 0), stop=(k == KD - 1)),
                    reads=['wgl', ('hT', s4)], writes=[pk])
            gi = s4 % 2
            kb.op('act', lambda e, pb=pb, gi=gi: e.activation(gsb[gi], pb[:, 0:96], AF.Sigmoid), reads=[pk], writes=[('gsb', gi)])
            r0 = tt * TT + s4 * 128
            kb.dma('pool', Gs[r0:r0 + 128, :], gsb[gi], reads=[('gsb', gi)], writes=['nsaD'])

    c.areset()
    kcT = A('kcT', [128, 4, NCP], BF16)
    vcm = A('vcm', [128, 4, NCT, 128], BF16)
    KsT = A('KsT', [128, S], BF16)
    KwT = A('KwT', [128, S], BF16)
    Vs = A('Vs', [128, NQT, 128], BF16)
    Vw = A('Vw', [128, NQT, 128], BF16)
    Efull = A('Efull', [64, S], BF16)
    Eff = None
    aggf = A('aggf', [128, NCT, NSEL], F32)
    agg = A('agg', [128, NCT, NSEL], BF16)
    onesf = A('onesf', [128, 128], F32)
    qT = A('qT', [128, 8, 128], BF16)
    PT = [A('PT%d' % i, [128, 1024], BF16) for i in range(2)]
    PcDg = A('PcDg', [128, 1024], F32)
    Pc = PcDg.bitcast(BF16).rearrange("p (a b) -> p a b", a=2)
    Rt = A('Rt', [128, 1024], F32)
    acc = A('acc', [128, 1024], F32)
    Dg = PcDg
    DGK = [('Pc', 0), ('Pc', 1)]
    sgt = A('sgt', [128, 8, 128], BF16)
    yo = A('yo', [128, 8, 128], BF16)
    gat = A('gat', [128, 96], F32)
    tabs = A('tabs', [128, 2, NSEL], F32)
    sc1 = A('sc1', [128, NSEL], F32)
    sc2 = A('sc2', [128, NSEL], F32)
    m8 = A('m8', [128, 16], F32)
    selb = A('selb', [128, 64], BF16)
    selT4 = A('selT4', [64, 512], BF16)
    hTc = A('hTc', [128, NCP], BF16)
    cj = A('cj', [128, 2], F32)
    posT = A('posT', [128, 32], BF16)
    posTf = A('posTf', [128, 32], F32)
    w2t = A('w2t', [128, 128], BF16)
    kb.op('dve', lambda e: e.memset(onesf, 1.0), writes=['onesf'])
    kb.op('dve', lambda e: e.memset(selb, 0.0), writes=['selb'])
    kb.dma('pool', aggf, c.din('nsa_agg', [128, NCT, NSEL]), writes=['aggf'])
    kb.op('dve', lambda e: e.tensor_copy(agg, aggf), reads=['aggf'], writes=['agg'])
    kb.dma('pool', posTf, c.din('l%d_cmp_posT' % l, [128, 32]), writes=['posTf'])
    kb.op('dve', lambda e: e.tensor_copy(posT, posTf), reads=['posTf'], writes=['posT'])
    ef_in = c.din('nsa_E', [64, S])
    for e0 in range(0, S, 1024):
        ew = min(1024, S - e0)
        kb.dma('pool', Rt[0:64, 0:ew], ef_in[:, e0:e0 + ew], writes=['Rt'])
        kb.op('dve', lambda e, e0=e0, ew=ew: e.tensor_copy(Efull[:, e0:e0 + ew], Rt[0:64, 0:ew]), reads=['Rt'], writes=['Efull'])
    w1t = KwT[:, 0:4096].rearrange("p (t j) -> p t j", t=32) if S >= 4096 else None
    if w1t is None:
        w1t = A('w1t', [128, 32, 128], BF16)
        w1key = 'w1t'
    else:
        w1key = 'KwT'
    for kind, (w1s, w2s, srcT) in enumerate(((w1k, w2k, None), (w1v, w2v, None))):
        kb.dma('sp', w1t, w1s['all'][0][0, :, :, 0:128], reads=[('wscr', 'w1k' if kind == 0 else 'w1v', 'all', 0)], writes=[w1key])
        kb.dma('sp', w2t, w2s['all'][0][0, :, 0, 0:128], reads=[('wscr', 'w2k' if kind == 0 else 'w2v', 'all', 0)], writes=['w2t'])
        pcj, kcj = next_bank(c)
        for tau in range(32):
            kb.op('pe', lambda e, tau=tau, pcj=pcj: e.matmul(pcj[:, 0:1], w1t[:, tau, :], posT[:, tau:tau + 1],
                                                              start=(tau == 0), stop=(tau == 31)),
                  reads=[w1key, 'posT'], writes=[kcj])
        kb.op('act', lambda e, pcj=pcj, kind=kind: e.copy(cj[:, kind:kind + 1], pcj[:, 0:1]), reads=[kcj], writes=['cj'])
        for hk in range(4):
            srcD = KT[0, hk] if kind == 0 else VcT[hk]
            kb.dma('pool', KsT, srcD, writes=['KsT'])
            pp, pk = next_bank(c)
            for tau in range(32):
                kb.op('pe', lambda e, tau=tau, pp=pp: e.matmul(pp[:, 0:NC], w1t[:, tau, :], KsT[:, tau:tau + 16 * (NC - 1) + 1:16],
                                                                start=(tau == 0), stop=(tau == 31)),
                      reads=[w1key, 'KsT'], writes=[pk])
            kb.op('dve', lambda e: e.memset(hTc, 0.0), writes=['hTc'])
            kb.op('act', lambda e, pp=pp, kind=kind: e.activation(hTc[:, 0:NC], pp[:, 0:NC], AF.Silu, bias=cj[:, kind:kind + 1]),
                  reads=[pk, 'cj'], writes=['hTc'])
            if kind == 0:
                p2, k2 = next_bank(c)
                kb.op('pe', lambda e, p2=p2: e.matmul(p2[:, 0:NCP], w2t, hTc, start=True, stop=True), reads=['w2t', 'hTc'], writes=[k2])
                kb.op('act', lambda e, p2=p2, hk=hk: e.copy(kcT[:, hk, :], p2[:, 0:NCP]), reads=[k2], writes=['kcT'])
            else:
                for ct in range(NCT):
                    p2, k2 = next_bank(c)
                    kb.op('pe', lambda e, p2=p2, ct=ct: e.matmul(p2[:, 0:128], hTc[:, ct * 128:(ct + 1) * 128], w2t, start=True, stop=True),
                          reads=['w2t', 'hTc'], writes=[k2])
                    kb.op('act', lambda e, p2=p2, hk=hk, ct=ct: e.copy(vcm[:, hk, ct, :], p2[:, 0:128]), reads=[k2], writes=['vcm'])

    def exp_tile(sbanks, P):
        for hf in range(2):
            pb, pk = sbanks[hf]
            kb.op('act', lambda e, pb=pb, hf=hf, P=P: e.activation(P[0][:, hf * 512:(hf + 1) * 512], pb[:, :], AF.Exp, scale=NSA_SCALE),
                  reads=[pk], writes=[P[1]])

    def score_tile(lhsT, lkey):
        sb2 = []
        for hf in range(2):
            pb, pk = next_bank(c)
            kb.op('pe', lambda e, pb=pb, hf=hf: e.matmul(pb[:, :], lhsT, qT[:, hf * 4:(hf + 1) * 4, :], start=True, stop=True),
                  reads=[lkey, 'qT'], writes=[pk])
            sb2.append((pb, pk))
        return sb2

    def amask(P, base, cm, qstep):
        pv = P[0].rearrange("p (h q) -> p h q", h=8)
        kb.op('pool', lambda e: e.affine_select(out=pv, in_=pv, pattern=[[0, 8], [qstep, 128]], compare_op=ALU.is_ge,
                                                fill=0.0, base=base, channel_multiplier=cm),
              reads=[P[1]], writes=[P[1]])

    def gate_bc(hk, br):
        c0 = hk * 24 + br
        kb.op('dve', lambda e, c0=c0: e.tensor_tensor(
            Dg.rearrange("p (h q) -> p h q", h=8), c.ident_f[:].unsqueeze(1).to_broadcast([128, 8, 128]),
            gat[:, c0:c0 + 22:3].unsqueeze(2).to_broadcast([128, 8, 128]), ALU.mult),
            reads=['ident_f', 'gat'], writes=DGK)
        gb = []
        for hf in range(2):
            pb, pk = next_bank(c)
            kb.op('pe', lambda e, pb=pb, hf=hf: e.matmul(pb[:, :], onesf, Dg[:, hf * 512:(hf + 1) * 512], start=True, stop=True),
                  reads=['onesf'] + DGK, writes=[pk])
            gb.append((pb, pk))
        return gb

    for hk in range(4):
        kb.dma('pool', KsT, KT[1, hk], writes=['KsT'])
        kb.dma('pool', KwT, KT[2, hk], writes=['KwT'])
        kb.dma('pool', Vs, Vsw[0, :, hk * 128:(hk + 1) * 128].rearrange("(t k) d -> k t d", k=128), writes=['Vs'])
        kb.dma('pool', Vw, Vsw[1, :, hk * 128:(hk + 1) * 128].rearrange("(t k) d -> k t d", k=128), writes=['Vw'])
        for qt in range(NQT):
            qsl = slice(qt * 128, (qt + 1) * 128)
            kb.dma('pool', qT, QT[hk * 8:(hk + 1) * 8, :, qsl].rearrange("h d q -> d h q"), writes=['qT'])
            kb.dma('pool', sgt, SG[hk * 8:(hk + 1) * 8, :, qsl].rearrange("h d q -> d h q"), writes=['sgt'])
            kb.dma('pool', gat, Gs[qsl, :], writes=['gat'])
            kb.dma('pool', tabs, c.din_cached('nsa_tabs', [NQT, 128, 2, NSEL])[qt], writes=['tabs'])
            cts = []
            for ct in range(NCT):
                base = 128 * qt - 2048 * ct - 31
                if base + 127 < 0:
                    continue
                cts.append((ct, base))
            rsb_ = [reserve_bank(c), reserve_bank(c)]
            for n_, (ct, base) in enumerate(cts):
                sb2 = score_tile(kcT[:, hk, ct * 128:(ct + 1) * 128], 'kcT')
                P = (Pc[:, ct, :], ('Pc', ct))
                exp_tile(sb2, P)
                if base - 2032 < 0:
                    amask(P, base, -16, 1)
                for hf in range(2):
                    rb, rk, _ = rsb_[hf]
                    kb.op('pe', lambda e, rb=rb, hf=hf, P=P, n_=n_, nl=len(cts): e.matmul(rb[:, :], c.ones_bf[:], P[0][:, hf * 512:(hf + 1) * 512],
                                                                            start=(n_ == 0), stop=(n_ == nl - 1)),
                          reads=['ones_bf', P[1]], writes=[rk])
            if cts:
                for hf in range(2):
                    rb, rk, ridx = rsb_[hf]
                    hs = slice(hf * 512, (hf + 1) * 512)
                    kb.op('act', lambda e, rb=rb, hs=hs: e.activation(Rt[:, hs], rb[:, :], AF.Ln, bias=1e-30), reads=[rk], writes=['Rt'])
                    kb.op('act', lambda e, hs=hs: e.activation(Rt[:, hs], Rt[:, hs], AF.Exp, scale=-1.0), reads=['Rt'], writes=['Rt'])
                    for (ct, base) in cts:
                        kb.op('dve', lambda e, hs=hs, ct=ct: e.tensor_tensor(Pc[:, ct, hs], Pc[:, ct, hs], Rt[:, hs], ALU.mult),
                              reads=['Rt', ('Pc', ct)], writes=[('Pc', ct)])
            for hf in range(2):
                release_bank(c, rsb_[hf][2])
            if cts:
                ib, ik = next_bank(c)
                nmm = 8 * len(cts)
                i_ = 0
                for (ct, base) in cts:
                    for h in range(8):
                        kb.op('pe', lambda e, ib=ib, ct=ct, h=h, i_=i_, nmm=nmm: e.matmul(ib[:, 0:NSEL], Pc[:, ct, h * 128:(h + 1) * 128], agg[:, ct, :],
                                                                                 start=(i_ == 0), stop=(i_ == nmm - 1)),
                              reads=[('Pc', ct), 'agg'], writes=[ik])
                        i_ += 1
                kb.op('dve', lambda e, ib=ib: e.tensor_tensor(sc1, ib[:, 0:NSEL], tabs[:, 0, :], ALU.mult), reads=[ik, 'tabs'], writes=['sc1'])
                kb.op('dve', lambda e: e.tensor_tensor(sc1, sc1, tabs[:, 1, :], ALU.add), reads=['sc1', 'tabs'], writes=['sc1'])
            else:
                kb.op('dve', lambda e: e.tensor_copy(sc1, tabs[:, 1, :]), reads=['tabs'], writes=['sc1'])
            kb.op('dve', lambda e: e.max(out=m8[:, 0:8], in_=sc1), reads=['sc1'], writes=['m8'])
            kb.op('dve', lambda e: e.match_replace(out=sc2, in_to_replace=m8[:, 0:8], in_values=sc1, imm_value=-1e9),
                  reads=['sc1', 'm8'], writes=['sc2'])
            kb.op('dve', lambda e: e.max(out=m8[:, 8:16], in_=sc2), reads=['sc2'], writes=['m8'])
            kb.op('dve', lambda e: e.tensor_scalar(sc2, sc1, m8[:, 15:16], None, ALU.is_ge), reads=['sc1', 'm8'], writes=['sc2'])
            kb.op('dve', lambda e: e.tensor_tensor(selb[:, 0:NSEL], sc2, tabs[:, 0, :], ALU.mult), reads=['sc2', 'tabs'], writes=['selb'])
            if 'dbg_sel' in DEBUG_OUT:
                if not hasattr(c, 'dbg_sel'):
                    c.dbg_sel = c.dscr('dbg_sel', [4, NQT, 128, NSEL], F32)
                    c.dbg_sc1 = c.dscr('dbg_sc1', [4, NQT, 128, NSEL], F32)
                kb.dma('pool', c.dbg_sel[hk, qt], sc2, reads=['sc2'], writes=['dbgsel'])
                kb.dma('pool', c.dbg_sc1[hk, qt], sc1, reads=['sc1'], writes=['dbgsel'])
            tb, tk = next_bank(c)
            tbv = tb[:].bitcast(BF16)
            kb.op('pe', lambda e, tbv=tbv: e.transpose(tbv[0:64, 0:128], selb[:, 0:64], c.ident[:]), reads=['selb', 'ident'], writes=[tk])
            for r4 in range(4):
                kb.op('dve', lambda e, tbv=tbv, r4=r4: e.tensor_scalar(selT4[:, r4 * 128:(r4 + 1) * 128], tbv[0:64, 0:128], 1.0, 29952.0,
                                                                      ALU.subtract, ALU.mult), reads=[tk], writes=['selT4'])
            if cts:
                ob2 = []
                for hf in range(2):
                    pb, pk = next_bank(c)
                    for n_, (ct, base) in enumerate(cts):
                        kb.op('pe', lambda e, pb=pb, hf=hf, ct=ct, n_=n_, hk=hk, nl=len(cts): e.matmul(pb[:, :], vcm[:, hk, ct, :], Pc[:, ct, hf * 512:(hf + 1) * 512],
                                                                                  start=(n_ == 0), stop=(n_ == nl - 1)),
                              reads=['vcm', ('Pc', ct)], writes=[pk])
                    ob2.append((pb, pk))
                gb = gate_bc(hk, 0)
                for hf in range(2):
                    hs = slice(hf * 512, (hf + 1) * 512)
                    kb.op('act', lambda e, hf=hf, hs=hs, gb=gb: e.copy(Rt[:, hs], gb[hf][0][:, :]), reads=[gb[hf][1]], writes=['Rt'])
                    kb.op('dve', lambda e, hf=hf, hs=hs, ob2=ob2: e.tensor_tensor(acc[:, hs], Rt[:, hs], ob2[hf][0][:, :], ALU.mult),
                          reads=['Rt', ob2[hf][1]], writes=['acc'])
                if 0 not in NSA_BR:
                    kb.op('dve', lambda e: e.memset(acc, 0.0), writes=['acc'])
            else:
                kb.op('dve', lambda e: e.memset(acc, 0.0), writes=['acc'])
            for br in (1, 2):
                if br == 1:
                    kts = list(range(0, qt + 1))
                    Ksrc, Kkey, Vsrc, Vkey = KsT, 'KsT', Vs, 'Vs'
                else:
                    kts = list(range(max(0, qt - 4), qt + 1))
                    Ksrc, Kkey, Vsrc, Vkey = KwT, 'KwT', Vw, 'Vw'
                obk = [reserve_bank(c), reserve_bank(c)]
                rbk = [reserve_bank(c), reserve_bank(c)]
                def flush(pend, nl):
                    P, kt, n_ = pend
                    for hf in range(2):
                        hs = slice(hf * 512, (hf + 1) * 512)
                        pb, pk, _ = obk[hf]
                        kb.op('pe', lambda e, pb=pb, kt=kt, hs=hs, P=P, n_=n_, Vsrc=Vsrc, nl=nl: e.matmul(pb[:, :], Vsrc[:, kt, :], P[0][:, hs],
                                                                                               start=(n_ == 0), stop=(n_ == nl - 1)),
                              reads=[Vkey, P[1]], writes=[pk])
                        rb, rk, _ = rbk[hf]
                        kb.op('pe', lambda e, rb=rb, hs=hs, P=P, n_=n_, nl=nl: e.matmul(rb[:, :], c.ones_bf[:], P[0][:, hs],
                                                                                    start=(n_ == 0), stop=(n_ == nl - 1)),
                              reads=['ones_bf', P[1]], writes=[rk])
                pend = None
                for n_, kt in enumerate(kts):
                    ksl = slice(kt * 128, (kt + 1) * 128)
                    sb2 = []
                    for hf in range(2):
                        pb, pk = next_bank(c)
                        kb.op('pe', lambda e, pb=pb, hf=hf, ksl=ksl, Ksrc=Ksrc, br=br: e.matmul(
                            pb[:, :], Ksrc[:, ksl], qT[:, hf * 4:(hf + 1) * 4, :], start=True, stop=(br != 1)),
                            reads=[Kkey, 'qT'], writes=[pk])
                        if br == 1:
                            kb.op('pe', lambda e, pb=pb, ksl=ksl: e.matmul(pb[:, :], Efull[:, ksl], selT4, start=False, stop=True),
                                  reads=['Efull', 'selT4'], writes=[pk])
                        sb2.append((pb, pk))
                    pi_ = n_ % 2
                    P = (PT[pi_], ('PT', pi_))
                    exp_tile(sb2, P)
                    if kt == qt:
                        amask(P, 0, -1, 1)
                    if br == 2 and kt == qt - 4:
                        amask(P, -1, 1, -1)
                    if pend is not None:
                        flush(pend, len(kts))
                    pend = (P, kt, n_)
                flush(pend, len(kts))
                gb = gate_bc(hk, br)
                for hf in range(2):
                    hs = slice(hf * 512, (hf + 1) * 512)
                    pb, pk, oidx = obk[hf]
                    rb, rk, ridx = rbk[hf]
                    kb.op('act', lambda e, rb=rb, hs=hs: e.activation(Rt[:, hs], rb[:, :], AF.Ln), reads=[rk], writes=['Rt'])
                    kb.op('act', lambda e, hs=hs: e.activation(Rt[:, hs], Rt[:, hs], AF.Exp, scale=-1.0), reads=['Rt'], writes=['Rt'])
                    kb.op('dve', lambda e, hs=hs, hf=hf, gb=gb: e.tensor_tensor(Rt[:, hs], Rt[:, hs], gb[hf][0][:, :], ALU.mult),
                          reads=['Rt', gb[hf][1]], writes=['Rt'])
                    kb.op('dve', lambda e, pb=pb, hs=hs: e.tensor_tensor(Rt[:, hs], Rt[:, hs], pb[:, :], ALU.mult),
                          reads=['Rt', pk], writes=['Rt'])
                    if br in NSA_BR:
                        kb.op('dve', lambda e, hs=hs: e.tensor_tensor(acc[:, hs], acc[:, hs], Rt[:, hs], ALU.add),
                              reads=['Rt', 'acc'], writes=['acc'])
                    release_bank(c, oidx)
                    release_bank(c, ridx)
            kb.op('dve', lambda e: e.tensor_tensor(yo, acc[:, :].rearrange("p (h q) -> p h q", h=8), sgt, ALU.mult),
                  reads=['acc', 'sgt'], writes=['yo'])
            kb.dma('pool', Y[hk * 8:(hk + 1) * 8, :, qsl].rearrange("h d q -> d h q"), yo, reads=['yo'], writes=['nsaY'])

    c.areset()
    c.rows2()
    for tt in range(c.NT):
        tsl = slice(tt * TT, (tt + 1) * TT)
        load_norm_transpose(c, x_src, tt)
        kb.dma('pool', c.yT[:, 0:32, :], Y[:, :, tsl].rearrange("h d t -> d h t"), reads=['nsaY'], writes=[('yT', f) for f in range(32)])
        mem_branch(c, l, win)
        wout_phase(c, l, wo, x_src, dst, tt)


LAYER_FNS = {0: layer0, 1: layer1, 2: layer2, 3: layer3}


def _pc(v, nchunk):
    return np.ascontiguousarray(np.asarray(v, np.float32).reshape(nchunk, 128).T)


def _bc(v):
    v = np.asarray(v, np.float32).reshape(1, -1)
    return np.ascontiguousarray(np.broadcast_to(v, (128, v.shape[1])))


def make_in_map(inp, b, S, layers):
    m = {}
    m['x'] = np.ascontiguousarray(inp['x'][b, :S])
    m['mem'] = np.ascontiguousarray(inp['mem'][b])
    m['ident'] = np.eye(128, dtype=np.float32)
    m['mem_norm_pc'] = _pc(inp['mem_norm'], KD)
    for l in layers:
        p = 'l%d_' % l
        m[p + 'norm_pc'] = _pc(inp[p + 'norm'], KD)
        m[p + 'w_in'] = inp[p + 'w_in']
        m[p + 'w_out'] = inp[p + 'w_out']
        m[p + 'mem_wkv'] = inp[p + 'mem_wkv']
        m[p + 'mem_qnorm_pc'] = _pc(inp[p + 'mem_qnorm'], 2)
        m[p + 'mem_knorm_pc'] = _pc(inp[p + 'mem_knorm'], 2)
        if l == 0:
            m[p + 'pool_w'] = inp[p + 'pool_w']
            m[p + 'pool_scale_pc'] = _pc(inp[p + 'pool_scale'], 32)
            t = np.arange(TT)
            rc = np.stack([1.0 / np.minimum(t + 1, w) for w in POOL_W]).astype(np.float32)
            m['pool_rc0'] = np.ascontiguousarray(np.broadcast_to(rc[None], (128, 4, TT)))
        if l == 1:
            m.update(nsa_host_tables(S))
            m['pos_bc'] = np.ascontiguousarray(np.broadcast_to(inp['positions'][b, :S][None].astype(np.int32), (128, S)))
            m[p + 'qnorm_pc'] = _pc(inp[p + 'qnorm'], 1)
            m[p + 'knorm_pc'] = _pc(inp[p + 'knorm'], 1)
            m[p + 'cmp_posT'] = np.ascontiguousarray(inp[p + 'cmp_pos'].T)
            for nm in ('cmp_k_w1', 'cmp_k_w2', 'cmp_v_w1', 'cmp_v_w2'):
                m[p + nm] = inp[p + nm]
        if l == 2:
            m[p + 'conv_w_pc'] = np.ascontiguousarray(inp[p + 'conv_w'].reshape(4, 48, 128).transpose(2, 1, 0))
            m[p + 'conv_b_pc'] = _pc(inp[p + 'conv_b'], 48)
            m[p + 'gnorm_pc'] = _pc(inp[p + 'gnorm'], 32)
            m[p + 'dt_bias_bc'] = _bc(inp[p + 'dt_bias'])
            m[p + 'A_log_bc'] = _bc(inp[p + 'A_log'])
            m[p + 'D_bc'] = _bc(inp[p + 'D'])
            m['trilT'] = np.triu(np.ones((128, 128), np.float32))
            s8 = np.zeros((8, 8, 128), np.float32)
            for h in range(8):
                s8[h, h, :] = 1.0
            m['sel8'] = s8
        if l == 3:
            m[p + 'sgu_wT'] = np.ascontiguousarray(np.transpose(inp[p + 'sgu_w'], (2, 0, 1)))
            m[p + 'sgu_b_bc'] = np.ascontiguousarray(np.broadcast_to(inp[p + 'sgu_b'][None], (128, 8, 128)))
            m[p + 'v_norm_pc'] = _pc(inp[p + 'v_norm'], 32)
            m['trilT'] = np.triu(np.ones((128, 128), np.float32))
    return m


def nsa_host_tables(S):
    NQT, NSEL, NC = S // 128, S // 64, S // 16 - 1
    NCT = (NC + 127) // 128
    t = {}
    inv = (np.float32(10000.0) ** (-np.arange(0, 128, 2, dtype=np.float32) / np.float32(128))).astype(np.float32)
    t['rope_inv_pc'] = np.ascontiguousarray(np.concatenate([inv, inv])[:, None])
    pr = np.zeros((128, 128), np.float32)
    for d in range(64):
        pr[d + 64, d] = -1.0
        pr[d, d + 64] = 1.0
    t['rope_prot'] = pr
    cs = np.arange(NCT * 128)
    cstart, cend = cs * 16, cs * 16 + 31
    sel0 = np.arange(NSEL) * 64
    agg = ((cstart[:, None] < sel0[None, :] + 64) & (cend[:, None] >= sel0[None, :]) & (cs[:, None] < NC)).astype(np.float32)
    t['nsa_agg'] = np.ascontiguousarray(agg.reshape(NCT, 128, NSEL).transpose(1, 0, 2))
    t['nsa_E'] = (np.arange(S)[None, :] // 64 == np.arange(64)[:, None]).astype(np.float32)
    tok = np.arange(S)
    bt = tok // 64
    j = np.arange(NSEL)[None, :]
    causal = (j <= bt[:, None])
    forced = (j == 0) | (j == bt[:, None]) | (j == bt[:, None] - 1)
    bias = np.where(forced, 1000.0, np.where(causal, 0.0, -1.0)).astype(np.float32)
    tabs = np.stack([causal.astype(np.float32), bias], axis=1)
    t['nsa_tabs'] = np.ascontiguousarray(tabs.reshape(NQT, 128, 2, NSEL))
    return t


_PROG = {}


_INPUT_NAMES = (
    'x', 'mem', 'positions', 'mem_norm',
    'l0_norm', 'l0_w_in', 'l0_w_out', 'l0_mem_wkv', 'l0_mem_qnorm', 'l0_mem_knorm', 'l0_pool_w', 'l0_pool_scale',
    'l1_norm', 'l1_w_in', 'l1_w_out', 'l1_mem_wkv', 'l1_mem_qnorm', 'l1_mem_knorm', 'l1_qnorm', 'l1_knorm',
    'l1_cmp_pos', 'l1_cmp_k_w1', 'l1_cmp_k_w2', 'l1_cmp_v_w1', 'l1_cmp_v_w2',
    'l2_norm', 'l2_w_in', 'l2_w_out', 'l2_mem_wkv', 'l2_mem_qnorm', 'l2_mem_knorm', 'l2_conv_w', 'l2_conv_b',
    'l2_dt_bias', 'l2_A_log', 'l2_D', 'l2_gnorm',
    'l3_norm', 'l3_w_in', 'l3_w_out', 'l3_mem_wkv', 'l3_mem_qnorm', 'l3_mem_knorm', 'l3_v_norm', 'l3_sgu_w', 'l3_sgu_b',
)


def kernel(**inputs):
    missing = [n for n in _INPUT_NAMES if n not in inputs]
    assert not missing, missing
    S = inputs['x'].shape[1]
    B = inputs['x'].shape[0]
    layers = [0, 1, 2, 3]
    if S not in _PROG:
        _PROG[S] = build_program(S, layers)
    nc, _es = _PROG[S]
    inp = {k: np.asarray(v) for k, v in inputs.items()}
    in_maps = [make_in_map(inp, b, S, layers) for b in range(B)]
    res = run_bass_kernel_spmd(nc, in_maps, core_ids=list(range(B)))
    out = np.stack([np.asarray(r['out']) for r in res.results], axis=0)
    return out.astype(np.float32)
```

```python
import numpy as np
import ml_dtypes
from contextlib import ExitStack
import concourse.bass as bass
import concourse.mybir as mybir
from concourse.bass_utils import run_bass_kernel_spmd

F32 = mybir.dt.float32
BF16 = mybir.dt.bfloat16
AF = mybir.ActivationFunctionType
ALU = mybir.AluOpType
AX = mybir.AxisListType

D = 2048
KD = D // 128
WTOK = 4096
WMEM = 1024
MEMT = 256
EPS = 1e-6
TT = 512
NDSEM = 6
NSTG = 4
SEM_MAX = 8000

STRICT_SAME_ENGINE = True
DEBUG_OUT = set()
NSA_BR = (0, 1, 2)


class KB:
    def __init__(self, nc, es):
        self.nc = nc
        self.es = es
        self.streams = {e: [] for e in ('pe', 'act', 'dve', 'pool', 'sp')}
        self.esem = {e: es.enter_context(nc.semaphore('s_' + e)) for e in self.streams}
        self.ecount = {e: 0 for e in self.streams}
        self.dsems = {}
        for q in ('sp', 'pool', 'act'):
            self.dsems[q] = [[es.enter_context(nc.semaphore('d_%s%d' % (q, i))), 0] for i in range(NDSEM)]
        self.drr = {q: 0 for q in self.dsems}
        self.last_w = {}
        self.readers = {}
        self.seen = {e: {} for e in self.streams}
        self.tail_tokens = []
        self.nops = 0
        self.retired = []
        self.nsem = 0

    def _deps(self, eng, reads, writes):
        need = {}

        def add(tok):
            if tok is None:
                return
            sem, val, src = tok
            if (not STRICT_SAME_ENGINE or eng == 'pe') and src == eng:
                return
            k = id(sem)
            if self.seen[eng].get(k, 0) >= val:
                return
            if k not in need or need[k][1] < val:
                need[k] = (sem, val)

        for r in reads:
            add(self.last_w.get(r))
            if isinstance(r, tuple) and r[0] == 'pb':
                for t in self.readers.get(r, ()):
                    if t[2] != eng:
                        add(t)
        for w in writes:
            add(self.last_w.get(w))
            for t in self.readers.get(w, ()):
                add(t)
        waits = list(need.values())
        for sem, val in waits:
            self.seen[eng][id(sem)] = val
        return waits

    def _commit(self, tok, reads, writes):
        for r in reads:
            self.readers.setdefault(r, []).append(tok)
        for w in writes:
            self.last_w[w] = tok
            self.readers[w] = []

    def _rotate(self, eng):
        if self.ecount[eng] >= SEM_MAX:
            self.retired.append((self.esem[eng], self.ecount[eng], eng))
            self.nsem += 1
            self.esem[eng] = self.es.enter_context(self.nc.semaphore('s_%s_%d' % (eng, self.nsem)))
            self.ecount[eng] = 0

    def op(self, eng, fn, reads=(), writes=()):
        self._rotate(eng)
        waits = self._deps(eng, reads, writes)
        self.ecount[eng] += 1
        tok = (self.esem[eng], self.ecount[eng], eng)
        self.streams[eng].append((waits, fn, (self.esem[eng], 1)))
        self._commit(tok, reads, writes)
        self.nops += 1
        return tok

    def dma(self, q, out, in_, reads=(), writes=(), final=False):
        slot = self.dsems[q][self.drr[q]]
        self.drr[q] = (self.drr[q] + 1) % NDSEM
        if slot[1] >= SEM_MAX:
            self.retired.append((slot[0], slot[1], 'dma'))
            self.nsem += 1
            slot[0] = self.es.enter_context(self.nc.semaphore('d_%s_%d' % (q, self.nsem)))
            slot[1] = 0
        sem = slot[0]
        waits = self._deps(q, reads, writes)
        if slot[1] > 0 and self.seen[q].get(id(sem), 0) < slot[1]:
            waits.append((sem, slot[1]))
            self.seen[q][id(sem)] = slot[1]
        slot[1] += 16
        tok = (sem, slot[1], 'dma')
        self.streams[q].append((waits, lambda e: e.dma_start(out=out, in_=in_), (sem, 16)))
        self._commit(tok, reads, writes)
        if final:
            self.tail_tokens.append(tok)
        self.nops += 1
        return tok

    def barrier(self):
        toks = list(self.retired)
        for e2 in self.streams:
            if self.ecount[e2] > 0:
                toks.append((self.esem[e2], self.ecount[e2], e2))
        for q in self.dsems:
            for sem, val in self.dsems[q]:
                if val > 0:
                    toks.append((sem, val, 'dma'))
        for eng in self.streams:
            waits = []
            for sem, val, src in toks:
                if src == eng:
                    continue
                if self.seen[eng].get(id(sem), 0) < val:
                    waits.append((sem, val))
                    self.seen[eng][id(sem)] = val
            if waits:
                self._rotate(eng)
                self.ecount[eng] += 1
                self.streams[eng].append((waits, lambda e: e.nop(), (self.esem[eng], 1)))

    def emit(self):
        nc = self.nc
        fm = {}
        for t in self.tail_tokens:
            if id(t[0]) not in fm or fm[id(t[0])][1] < t[1]:
                fm[id(t[0])] = (t[0], t[1])
        fin = list(fm.values())
        engmap = {'pe': 'tensor', 'act': 'scalar', 'dve': 'vector', 'pool': 'gpsimd', 'sp': 'sync'}
        with nc.Block() as block:
            for ename, bname in engmap.items():
                stream = self.streams[ename]
                extra = fin if ename == 'sp' else []

                def body(e, stream=stream, extra=extra):
                    for waits, fn, inc in stream:
                        for sem, val in waits:
                            e.wait_ge(sem, val)
                        ins = fn(e)
                        ins.then_inc(inc[0], inc[1])
                    for sem, val in extra:
                        e.wait_ge(sem, val)
                getattr(block, bname)(body)


class Ctx:
    pass


def build_program(S, layers, first_layer_input='x'):
    nc = bass.Bass("TRN2", target_bir_lowering=False)
    es = ExitStack()
    kb = KB(nc, es)
    NT = S // TT
    c = Ctx()
    c.nc, c.kb, c.es, c.S, c.NT = nc, kb, es, S, NT

    _dins = {}

    def din(name, shape, dt=F32):
        if name not in _dins:
            _dins[name] = nc.dram_tensor(name, list(shape), dt, kind="ExternalInput").ap()
        return _dins[name]

    def dscr(name, shape, dt=F32):
        kind = "ExternalOutput" if name in DEBUG_OUT else "Internal"
        return nc.dram_tensor('scr_' + name, list(shape), dt, kind=kind).ap()

    c.din, c.dscr = din, dscr
    c._dinc = {}

    def din_cached(name, shape, dt=F32):
        if name not in c._dinc:
            c._dinc[name] = din(name, shape, dt)
        return c._dinc[name]
    c.din_cached = din_cached
    x_in = din('x', [S, D])
    mem_in = din('mem', [MEMT, D])
    ident_in = din('ident', [128, 128])
    out = nc.dram_tensor('out', [S, D], F32, kind="ExternalOutput").ap()
    c.mem_in = mem_in
    xbuf = [(dscr('xs0', [S, D]), 'xs0'), (dscr('xs1', [S, D]), 'xs1')]

    def sb(name, shape, dt=F32):
        return es.enter_context(nc.sbuf_tensor('sb_' + name, list(shape), dt))

    def ps(name, shape, dt=F32):
        return es.enter_context(nc.psum_tensor('ps_' + name, list(shape), dt))
    c.sb, c.ps = sb, ps

    c.ident_f = sb('ident_f', [128, 128], F32)
    c.ident = sb('ident', [128, 128], BF16)
    c.ones_bf = sb('ones_bf', [128, 128], BF16)
    c.xt = [sb('xt0', [128, D], F32)] * 2
    c.hn = [sb('hn0', [128, D], BF16)] * 2
    c.nrb = 1
    c.stat = [sb('stat0', [128, 8], F32)] * 2
    c.hT = sb('hT', [128, KD, TT], BF16)
    c.yT = sb('yT', [128, 40, TT], BF16)
    c.wt = [sb('wt%d' % i, [128, 16, 512], BF16) for i in range(2)]
    c.wrr = 0
    c.pbank = [ps('pb%d' % i, [128, 512], F32) for i in range(8)]
    c.prr = 0
    c.kT = sb('kT', [128, 4, 2, MEMT], BF16)
    c.Vm = sb('Vm', [128, 2, WMEM], BF16)
    c.mq = sb('mq', [128, 2, TT], BF16)
    c.mqs = sb('mqs', [128, 2, TT], BF16)
    c.mrs = sb('mrs', [128, TT], F32)
    c.msc = sb('msc', [128, TT], F32)
    c.mP = sb('mP', [128, 2, TT], BF16)
    c.mrd = sb('mrd', [128, TT], F32)
    c.msg = sb('msg', [128, TT], F32)
    c.mo = sb('mo', [128, TT], F32)
    c.xr = [sb('xr%d' % i, [128, 512], F32) for i in range(2)]
    c.xo = [sb('xo%d' % i, [128, 512], F32) for i in range(2)]
    c.xrr = 0
    c.gpc = sb('gpc', [128, 64], F32)
    ARENA = 74 * 1024
    c.arena = sb('arena', [128, ARENA // 4], F32)
    c.aoff = 0
    c.agen = 0

    def areset():
        kb.barrier()
        c.aoff = 0
        c.agen += 1
        c.nrb = 1
        c.xt[1], c.hn[1], c.stat[1] = c.xt[0], c.hn[0], c.stat[0]

    def rows2():
        c.xt[1] = c.aalloc('xt1', [128, D], F32)
        c.hn[1] = c.aalloc('hn1', [128, D], BF16)
        c.stat[1] = c.aalloc('stat1', [128, 8], F32)
        c.nrb = 2
    c.rows2 = rows2

    def aalloc(name, shape, dt=F32):
        esz = 2 if dt == BF16 else 4
        n = int(np.prod(shape[1:]))
        nbytes = (n * esz + 31) // 32 * 32
        assert c.aoff + nbytes <= ARENA, (name, c.aoff, nbytes)
        v = c.arena[0:shape[0], c.aoff // 4:(c.aoff + nbytes) // 4]
        c.aoff += nbytes
        if dt != F32:
            v = v.bitcast(dt)
        v = v[:, 0:n]
        if len(shape) == 3:
            v = v.rearrange("p (a b) -> p a b", a=shape[1])
        elif len(shape) == 4:
            v = v.rearrange("p (a b c) -> p a b c", a=shape[1], b=shape[2])
        return v
    c.areset, c.aalloc = areset, aalloc

    kb.dma('pool', c.ident_f[:], ident_in, writes=['ident_f'])
    kb.op('dve', lambda e: e.tensor_copy(c.ident[:], c.ident_f[:]), reads=['ident_f'], writes=['ident'])
    kb.op('dve', lambda e: e.memset(c.ones_bf[:], 1.0), writes=['ones_bf'])

    cur_in = (x_in, 'x')
    mem_prep_global(c)
    for li, l in enumerate(layers):
        dst = (out, 'out') if li == len(layers) - 1 else xbuf[li % 2]
        LAYER_FNS[l](c, l, cur_in, dst)
        cur_in = dst
    kb.emit()
    return nc, es


def next_bank(c):
    if not hasattr(c, 'rot'):
        c.rot = list(range(8))
    i = c.rot.pop(0)
    c.rot.append(i)
    return c.pbank[i], ('pb', i)


def reserve_bank(c):
    if not hasattr(c, 'rot'):
        c.rot = list(range(8))
    i = c.rot.pop(0)
    return c.pbank[i], ('pb', i), i


def release_bank(c, i):
    c.rot.append(i)


def norm_rows(c, i, src_ap, srckeys):
    kb = c.kb
    i = i % c.nrb
    xt, hn, st = c.xt[i], c.hn[i], c.stat[i]
    kb.dma('pool', xt[:], src_ap, reads=srckeys, writes=[('xt', i)])
    kb.op('act', lambda e: e.activation(hn[:], xt[:], AF.Square, accum_out=st[:, 0:1]),
          reads=[('xt', i)], writes=[('hn', i), ('st', i)])
    kb.op('dve', lambda e: e.tensor_scalar(st[:, 1:2], st[:, 0:1], 1.0 / D, EPS, ALU.mult, ALU.add),
          reads=[('st', i)], writes=[('st', i)])
    kb.op('act', lambda e: e.activation(st[:, 2:3], st[:, 1:2], AF.Sqrt),
          reads=[('st', i)], writes=[('st', i)])
    kb.op('dve', lambda e: e.reciprocal(st[:, 3:4], st[:, 2:3]),
          reads=[('st', i)], writes=[('st', i)])
    kb.op('dve', lambda e: e.tensor_scalar(hn[:], xt[:], st[:, 3:4], None, ALU.mult),
          reads=[('xt', i), ('st', i)], writes=[('hn', i)])


def transpose_rows(c, i, dst, dkey, s):
    kb = c.kb
    i = i % c.nrb
    hn = c.hn[i]
    for g in range(KD // 4):
        pb, pk = next_bank(c)
        pbv = pb[:].bitcast(BF16)
        for j in range(4):
            k = g * 4 + j
            kb.op('pe', lambda e, k=k, j=j, pbv=pbv: e.transpose(
                pbv[:, j * 128:(j + 1) * 128], hn[:, k * 128:(k + 1) * 128], c.ident[:]),
                reads=[('hn', i), 'ident'], writes=[pk])
        src = pbv[:, 0:512].rearrange("p (j t) -> p j t", j=4)
        o = dst[:, g * 4:(g + 1) * 4, s * 128:(s + 1) * 128]
        if g % 2 == 0:
            kb.op('act', lambda e, o=o, src=src: e.copy(o, src), reads=[pk], writes=[dkey])
        else:
            kb.op('dve', lambda e, o=o, src=src: e.tensor_copy(o, src), reads=[pk], writes=[dkey])


def load_norm_transpose(c, x_src, tt):
    for s in range(TT // 128):
        i = (tt * 4 + s) % 2
        r0 = tt * TT + s * 128
        norm_rows(c, i, x_src[0][r0:r0 + 128, :], [('xd', x_src[1], tt)])
        transpose_rows(c, i, c.hT, ('hT', s), s)


def load_pc(c, col0, name, n):
    c.kb.dma('pool', c.gpc[:, col0:col0 + n], c.din(name, [128, n]), writes=['gpc'])
    return c.gpc[:, col0:col0 + n]


def prep_w(c, name, w_ap, K, segs, rowscale=None):
    kb = c.kb
    KC = K // 128
    wv = w_ap.rearrange("(k p) n -> p k n", p=128)
    res = {}
    for (sname, col0, ncols) in segs:
        nb = (ncols + 511) // 512
        scr = c.dscr('w_%s_%s' % (name, sname), [nb, 128, KC, 512], BF16)
        res[sname] = (scr, ncols)
        for b in range(nb):
            n = min(512, ncols - b * 512)
            for k0 in range(0, KC, 2):
                kn = min(2, KC - k0)
                i = c.srr
                c.srr = (c.srr + 1) % NSTG
                stg, stb = c.stg[i], c.stb[i]
                kb.dma('sp', stg[:, 0:kn, 0:n], wv[:, k0:k0 + kn, col0 + b * 512: col0 + b * 512 + n],
                       writes=[('stg', i)])
                if rowscale is None:
                    if i % 2 == 0:
                        kb.op('dve', lambda e, stg=stg, stb=stb, kn=kn, n=n: e.tensor_copy(stb[:, 0:kn, 0:n], stg[:, 0:kn, 0:n]),
                              reads=[('stg', i)], writes=[('stb', i)])
                    else:
                        kb.op('act', lambda e, stg=stg, stb=stb, kn=kn, n=n: e.copy(stb[:, 0:kn, 0:n], stg[:, 0:kn, 0:n]),
                              reads=[('stg', i)], writes=[('stb', i)])
                else:
                    for kk in range(kn):
                        sc = rowscale[:, k0 + kk:k0 + kk + 1]
                        if kk == 0:
                            kb.op('dve', lambda e, stg=stg, stb=stb, kk=kk, n=n, sc=sc: e.tensor_scalar(
                                stb[:, kk, 0:n], stg[:, kk, 0:n], sc, None, ALU.mult),
                                reads=[('stg', i), 'gpc'], writes=[('stb', i)])
                        else:
                            kb.op('act', lambda e, stg=stg, stb=stb, kk=kk, n=n, sc=sc: e.activation(
                                stb[:, kk, 0:n], stg[:, kk, 0:n], AF.Copy, scale=sc),
                                reads=[('stg', i), 'gpc'], writes=[('stb', i)])
                kb.dma('pool', scr[b, :, k0:k0 + kn, 0:n], stb[:, 0:kn, 0:n],
                       reads=[('stb', i)], writes=[('wscr', name, sname, b)])
    return res


def prep_begin(c):
    c.areset()
    c.stg = [c.aalloc('stg%d' % i, [128, 2, 512], F32) for i in range(NSTG)]
    c.stb = [c.aalloc('stb%d' % i, [128, 2, 512], BF16) for i in range(NSTG)]
    c.srr = 0


def load_w_dep(c, src_ap, srckey):
    i = c.wrr
    c.wrr = (c.wrr + 1) % 2
    wt = c.wt[i]
    shp = src_ap.shape
    c.kb.dma('sp', wt[:, 0:shp[1], 0:shp[2]], src_ap, reads=[srckey], writes=[('wt', i)])
    return wt, ('wt', i)


def proj_fm(c, wseg, name, sname, kc=KD, rhs_of=None, ntok=TT, b0=0, nblk=None):
    kb = c.kb
    scr, ncols = wseg[sname]
    nb = (ncols + 511) // 512
    if nblk is None:
        nblk = nb - b0
    if rhs_of is None:
        rhs_of = lambda k: (c.hT[:, k, :], [('hT', s) for s in range(4)])
    for b in range(b0, b0 + nblk):
        n = min(512, ncols - b * 512)
        wt, wk = load_w_dep(c, scr[b, :, 0:kc, 0:n], ('wscr', name, sname, b))
        for sub in range((n + 127) // 128):
            m = min(128, n - sub * 128)
            pb, pk = next_bank(c)
            for k in range(kc):
                rhs, rk = rhs_of(k)
                kb.op('pe', lambda e, pb=pb, wt=wt, k=k, sub=sub, m=m, rhs=rhs: e.matmul(
                    pb[0:m, 0:ntok], wt[:, k, sub * 128: sub * 128 + m], rhs, start=(k == 0), stop=(k == kc - 1)),
                    reads=[wk] + rk, writes=[pk])
            yield (b - b0) * 4 + sub, m, pb, pk


def wout_phase(c, l, wo, x_src, dst, tt, nxt=None):
    kb = c.kb
    scr, _ = wo['all']
    for cb in range(4):
        if nxt is not None:
            r0n = (tt + 1) * TT + cb * 128
            norm_rows(c, ((tt + 1) * 4 + cb) % 2, nxt[0][r0n:r0n + 128, :], [('xd', nxt[1], tt + 1)])
        banks = [next_bank(c) for _ in range(4)]
        for kg, (k0, kn) in enumerate(((0, 16), (16, 16), (32, 8))):
            wt, wk = load_w_dep(c, scr[cb, :, k0:k0 + kn, :], ('wscr', 'wo%d' % l, 'all', cb))
            for s in range(4):
                pb, pk = banks[s]
                for kk in range(kn):
                    f = k0 + kk
                    kb.op('pe', lambda e, pb=pb, wt=wt, kk=kk, f=f, s=s: e.matmul(
                        pb[:, :], c.yT[:, f, s * 128:(s + 1) * 128], wt[:, kk, :], start=(f == 0), stop=(f == 39)),
                        reads=[wk, ('yT', f)], writes=[pk])
        for s in range(4):
            pb, pk = banks[s]
            i = c.xrr
            c.xrr = (c.xrr + 1) % 2
            xr, xo = c.xr[i], c.xo[i]
            r0 = tt * TT + s * 128
            kb.dma('pool', xr[:], x_src[0][r0:r0 + 128, cb * 512:(cb + 1) * 512], reads=[('xd', x_src[1], tt)],
                   writes=[('xr', i)])
            kb.op('dve', lambda e, xo=xo, xr=xr, pb=pb: e.tensor_tensor(xo[:], pb[:, :], xr[:], ALU.add),
                  reads=[pk, ('xr', i)], writes=[('xo', i)])
            kb.dma('pool', dst[0][r0:r0 + 128, cb * 512:(cb + 1) * 512], xo[:], reads=[('xo', i)],
                   writes=[('xd', dst[1], tt)], final=True)
        if nxt is not None:
            transpose_rows(c, ((tt + 1) * 4 + cb) % 2, c.hT, ('hT', cb), cb)


def mem_prep_global(c):
    kb = c.kb
    c.areset()
    memnT = c.aalloc('memnT', [128, KD, MEMT], BF16)
    c.memnT_scr = c.dscr('memnT', [128, KD, MEMT], BF16)
    for s in range(2):
        norm_rows(c, s, c.mem_in[s * 128:(s + 1) * 128, :], [])
        transpose_rows(c, s, memnT, 'memnT', s)
    kb.dma('pool', c.memnT_scr, memnT, reads=['memnT'], writes=['memnT_scr'])


def mem_prep_layer(c, l):
    kb = c.kb
    gm = load_pc(c, 32, 'mem_norm_pc', KD)
    wkv = prep_w(c, 'wkv%d' % l, c.din('l%d_mem_wkv' % l, [D, 2 * WMEM]), D, [('k', 0, WMEM), ('v', WMEM, WMEM)],
                 rowscale=gm)
    memnT = c.aalloc('memnT', [128, KD, MEMT], BF16)
    kraw = c.aalloc('kraw', [128, 2, MEMT], F32)
    ksq = c.aalloc('ksq', [128, 2, MEMT], BF16)
    krs = c.aalloc('krs', [128, MEMT], F32)
    kb.dma('pool', memnT, c.memnT_scr, reads=['memnT_scr'], writes=['memnT'])
    qkg = c.gpc[:, 48:54]
    load_pc(c, 48, 'l%d_mem_qnorm_pc' % l, 2)
    load_pc(c, 50, 'l%d_mem_knorm_pc' % l, 2)
    kb.op('dve', lambda e: e.tensor_tensor(qkg[:, 4:6], qkg[:, 0:2], qkg[:, 2:4], ALU.mult),
          reads=['gpc'], writes=['gpc2'])
    memrhs = lambda k: (memnT[:, k, :], ['memnT'])
    for ci, m, pb, pk in proj_fm(c, wkv, 'wkv%d' % l, 'k', rhs_of=memrhs, ntok=MEMT):
        hd, dc = ci // 2, ci % 2
        kb.op('act', lambda e, pb=pb, dc=dc: e.copy(kraw[:, dc, :], pb[:, 0:MEMT]),
              reads=[pk], writes=[('kraw', dc)])
        kb.op('act', lambda e, pb=pb, dc=dc: e.activation(ksq[:, dc, :], pb[:, 0:MEMT], AF.Square),
              reads=[pk], writes=[('ksq', dc)])
        if dc == 1:
            pb2, pk2 = next_bank(c)
            for d2 in range(2):
                kb.op('pe', lambda e, pb2=pb2, d2=d2: e.matmul(pb2[:, 0:MEMT], c.ones_bf[:], ksq[:, d2, :],
                                                                start=(d2 == 0), stop=(d2 == 1)),
                      reads=['ones_bf', ('ksq', d2)], writes=[pk2])
            kb.op('dve', lambda e, pb2=pb2: e.tensor_scalar(krs, pb2[:, 0:MEMT], 1.0 / 256, EPS, ALU.mult, ALU.add),
                  reads=[pk2], writes=['krs'])
            kb.op('act', lambda e: e.activation(krs, krs, AF.Sqrt), reads=['krs'], writes=['krs'])
            kb.op('dve', lambda e: e.reciprocal(krs, krs), reads=['krs'], writes=['krs'])
            for d2 in range(2):
                kb.op('dve', lambda e, hd=hd, d2=d2: e.scalar_tensor_tensor(
                    c.kT[:, hd, d2, :], kraw[:, d2, :], qkg[:, 4 + d2:5 + d2], krs, ALU.mult, ALU.mult),
                    reads=[('kraw', d2), 'gpc2', 'krs'], writes=['kT'])
    scr, _ = wkv['v']
    for b in range(2):
        wt, wk = load_w_dep(c, scr[b, :, :, :], ('wscr', 'wkv%d' % l, 'v', b))
        for mt in range(2):
            pb, pk = next_bank(c)
            for k in range(KD):
                kb.op('pe', lambda e, pb=pb, wt=wt, k=k, mt=mt: e.matmul(
                    pb[:, :], memnT[:, k, mt * 128:(mt + 1) * 128], wt[:, k, :], start=(k == 0), stop=(k == KD - 1)),
                    reads=[wk, 'memnT'], writes=[pk])
            kb.op('act', lambda e, pb=pb, mt=mt, b=b: e.copy(c.Vm[:, mt, b * 512:(b + 1) * 512], pb[:, :]),
                  reads=[pk], writes=['Vm'])


def mem_branch(c, l, win):
    kb = c.kb
    gate_it = proj_fm(c, win, 'win%d' % l, 'memg')
    q_it = proj_fm(c, win, 'win%d' % l, 'memq')
    G = [(c.msg, 'msg'), (c.mo, 'mo')]

    def gate_chunk(i):
        gci, gm_, gpb, gpk = next(gate_it)
        kb.op('act', lambda e, gpb=gpb, i=i: e.activation(G[i][0][:], gpb[:, :], AF.Silu), reads=[gpk], writes=[G[i][1]])

    for hd in range(4):
        for dc in range(2):
            ci, m, pb, pk = next(q_it)
            kb.op('act', lambda e, pb=pb, dc=dc: e.copy(c.mq[:, dc, :], pb[:, :]), reads=[pk], writes=[('mq', dc)])
            kb.op('act', lambda e, pb=pb, dc=dc: e.activation(c.mqs[:, dc, :], pb[:, :], AF.Square),
                  reads=[pk], writes=[('mqs', dc)])
        gate_chunk(0)
        pb2, pk2 = next_bank(c)
        for d2 in range(2):
            kb.op('pe', lambda e, pb2=pb2, d2=d2: e.matmul(pb2[:, :], c.ones_bf[:], c.mqs[:, d2, :],
                                                            start=(d2 == 0), stop=(d2 == 1)),
                  reads=['ones_bf', ('mqs', d2)], writes=[pk2])
        kb.op('act', lambda e, pb2=pb2: e.activation(c.mrs[:], pb2[:, :], AF.Ln, scale=1.0 / 256, bias=EPS),
              reads=[pk2], writes=['mrs'])
        kb.op('act', lambda e: e.activation(c.mrs[:], c.mrs[:], AF.Exp, scale=-0.5), reads=['mrs'], writes=['mrs'])
        gate_chunk(1)
        for mt in range(2):
            pb3, pk3 = next_bank(c)
            for d2 in range(2):
                kb.op('pe', lambda e, pb3=pb3, d2=d2, mt=mt, hd=hd: e.matmul(
                    pb3[:, :], c.kT[:, hd, d2, mt * 128:(mt + 1) * 128], c.mq[:, d2, :], start=(d2 == 0), stop=(d2 == 1)),
                    reads=['kT', ('mq', d2)], writes=[pk3])
            kb.op('dve', lambda e, pb3=pb3: e.tensor_tensor(c.msc[:], pb3[:, :], c.mrs[:], ALU.mult),
                  reads=[pk3, 'mrs'], writes=['msc'])
            kb.op('act', lambda e, mt=mt: e.activation(c.mP[:, mt, :], c.msc[:], AF.Exp, scale=1.0 / 16),
                  reads=['msc'], writes=[('mP', mt)])
        pb4, pk4 = next_bank(c)
        for mt in range(2):
            kb.op('pe', lambda e, pb4=pb4, mt=mt: e.matmul(pb4[:, :], c.ones_bf[:], c.mP[:, mt, :],
                                                            start=(mt == 0), stop=(mt == 1)),
                  reads=['ones_bf', ('mP', mt)], writes=[pk4])
        pvb = []
        for d2 in range(2):
            pb5, pk5 = next_bank(c)
            for mt in range(2):
                kb.op('pe', lambda e, pb5=pb5, mt=mt, d2=d2, hd=hd: e.matmul(
                    pb5[:, :], c.Vm[:, mt, hd * 256 + d2 * 128: hd * 256 + (d2 + 1) * 128], c.mP[:, mt, :],
                    start=(mt == 0), stop=(mt == 1)),
                    reads=['Vm', ('mP', mt)], writes=[pk5])
            pvb.append((pb5, pk5))
        kb.op('act', lambda e, pb4=pb4: e.activation(c.mrd[:], pb4[:, :], AF.Ln), reads=[pk4], writes=['mrd'])
        kb.op('act', lambda e: e.activation(c.mrd[:], c.mrd[:], AF.Exp, scale=-1.0), reads=['mrd'], writes=['mrd'])
        for d2 in range(2):
            pb5, pk5 = pvb[d2]
            f = hd * 2 + d2
            kb.op('dve', lambda e, pb5=pb5: e.tensor_tensor(c.msc[:], pb5[:, :], c.mrd[:], ALU.mult),
                  reads=[pk5, 'mrd'], writes=['msc'])
            kb.op('dve', lambda e, f=f, d2=d2: e.tensor_tensor(c.yT[:, 32 + f, :], c.msc[:], G[d2][0][:], ALU.mult),
                  reads=['msc', G[d2][1]], writes=[('yT', 32 + f)])


POOL_W = (2, 4, 8, 16)


def layer0(c, l, x_src, dst):
    kb = c.kb
    prep_begin(c)
    gain = load_pc(c, 0, 'l%d_norm_pc' % l, KD)
    win = prep_w(c, 'win%d' % l, c.din('l%d_w_in' % l, [D, 10240]), D,
                 [('v', 0, 4096), ('gate', 4096, 4096), ('memq', 8192, 1024), ('memg', 9216, 1024)], rowscale=gain)
    wo = prep_w(c, 'wo%d' % l, c.din('l%d_w_out' % l, [5120, D]), 5120, [('all', 0, D)])
    pw_in = c.din('l%d_pool_w' % l, [4, 1024, 1024])
    pws = [prep_w(c, 'pw%d_%d' % (l, g), pw_in[g], 1024, [('all', 0, 1024)]) for g in range(4)]
    mem_prep_layer(c, l)
    c.areset()
    c.rows2()
    psc = c.aalloc('pool_sc', [128, 32], F32)
    kb.dma('pool', psc, c.din('l%d_pool_scale_pc' % l, [128, 32]), writes=['psc'])
    rc0 = c.aalloc('pool_rc0', [128, 4, TT], F32)
    kb.dma('pool', rc0, c.din('pool_rc0', [128, 4, TT]), writes=['rc0'])
    carry = c.aalloc('pool_carry', [128, 32, 16], F32)
    kb.op('dve', lambda e: e.memset(carry, 0.0), writes=['carry'])
    va = c.aalloc('pool_va', [128, 16 + TT], F32)
    vb = c.aalloc('pool_vb', [128, 16 + TT], F32)
    vc = c.aalloc('pool_vc', [128, 16 + TT], F32)
    mixT = c.aalloc('pool_mixT', [128, 8, TT], BF16)
    sg = c.aalloc('pool_sg', [128, TT], F32)
    for tt in range(c.NT):
        if tt == 0:
            load_norm_transpose(c, x_src, tt)
        for g in range(4):
            w = POOL_W[g]
            for ci, m, pb, pk in proj_fm(c, win, 'win%d' % l, 'v', b0=2 * g, nblk=2):
                ch = g * 8 + ci
                kb.op('act', lambda e, pb=pb: e.copy(va[:, 16:16 + TT], pb[:, :]), reads=[pk], writes=['va'])
                kb.op('pool', lambda e, ch=ch: e.tensor_copy(va[:, 0:16], carry[:, ch, :]),
                      reads=['carry'], writes=['va'])
                kb.op('pool', lambda e, ch=ch: e.tensor_copy(carry[:, ch, :], va[:, TT:TT + 16]),
                      reads=['va'], writes=['carry'])
                src, srck = va, 'va'
                bufs = [(vb, 'vb'), (vc, 'vc')]
                sh = 1
                bi = 0
                while sh < w:
                    dstb, dk = bufs[bi]
                    bi ^= 1
                    kb.op('dve', lambda e, src=src, dstb=dstb, sh=sh: e.tensor_tensor(
                        dstb[:, sh:16 + TT], src[:, sh:16 + TT], src[:, 0:16 + TT - sh], ALU.add),
                        reads=[srck], writes=[dk])
                    src, srck = dstb, dk
                    sh *= 2
                if tt == 0:
                    dstb, dk = bufs[bi]
                    kb.op('dve', lambda e, src=src, dstb=dstb, g=g: e.tensor_tensor(
                        dstb[:, 16:16 + TT], src[:, 16:16 + TT], rc0[:, g, :], ALU.mult),
                        reads=[srck, 'rc0'], writes=[dk])
                    kb.op('dve', lambda e, dstb=dstb, ci=ci: e.tensor_tensor(
                        mixT[:, ci, :], dstb[:, 16:16 + TT], va[:, 16:16 + TT], ALU.subtract),
                        reads=[dk, 'va'], writes=[('mixT', ci)])
                else:
                    kb.op('dve', lambda e, src=src, ci=ci, w=w: e.scalar_tensor_tensor(
                        mixT[:, ci, :], src[:, 16:16 + TT], 1.0 / w, va[:, 16:16 + TT], ALU.mult, ALU.subtract),
                        reads=[srck, 'va'], writes=[('mixT', ci)])
            gate_it = proj_fm(c, win, 'win%d' % l, 'gate', b0=2 * g, nblk=2)
            mixrhs = lambda k: (mixT[:, k, :], [('mixT', k)])
            for ci, m, pb, pk in proj_fm(c, pws[g], 'pw%d_%d' % (l, g), 'all', kc=8, rhs_of=mixrhs):
                gci, gm, gpb, gpk = next(gate_it)
                ch = g * 8 + ci
                kb.op('act', lambda e, gpb=gpb: e.activation(sg, gpb[:, :], AF.Silu), reads=[gpk], writes=['sg'])
                kb.op('dve', lambda e, pb=pb, ch=ch: e.scalar_tensor_tensor(
                    c.yT[:, ch, :], pb[:, :], psc[:, ch:ch + 1], sg, ALU.mult, ALU.mult),
                    reads=[pk, 'psc', 'sg'], writes=[('yT', ch)])
        mem_branch(c, l, win)
        wout_phase(c, l, wo, x_src, dst, tt, nxt=(x_src if tt + 1 < c.NT else None))


def layer3(c, l, x_src, dst):
    kb = c.kb
    prep_begin(c)
    gain = load_pc(c, 0, 'l%d_norm_pc' % l, KD)
    win = prep_w(c, 'win%d' % l, c.din('l%d_w_in' % l, [D, 14336]), D,
                 [('u', 0, 4096), ('v', 4096, 4096), ('gate', 8192, 4096), ('memq', 12288, 1024),
                  ('memg', 13312, 1024)], rowscale=gain)
    wo = prep_w(c, 'wo%d' % l, c.din('l%d_w_out' % l, [5120, D]), 5120, [('all', 0, D)])
    mem_prep_layer(c, l)
    c.areset()
    c.rows2()
    vg = c.aalloc('sgu_vg', [128, 4, 4096], BF16)
    WTf = c.aalloc('sgu_WTf', [128, 8, 128], F32)
    WT = c.aalloc('sgu_WT', [128, 8, 128], BF16)
    tril = c.aalloc('sgu_tril', [128, 128], F32)
    bbc = c.aalloc('sgu_bbc', [128, 8, 128], F32)
    vng = c.aalloc('sgu_vng', [128, 32], F32)
    ssq = c.aalloc('sgu_ssq', [128, 4, 8], F32)
    rst = c.aalloc('sgu_rst', [128, 8], F32)
    t1 = c.aalloc('sgu_t1', [128, TT], F32)
    gu = c.aalloc('sgu_gu', [128, TT], F32)
    sg = c.aalloc('sgu_sg', [128, TT], F32)
    kb.dma('pool', WTf, c.din('l%d_sgu_wT' % l, [128, 8, 128]), writes=['WTf'])
    kb.dma('pool', tril, c.din('trilT', [128, 128]), writes=['tril'])
    kb.dma('pool', bbc, c.din('l%d_sgu_b_bc' % l, [128, 8, 128]), writes=['bbc'])
    kb.dma('pool', vng, c.din('l%d_v_norm_pc' % l, [128, 32]), writes=['vng'])
    for g in range(8):
        kb.op('dve', lambda e, g=g: e.tensor_tensor(WT[:, g, :], WTf[:, g, :], tril, ALU.mult),
              reads=['WTf', 'tril'], writes=['WT'])
    scr_v, _ = win['v']
    for tt in range(c.NT):
        if tt == 0:
            load_norm_transpose(c, x_src, tt)
        for b in range(8):
            wt, wk = load_w_dep(c, scr_v[b, :, :, :], ('wscr', 'win%d' % l, 'v', b))
            for s in range(4):
                pb, pk = next_bank(c)
                for k in range(KD):
                    kb.op('pe', lambda e, pb=pb, wt=wt, k=k, s=s: e.matmul(
                        pb[:, :], c.hT[:, k, s * 128:(s + 1) * 128], wt[:, k, :], start=(k == 0), stop=(k == KD - 1)),
                        reads=[wk, ('hT', s)], writes=[pk])
                kb.op('act', lambda e, pb=pb, s=s, b=b: e.activation(vg[:, s, b * 512:(b + 1) * 512], pb[:, :], AF.Gelu),
                      reads=[pk], writes=[('vg', s)])
                kb.op('act', lambda e, s=s, b=b: e.activation(t1, vg[:, s, b * 512:(b + 1) * 512], AF.Square,
                                                              accum_out=ssq[:, s, b:b + 1]),
                      reads=[('vg', s)], writes=['t1', ('ssq', s)])
        for s in range(4):
            kb.op('dve', lambda e, s=s: e.tensor_reduce(rst[:, s:s + 1], ssq[:, s, :], AX.X, ALU.add),
                  reads=[('ssq', s)], writes=[('rst', s)])
            kb.op('dve', lambda e, s=s: e.tensor_scalar(rst[:, s:s + 1], rst[:, s:s + 1], 1.0 / 4096, EPS, ALU.mult, ALU.add),
                  reads=[('rst', s)], writes=[('rst', s)])
            kb.op('act', lambda e, s=s: e.activation(rst[:, s:s + 1], rst[:, s:s + 1], AF.Sqrt),
                  reads=[('rst', s)], writes=[('rst', s)])
            kb.op('dve', lambda e, s=s: e.reciprocal(rst[:, s:s + 1], rst[:, s:s + 1]),
                  reads=[('rst', s)], writes=[('rst', s)])
            kb.op('dve', lambda e, s=s: e.tensor_scalar(vg[:, s, :], vg[:, s, :], rst[:, s:s + 1], None, ALU.mult),
                  reads=[('rst', s), ('vg', s)], writes=[('vg', s)])
        u_it = proj_fm(c, win, 'win%d' % l, 'u')
        g_it = proj_fm(c, win, 'win%d' % l, 'gate')
        for fc in range(32):
            g = fc // 4
            _, _, upb, upk = next(u_it)
            kb.op('act', lambda e, upb=upb: e.activation(gu, upb[:, :], AF.Gelu), reads=[upk], writes=['gu'])
            _, _, gpb, gpk = next(g_it)
            kb.op('act', lambda e, gpb=gpb: e.activation(sg, gpb[:, :], AF.Silu), reads=[gpk], writes=['sg'])
            pb, pk = next_bank(c)
            for s in range(4):
                kb.op('pe', lambda e, pb=pb, s=s, fc=fc, g=g: e.matmul(
                    pb[:, s * 128:(s + 1) * 128], vg[:, s, fc * 128:(fc + 1) * 128], WT[:, g, :], start=True, stop=True),
                    reads=[('vg', s), 'WT'], writes=[pk])
            for s in range(4):
                kb.op('dve', lambda e, pb=pb, s=s, fc=fc, g=g: e.scalar_tensor_tensor(
                    t1[:, s * 128:(s + 1) * 128], pb[:, s * 128:(s + 1) * 128], vng[:, fc:fc + 1], bbc[:, g, :],
                    ALU.mult, ALU.add), reads=[pk, 'vng', 'bbc'], writes=['t1'])
            kb.op('pool', lambda e: e.tensor_tensor(gu, gu, sg, ALU.mult), reads=['gu', 'sg'], writes=['gu'])
            kb.op('dve', lambda e, fc=fc: e.tensor_tensor(c.yT[:, fc, :], t1, gu, ALU.mult),
                  reads=['t1', 'gu'], writes=[('yT', fc)])
        mem_branch(c, l, win)
        wout_phase(c, l, wo, x_src, dst, tt, nxt=(x_src if tt + 1 < c.NT else None))


def layer2(c, l, x_src, dst):
    kb = c.kb
    HT = 256
    prep_begin(c)
    gain = load_pc(c, 0, 'l%d_norm_pc' % l, KD)
    win = prep_w(c, 'win%d' % l, c.din('l%d_w_in' % l, [D, 12352]), D,
                 [('z', 0, 4096), ('xbc', 4096, 6144), ('dt', 10240, 64), ('memq', 10304, 1024),
                  ('memg', 11328, 1024)], rowscale=gain)
    wo = prep_w(c, 'wo%d' % l, c.din('l%d_w_out' % l, [5120, D]), 5120, [('all', 0, D)])
    mem_prep_layer(c, l)
    c.areset()
    A = c.aalloc
    xtok = A('xtok', [128, 2, 4096], BF16)
    BT = A('BT', [128, 8, HT], BF16)
    CT = A('CT', [128, 8, HT], BF16)
    Btok = A('Btok', [128, 2, 1024], BF16)
    state = A('state', [128, 4096], F32)
    stbf = A('stbf', [128, 512], BF16)
    xw = A('xw', [128, 512], BF16)
    cvb = [A('cv%d' % i, [128, 3 + HT], F32) for i in range(2)]
    acc = [A('acc%d' % i, [128, HT], F32) for i in range(2)]
    xss = [A('xs%d' % i, [128, HT], BF16) for i in range(3)]
    carry = A('carry', [128, 48, 3], F32)
    cw = A('cw', [128, 48, 4], F32)
    cb = A('cb', [128, 48], F32)
    gn = A('gn', [128, 32], F32)
    wdt = A('wdt', [128, KD, 64], BF16)
    U = A('U', [128, 128], F32)
    onesf = A('onesf', [128, 128], F32)
    tab = A('tab', [128, 4, 64], F32)
    dtt = A('dtt', [128, 2, 64], F32)
    dtA = A('dtA', [128, 2, 64], F32)
    atok = A('atok', [128, 2, 64], F32)
    sel8 = A('sel8', [8, 8, 128], F32)
    aTg = A('aTg', [8, 128], F32)
    eAe = A('eAe', [128, 2, 64], F32)
    wtok = A('wtok', [128, 2, 64], F32)
    Gm = A('Gm', [128, 128], F32)
    szg = A('szg', [128, 4, HT], BF16)
    ty = A('ty', [128, 128], F32)
    ysq = A('ysq', [128, 128], BF16)
    rbc = A('rbc', [128, HT], F32)
    kb.dma('pool', cw, c.din('l%d_conv_w_pc' % l, [128, 48, 4]), writes=['cw'])
    kb.dma('pool', cb, c.din('l%d_conv_b_pc' % l, [128, 48]), writes=['cb'])
    kb.dma('pool', gn, c.din('l%d_gnorm_pc' % l, [128, 32]), writes=['gn'])
    kb.dma('pool', U, c.din('trilT', [128, 128]), writes=['U'])
    kb.dma('pool', sel8, c.din('sel8', [8, 8, 128]), writes=['sel8'])
    kb.dma('pool', tab[:, 0, :], c.din('l%d_dt_bias_bc' % l, [128, 64]), writes=['tab'])
    kb.dma('pool', tab[:, 1, :], c.din('l%d_A_log_bc' % l, [128, 64]), writes=['tab'])
    kb.dma('pool', tab[:, 2, :], c.din('l%d_D_bc' % l, [128, 64]), writes=['tab'])
    kb.op('act', lambda e: e.activation(tab[:, 3, :], tab[:, 1, :], AF.Exp), reads=['tab'], writes=['tab3'])
    kb.op('dve', lambda e: e.tensor_scalar(tab[:, 1, :], tab[:, 3, :], -1.0, None, ALU.mult), reads=['tab3', 'tab'], writes=['tab'])
    kb.op('dve', lambda e: e.memset(onesf, 1.0), writes=['onesf'])
    kb.op('dve', lambda e: e.memset(state, 0.0), writes=['state'])
    kb.op('dve', lambda e: e.memset(carry, 0.0), writes=['carry'])
    kb.dma('sp', wdt, win['dt'][0][0, :, :, 0:64], reads=[('wscr', 'win%d' % l, 'dt', 0)], writes=['wdt'])
    hkeys = [('hT', s) for s in range(4)]
    for tt in range(c.NT):
        if tt == 0:
            load_norm_transpose(c, x_src, tt)
        for half in range(2):
            t0 = half * HT
            rhs_half = lambda k, t0=t0: (c.hT[:, k, t0:t0 + HT], hkeys)
            pend_tr = []
            for ci, m, pb, pk in proj_fm(c, win, 'win%d' % l, 'xbc', rhs_of=rhs_half, ntok=HT):
                i = ci % 2
                cv, ac = cvb[i], acc[i]
                kb.op('pool', lambda e, cv=cv, ci=ci: e.tensor_copy(cv[:, 0:3], carry[:, ci, :]),
                      reads=['carry'], writes=[('cv', i)])
                kb.op('act', lambda e, cv=cv, pb=pb: e.copy(cv[:, 3:3 + HT], pb[:, 0:HT]), reads=[pk], writes=[('cv', i)])
                kb.op('pool', lambda e, cv=cv, ci=ci: e.tensor_copy(carry[:, ci, :], cv[:, HT:HT + 3]),
                      reads=[('cv', i)], writes=['carry'])
                kb.op('dve', lambda e, cv=cv, ac=ac, ci=ci: e.tensor_scalar(
                    ac, cv[:, 3:3 + HT], cw[:, ci, 3:4], cb[:, ci:ci + 1], ALU.mult, ALU.add),
                    reads=[('cv', i), 'cw', 'cb'], writes=[('acc', i)])
                for j in range(3):
                    kb.op('dve', lambda e, cv=cv, ac=ac, ci=ci, j=j: e.scalar_tensor_tensor(
                        ac, cv[:, j:j + HT], cw[:, ci, j:j + 1], ac, ALU.mult, ALU.add),
                        reads=[('cv', i), 'cw', ('acc', i)], writes=[('acc', i)])
                if ci < 32:
                    o, ok = xss[ci % 3], ('xs', ci % 3)
                elif ci < 40:
                    o, ok = BT[:, ci - 32, :], 'BT'
                else:
                    o, ok = CT[:, ci - 40, :], 'CT'
                kb.op('act', lambda e, o=o, ac=ac: e.activation(o, ac, AF.Silu), reads=[('acc', i)], writes=[ok])
                if ci < 40:
                    pend_tr.append((ci, o, ok))
                while pend_tr and (pend_tr[0][0] <= ci - 2 or ci == 47):
                    ci_t, o, ok = pend_tr.pop(0)
                    tb, tk = next_bank(c)
                    tbv = tb[:].bitcast(BF16)
                    for s2 in range(2):
                        kb.op('pe', lambda e, o=o, s2=s2, tbv=tbv: e.transpose(
                            tbv[:, s2 * 128:(s2 + 1) * 128], o[:, s2 * 128:(s2 + 1) * 128], c.ident[:]),
                            reads=[ok, 'ident'], writes=[tk])
                    srcv = tbv[:, 0:256].rearrange("p (s f) -> p s f", s=2)
                    if ci_t < 32:
                        kb.op('dve', lambda e, srcv=srcv, ci_t=ci_t: e.tensor_copy(xtok[:, :, ci_t * 128:(ci_t + 1) * 128], srcv),
                              reads=[tk], writes=['xtok'])
                    else:
                        g = ci_t - 32
                        kb.op('dve', lambda e, srcv=srcv, g=g: e.tensor_copy(Btok[:, :, g * 128:(g + 1) * 128], srcv),
                              reads=[tk], writes=['Btok'])
            for s2 in range(2):
                pb, pk = next_bank(c)
                for k in range(KD):
                    kb.op('pe', lambda e, pb=pb, k=k, s2=s2, t0=t0: e.matmul(
                        pb[:, 0:64], c.hT[:, k, t0 + s2 * 128:t0 + (s2 + 1) * 128], wdt[:, k, :],
                        start=(k == 0), stop=(k == KD - 1)), reads=['wdt'] + hkeys, writes=[pk])
                kb.op('dve', lambda e, pb=pb, s2=s2: e.tensor_tensor(dtt[:, s2, :], pb[:, 0:64], tab[:, 0, :], ALU.add),
                      reads=[pk, 'tab'], writes=[('dtt', s2)])
                kb.op('act', lambda e, s2=s2: e.activation(dtt[:, s2, :], dtt[:, s2, :], AF.Exp),
                      reads=[('dtt', s2)], writes=[('dtt', s2)])
                kb.op('act', lambda e, s2=s2: e.activation(dtt[:, s2, :], dtt[:, s2, :], AF.Ln, bias=1.0),
                      reads=[('dtt', s2)], writes=[('dtt', s2)])
                kb.op('dve', lambda e, s2=s2: e.tensor_tensor(dtA[:, s2, :], dtt[:, s2, :], tab[:, 1, :], ALU.mult),
                      reads=[('dtt', s2), 'tab'], writes=[('dtA', s2)])
                pb1, pk1 = next_bank(c)
                kb.op('pe', lambda e, pb1=pb1, s2=s2: e.matmul(pb1[:, 0:64], U, dtA[:, s2, :], start=True, stop=True),
                      reads=['U', ('dtA', s2)], writes=[pk1])
                kb.op('act', lambda e, pb1=pb1, s2=s2: e.copy(atok[:, s2, :], pb1[:, 0:64]), reads=[pk1], writes=[('atok', s2)])
                pb3, pk3 = next_bank(c)
                kb.op('pe', lambda e, pb3=pb3, s2=s2: e.matmul(pb3[:, 0:64], onesf, dtA[:, s2, :], start=True, stop=True),
                      reads=['onesf', ('dtA', s2)], writes=[pk3])
                kb.op('act', lambda e, pb3=pb3, s2=s2: e.activation(eAe[:, s2, :], pb3[:, 0:64], AF.Exp),
                      reads=[pk3], writes=[('eAe', s2)])
                kb.op('dve', lambda e, pb3=pb3, s2=s2: e.tensor_tensor(wtok[:, s2, :], pb3[:, 0:64], atok[:, s2, :], ALU.subtract),
                      reads=[pk3, ('atok', s2)], writes=[('wtok', s2)])
                kb.op('act', lambda e, s2=s2: e.activation(wtok[:, s2, :], wtok[:, s2, :], AF.Exp),
                      reads=[('wtok', s2)], writes=[('wtok', s2)])
                kb.op('dve', lambda e, s2=s2: e.tensor_tensor(wtok[:, s2, :], wtok[:, s2, :], dtt[:, s2, :], ALU.mult),
                      reads=[('wtok', s2), ('dtt', s2)], writes=[('wtok', s2)])
            sres = [reserve_bank(c), reserve_bank(c)]
            z_it = proj_fm(c, win, 'win%d' % l, 'z', rhs_of=rhs_half, ntok=HT)
            tmpF = [(c.msc, 'msc'), (c.mrs, 'mrs')]
            DIb = [(c.mrd, 'mrd'), (c.msg, 'msg')]
            E4 = [(c.mq[:, j, :], ('mq', j)) for j in range(2)]
            M4 = [(c.mqs[:, j, :], ('mqs', j)) for j in range(2)]
            C4 = [(c.mP[:, j, :], ('mP', j)) for j in range(2)]
            v4 = lambda ap: ap.rearrange("p (h l) -> p h l", h=4)
            for g in range(8):
                for q4 in range(4):
                    _, _, zpb, zpk = next(z_it)
                    kb.op('act', lambda e, zpb=zpb, q4=q4: e.activation(szg[:, q4, :], zpb[:, 0:HT], AF.Silu),
                          reads=[zpk], writes=[('szg', q4)])
                for j in range(2):
                    h0 = g * 8 + j * 4
                    kb.op('dve', lambda e, j=j, h0=h0: e.tensor_tensor(
                        v4(DIb[j][0][:, :]), c.ident_f[:].unsqueeze(1).to_broadcast([128, 4, 128]),
                        tab[:, 2, h0:h0 + 4].unsqueeze(2).to_broadcast([128, 4, 128]), ALU.mult),
                        reads=['ident_f', 'tab'], writes=[DIb[j][1]])
                for s2 in range(2):
                    tc = slice(s2 * 128, (s2 + 1) * 128)
                    gb, gk = next_bank(c)
                    kb.op('pe', lambda e, gb=gb, g=g, tc=tc: e.matmul(gb[:, 0:128], BT[:, g, tc], CT[:, g, tc], start=True, stop=True),
                          reads=['BT', 'CT'], writes=[gk])
                    kb.op('dve', lambda e, gb=gb: e.tensor_tensor(Gm, gb[:, 0:128], U, ALU.mult), reads=[gk, 'U'], writes=['Gm'])
                    kb.op('pool', lambda e, g=g: e.tensor_copy(stbf, state[:, g * 512:(g + 1) * 512]),
                          reads=[('state', g)], writes=['stbf'])
                    atb, atk = next_bank(c)
                    kb.op('pe', lambda e, atb=atb, g=g, s2=s2: e.matmul(atb[0:8, 0:128], dtA[:, s2, g * 8:(g + 1) * 8], U, start=True, stop=True),
                          reads=['U', ('dtA', s2)], writes=[atk])
                    kb.op('act', lambda e, atb=atb: e.copy(aTg, atb[0:8, 0:128]), reads=[atk], writes=['aTg'])
                    abk = []
                    for j in range(2):
                        ab, ak = next_bank(c)
                        for hl in range(4):
                            kb.op('pe', lambda e, ab=ab, hl=hl, j=j: e.matmul(ab[:, hl * 128:(hl + 1) * 128], sel8[:, j * 4 + hl, :], aTg,
                                                                              start=True, stop=True),
                                  reads=['sel8', 'aTg'], writes=[ak])
                        abk.append((ab, ak))
                    for j in range(2):
                        h0 = g * 8 + j * 4
                        ab, ak = abk[j]
                        F, Fk = tmpF[j]
                        kb.op('dve', lambda e, ab=ab, F=F, h0=h0, s2=s2: e.tensor_tensor(
                            v4(F[:, :]), v4(ab[:, :]), atok[:, s2, h0:h0 + 4].unsqueeze(2).to_broadcast([128, 4, 128]), ALU.subtract),
                            reads=[ak, ('atok', s2)], writes=[Fk])
                        kb.op('dve', lambda e, F=F: e.tensor_scalar_min(F[:, :], F[:, :], 0.0), reads=[Fk], writes=[Fk])
                        kb.op('act', lambda e, F=F, j=j: e.activation(E4[j][0], F[:, :], AF.Exp), reads=[Fk], writes=[E4[j][1]])
                        kb.op('act', lambda e, ab=ab, j=j: e.activation(C4[j][0], ab[:, :], AF.Exp), reads=[ak], writes=[C4[j][1]])
                        kb.op('dve', lambda e, F=F, h0=h0, s2=s2: e.tensor_tensor(
                            v4(F[:, :]), Gm.unsqueeze(1).to_broadcast([128, 4, 128]),
                            dtt[:, s2, h0:h0 + 4].unsqueeze(2).to_broadcast([128, 4, 128]), ALU.mult),
                            reads=['Gm', ('dtt', s2), E4[j][1]], writes=[Fk])
                        kb.op('dve', lambda e, F=F, j=j: e.tensor_tensor(F[:, :], F[:, :], DIb[j][0][:, :], ALU.add),
                              reads=[Fk, DIb[j][1]], writes=[Fk])
                        kb.op('dve', lambda e, F=F, j=j: e.tensor_tensor(M4[j][0], E4[j][0], F[:, :], ALU.mult),
                              reads=[Fk, E4[j][1]], writes=[M4[j][1]])
                        kb.op('dve', lambda e, j=j, g=g, tc=tc: e.tensor_tensor(
                            v4(C4[j][0]), v4(C4[j][0]), CT[:, g, tc].unsqueeze(1).to_broadcast([128, 4, 128]), ALU.mult),
                            reads=['CT', C4[j][1]], writes=[C4[j][1]])
                    ybank = [None, None]
                    for hh in range(8):
                        h = g * 8 + hh
                        j, hl = hh // 4, hh % 4
                        yb, yk = next_bank(c)
                        ybank[hh % 2] = (yb, yk)
                        pc = h // 2
                        kb.op('pe', lambda e, yb=yb, s2=s2, pc=pc, j=j, hl=hl: e.matmul(
                            yb[:, 0:128], xtok[:, s2, pc * 128:(pc + 1) * 128], M4[j][0][:, hl * 128:(hl + 1) * 128], start=True, stop=False),
                            reads=['xtok', M4[j][1]], writes=[yk])
                        pl = hh // 2
                        kb.op('pe', lambda e, yb=yb, pl=pl, j=j, hl=hl: e.matmul(
                            yb[:, 0:128], stbf[:, pl * 128:(pl + 1) * 128], C4[j][0][:, hl * 128:(hl + 1) * 128], start=False, stop=True),
                            reads=['stbf', C4[j][1]], writes=[yk])
                        if hh % 2 == 1:
                            (ya, yak), (yb2, ybk) = ybank
                            kb.op('dve', lambda e, ya=ya, pl=pl, tc=tc: e.tensor_tensor(
                                ty[0:64, :], ya[0:64, 0:128], szg[0:64, pl, tc], ALU.mult),
                                reads=[yak, ('szg', pl)], writes=['ty'])
                            kb.op('dve', lambda e, yb2=yb2, pl=pl, tc=tc: e.tensor_tensor(
                                ty[64:128, :], yb2[64:128, 0:128], szg[64:128, pl, tc], ALU.mult),
                                reads=[ybk, ('szg', pl)], writes=['ty'])
                            col = slice(t0 + s2 * 128, t0 + (s2 + 1) * 128)
                            kb.op('act', lambda e, pc=pc, col=col: e.copy(c.yT[:, pc, col], ty), reads=['ty'], writes=[('yT', pc)])
                            kb.op('act', lambda e: e.activation(ysq, ty, AF.Square), reads=['ty'], writes=['ysq'])
                            sb_, sk_, _ = sres[s2]
                            kb.op('pe', lambda e, sb_=sb_, pc=pc: e.matmul(sb_[:, 0:128], c.ones_bf[:], ysq, start=(pc == 0), stop=(pc == 31)),
                                  reads=['ones_bf', 'ysq'], writes=[sk_])
                    kb.op('dve', lambda e, g=g, s2=s2: e.tensor_tensor(
                        xw.rearrange("p (h j) -> p h j", h=8), xtok[:, s2, g * 512:(g + 1) * 512].rearrange("p (h j) -> p h j", h=8),
                        wtok[:, s2, g * 8:(g + 1) * 8].unsqueeze(2).to_broadcast([128, 8, 64]), ALU.mult),
                        reads=['xtok', ('wtok', s2)], writes=['xw'])
                    nb_, nk_ = next_bank(c)
                    kb.op('pe', lambda e, nb_=nb_, s2=s2, g=g: e.matmul(nb_[:, :], Btok[:, s2, g * 128:(g + 1) * 128], xw, start=True, stop=True),
                          reads=['Btok', 'xw'], writes=[nk_])
                    for hh in range(8):
                        h = g * 8 + hh
                        kb.op('dve', lambda e, nb_=nb_, hh=hh, h=h, s2=s2: e.scalar_tensor_tensor(
                            state[:, h * 64:(h + 1) * 64], state[:, h * 64:(h + 1) * 64], eAe[:, s2, h:h + 1],
                            nb_[:, hh * 64:(hh + 1) * 64], ALU.mult, ALU.add),
                            reads=[nk_, ('eAe', s2), ('state', g)], writes=[('state', g)])
            for s2 in range(2):
                sb_, sk_, sidx = sres[s2]
                kb.op('dve', lambda e, sb_=sb_, s2=s2: e.tensor_scalar(rbc[:, s2 * 128:(s2 + 1) * 128], sb_[:, 0:128], 1.0 / 4096, EPS,
                                                                       ALU.mult, ALU.add), reads=[sk_], writes=['rbc'])
                release_bank(c, sidx)
            kb.op('act', lambda e: e.activation(rbc, rbc, AF.Sqrt), reads=['rbc'], writes=['rbc'])
            kb.op('dve', lambda e: e.reciprocal(rbc, rbc), reads=['rbc'], writes=['rbc'])
            for pc in range(32):
                kb.op('dve', lambda e, pc=pc, t0=t0: e.scalar_tensor_tensor(
                    c.yT[:, pc, t0:t0 + HT], c.yT[:, pc, t0:t0 + HT], gn[:, pc:pc + 1], rbc, ALU.mult, ALU.mult),
                    reads=[('yT', pc), 'gn', 'rbc'], writes=[('yT', pc)])
        mem_branch(c, l, win)
        wout_phase(c, l, wo, x_src, dst, tt, nxt=(x_src if tt + 1 < c.NT else None))


I32 = mybir.dt.int32
NSA_SCALE = 128 ** -0.5


def layer1(c, l, x_src, dst):
    kb = c.kb
    S = c.S
    NQT = S // 128
    NSEL = S // 64
    NC = S // 16 - 1
    NCT = (NC + 127) // 128
    NCP = NCT * 128
    prep_begin(c)
    gain = load_pc(c, 0, 'l%d_norm_pc' % l, KD)
    segs = [('q', 0, 4096), ('kc', 4096, 512), ('vc', 4608, 512), ('ks', 5120, 512), ('vs', 5632, 512),
            ('kw', 6144, 512), ('vw', 6656, 512), ('gl', 7168, 96), ('gate', 7264, 4096),
            ('memq', 11360, 1024), ('memg', 12384, 1024)]
    wname = 'win%d' % l
    win = prep_w(c, wname, c.din('l%d_w_in' % l, [D, 13408]), D, segs, rowscale=gain)
    wo = prep_w(c, 'wo%d' % l, c.din('l%d_w_out' % l, [5120, D]), 5120, [('all', 0, D)])
    w1k = prep_w(c, 'w1k', c.din('l%d_cmp_k_w1' % l, [4096, 128]), 4096, [('all', 0, 128)])
    w1v = prep_w(c, 'w1v', c.din('l%d_cmp_v_w1' % l, [4096, 128]), 4096, [('all', 0, 128)])
    w2k = prep_w(c, 'w2k', c.din('l%d_cmp_k_w2' % l, [128, 128]), 128, [('all', 0, 128)])
    w2v = prep_w(c, 'w2v', c.din('l%d_cmp_v_w2' % l, [128, 128]), 128, [('all', 0, 128)])
    mem_prep_layer(c, l)
    QT = c.dscr('nsa_QT', [32, 128, S], BF16)
    KT = c.dscr('nsa_KT', [3, 4, 128, S], BF16)
    VcT = c.dscr('nsa_VcT', [4, 128, S], BF16)
    Vsw = c.dscr('nsa_Vsw', [2, S, 512], BF16)
    Gs = c.dscr('nsa_Gs', [S, 96], F32)
    SG = c.dscr('nsa_SG', [32, 128, S], BF16)
    Y = c.dscr('nsa_Y', [32, 128, S], BF16)
    cosD = c.dscr('nsa_cos', [128, S], F32)
    sinD = c.dscr('nsa_sin', [128, S], F32)

    c.areset()
    A = c.aalloc
    PW = min(1024, S)
    pi = A('pi', [128, PW], I32)
    ang = A('ang', [128, PW], F32)
    uu = A('uu', [128, PW], F32)
    nn_i = A('nn_i', [128, PW], I32)
    nn_f = A('nn_f', [128, PW], F32)
    inv = A('inv', [128, 1], F32)
    kb.dma('pool', inv, c.din('rope_inv_pc', [128, 1]), writes=['inv'])
    pos_in = c.din('pos_bc', [128, S], I32)
    for p0 in range(0, S, PW):
        kb.dma('pool', pi, pos_in[:, p0:p0 + PW], writes=['pi'])
        kb.op('dve', lambda e: e.tensor_copy(ang, pi), reads=['pi'], writes=['ang'])
        kb.op('dve', lambda e: e.tensor_scalar(ang, ang, inv[:, 0:1], None, ALU.mult), reads=['ang', 'inv'], writes=['ang'])
        for which, off, dstD in (('sin', 0.5, sinD), ('cos', 0.75, cosD)):
            kb.op('dve', lambda e, off=off: e.tensor_scalar(uu, ang, 1.0 / (2 * np.pi), off, ALU.mult, ALU.add),
                  reads=['ang'], writes=['uu'])
            kb.op('dve', lambda e: e.tensor_copy(nn_i, uu), reads=['uu'], writes=['nn_i'])
            kb.op('dve', lambda e: e.tensor_copy(nn_f, nn_i), reads=['nn_i'], writes=['nn_f'])
            kb.op('dve', lambda e: e.tensor_tensor(uu, uu, nn_f, ALU.subtract), reads=['uu', 'nn_f'], writes=['uu'])
            kb.op('dve', lambda e: e.tensor_scalar(nn_f, uu, 0.0, None, ALU.is_lt), reads=['uu'], writes=['nn_f'])
            kb.op('dve', lambda e: e.tensor_tensor(uu, uu, nn_f, ALU.add), reads=['uu', 'nn_f'], writes=['uu'])
            kb.op('act', lambda e: e.activation(uu, uu, AF.Sin, scale=2 * np.pi, bias=-np.pi), reads=['uu'], writes=['uu'])
            kb.dma('pool', dstD[:, p0:p0 + PW], uu, reads=['uu'], writes=['ropeD'])

    c.areset()
    c.rows2()
    cosT = A('cosT', [128, TT], F32)
    sinT = A('sinT', [128, TT], F32)
    NRB = 3
    xsqs = [A('xsq%d' % i, [128, TT], BF16) for i in range(NRB)]
    rsbs = [A('rsb%d' % i, [128, TT], F32) for i in range(NRB)]
    xns = [A('xn%d' % i, [128, TT], BF16) for i in range(NRB)]
    t1s = [A('t1_%d' % i, [128, TT], F32) for i in range(NRB)]
    t2s = [A('t2_%d' % i, [128, TT], F32) for i in range(NRB)]
    ob = [A('ob%d' % i, [128, TT], BF16) for i in range(4)]
    obr = [0]
    gsb = [A('gsb%d' % i, [128, 96], F32) for i in range(2)]
    prot = A('prot', [128, 128], BF16)
    protf = A('protf', [128, 128], F32)
    qkn = A('qkn', [128, 2], F32)
    wgl = A('wgl', [128, KD, 96], BF16)
    kb.dma('pool', protf, c.din('rope_prot', [128, 128]), writes=['protf'])
    kb.op('dve', lambda e: e.tensor_copy(prot, protf), reads=['protf'], writes=['prot'])
    kb.dma('pool', qkn[:, 0:1], c.din('l%d_qnorm_pc' % l, [128, 1]), writes=['qkn'])
    kb.dma('pool', qkn[:, 1:2], c.din('l%d_knorm_pc' % l, [128, 1]), writes=['qkn'])
    kb.dma('sp', wgl, win['gl'][0][0, :, :, 0:96], reads=[('wscr', wname, 'gl', 0)], writes=['wgl'])

    def next_ob():
        i = obr[0]
        obr[0] = (i + 1) % 4
        return ob[i], ('ob', i)

    def nr_stage1(job, bi):
        pb, pk, gcol, dst_ap = job
        xsq = xsqs[bi]
        kb.op('act', lambda e: e.activation(xsq, pb[:, :], AF.Square), reads=[pk], writes=[('xsq', bi)])

    def nr_stage2(job, bi):
        pb, pk, gcol, dst_ap = job
        xsq, rsb, xn = xsqs[bi], rsbs[bi], xns[bi]
        p2, k2 = next_bank(c)
        kb.op('pe', lambda e: e.matmul(p2[:, :], c.ones_bf[:], xsq, start=True, stop=True), reads=['ones_bf', ('xsq', bi)], writes=[k2])
        kb.op('act', lambda e: e.activation(rsb, p2[:, :], AF.Ln, scale=1.0 / 128, bias=EPS), reads=[k2], writes=[('rsb', bi)])
        kb.op('act', lambda e: e.activation(rsb, rsb, AF.Exp, scale=-0.5), reads=[('rsb', bi)], writes=[('rsb', bi)])
        kb.op('dve', lambda e: e.scalar_tensor_tensor(xn, pb[:, :], qkn[:, gcol:gcol + 1], rsb, ALU.mult, ALU.mult),
              reads=[pk, 'qkn', ('rsb', bi)], writes=[('xn', bi)])

    def nr_stage3(job, bi):
        pb, pk, gcol, dst_ap = job
        xn, t1, t2 = xns[bi], t1s[bi], t2s[bi]
        p3, k3 = next_bank(c)
        kb.op('pe', lambda e: e.matmul(p3[:, :], prot, xn, start=True, stop=True), reads=['prot', ('xn', bi)], writes=[k3])
        kb.op('dve', lambda e: e.tensor_tensor(t1, xn, cosT, ALU.mult), reads=[('xn', bi), 'cosT'], writes=[('t1', bi)])
        kb.op('dve', lambda e: e.tensor_tensor(t2, p3[:, :], sinT, ALU.mult), reads=[k3, 'sinT'], writes=[('t2', bi)])
        o, ok = next_ob()
        kb.op('pool', lambda e: e.tensor_tensor(o, t1, t2, ALU.add), reads=[('t1', bi), ('t2', bi)], writes=[ok])
        kb.dma('pool', dst_ap, o, reads=[ok], writes=['nsaD'])

    def norm_rope_pipeline(jobgen):
        jobs = []
        done = False
        i = 0
        while True:
            if not done:
                try:
                    jobs.append(next(jobgen))
                    nr_stage1(jobs[i], i % NRB)
                except StopIteration:
                    done = True
            if 0 <= i - 1 < len(jobs):
                nr_stage2(jobs[i - 1], (i - 1) % NRB)
            if 0 <= i - 2 < len(jobs):
                nr_stage3(jobs[i - 2], (i - 2) % NRB)
            i += 1
            if done and i - 2 >= len(jobs):
                break

    for tt in range(c.NT):
        tsl = slice(tt * TT, (tt + 1) * TT)
        load_norm_transpose(c, x_src, tt)
        kb.dma('pool', cosT, cosD[:, tsl], reads=['ropeD'], writes=['cosT'])
        kb.dma('pool', sinT, sinD[:, tsl], reads=['ropeD'], writes=['sinT'])
        def jobgen(tsl=tsl):
            for ci, m, pb, pk in proj_fm(c, win, wname, 'q'):
                yield (pb, pk, 0, QT[ci, :, tsl])
            for ki, sname in enumerate(('kc', 'ks', 'kw')):
                for ci, m, pb, pk in proj_fm(c, win, wname, sname):
                    yield (pb, pk, 1, KT[ki, ci, :, tsl])
        norm_rope_pipeline(jobgen())
        for ci, m, pb, pk in proj_fm(c, win, wname, 'vc'):
            o, ok = next_ob()
            kb.op('act', lambda e, o=o, pb=pb: e.copy(o, pb[:, :]), reads=[pk], writes=[ok])
            kb.dma('pool', VcT[ci, :, tsl], o, reads=[ok], writes=['nsaD'])
        for ci, m, pb, pk in proj_fm(c, win, wname, 'gate'):
            o, ok = next_ob()
            kb.op('act', lambda e, o=o, pb=pb: e.activation(o, pb[:, :], AF.Silu), reads=[pk], writes=[ok])
            kb.dma('pool', SG[ci, :, tsl], o, reads=[ok], writes=['nsaD'])
        for vi, sname in enumerate(('vs', 'vw')):
            wt, wk = load_w_dep(c, win[sname][0][0, :, :, :], ('wscr', wname, sname, 0))
            for s4 in range(4):
                pb, pk = next_bank(c)
                for k in range(KD):
                    kb.op('pe', lambda e, pb=pb, wt=wt, k=k, s4=s4: e.matmul(
                        pb[:, :], c.hT[:, k, s4 * 128:(s4 + 1) * 128], wt[:, k, :], start=(k == 0), stop=(k == KD - 1)),
                        reads=[wk, ('hT', s4)], writes=[pk])
                o, ok = next_ob()
                kb.op('act', lambda e, o=o, pb=pb: e.copy(o, pb[:, :]), reads=[pk], writes=[ok])
                r0 = tt * TT + s4 * 128
                kb.dma('pool', Vsw[vi, r0:r0 + 128, :], o, reads=[ok], writes=['nsaD'])
        for s4 in range(4):
            pb, pk = next_bank(c)
            for k in range(KD):
                kb.op('pe', lambda e, pb=pb, k=k, s4=s4: e.matmul(
                    pb[:, 0:96], c.hT[:, k, s4 * 128:(s4 + 1) * 128], wgl[:, k, :], start=(k == 0), stop=(k == KD - 1)),
                    reads=['wgl', ('hT', s4)], writes=[pk])
            gi = s4 % 2
            kb.op('act', lambda e, pb=pb, gi=gi: e.activation(gsb[gi], pb[:, 0:96], AF.Sigmoid), reads=[pk], writes=[('gsb', gi)])
            r0 = tt * TT + s4 * 128
            kb.dma('pool', Gs[r0:r0 + 128, :], gsb[gi], reads=[('gsb', gi)], writes=['nsaD'])

    c.areset()
    kcT = A('kcT', [128, 4, NCP], BF16)
    vcm = A('vcm', [128, 4, NCT, 128], BF16)
    KsT = A('KsT', [128, S], BF16)
    KwT = A('KwT', [128, S], BF16)
    Vs = A('Vs', [128, NQT, 128], BF16)
    Vw = A('Vw', [128, NQT, 128], BF16)
    Efull = A('Efull', [64, S], BF16)
    Eff = None
    aggf = A('aggf', [128, NCT, NSEL], F32)
    agg = A('agg', [128, NCT, NSEL], BF16)
    onesf = A('onesf', [128, 128], F32)
    qT = A('qT', [128, 8, 128], BF16)
    PT = [A('PT%d' % i, [128, 1024], BF16) for i in range(2)]
    PcDg = A('PcDg', [128, 1024], F32)
    Pc = PcDg.bitcast(BF16).rearrange("p (a b) -> p a b", a=2)
    Rt = A('Rt', [128, 1024], F32)
    acc = A('acc', [128, 1024], F32)
    Dg = PcDg
    DGK = [('Pc', 0), ('Pc', 1)]
    sgt = A('sgt', [128, 8, 128], BF16)
    yo = A('yo', [128, 8, 128], BF16)
    gat = A('gat', [128, 96], F32)
    tabs = A('tabs', [128, 2, NSEL], F32)
    sc1 = A('sc1', [128, NSEL], F32)
    sc2 = A('sc2', [128, NSEL], F32)
    m8 = A('m8', [128, 16], F32)
    selb = A('selb', [128, 64], BF16)
    selT4 = A('selT4', [64, 512], BF16)
    hTc = A('hTc', [128, NCP], BF16)
    cj = A('cj', [128, 2], F32)
    posT = A('posT', [128, 32], BF16)
    posTf = A('posTf', [128, 32], F32)
    w2t = A('w2t', [128, 128], BF16)
    kb.op('dve', lambda e: e.memset(onesf, 1.0), writes=['onesf'])
    kb.op('dve', lambda e: e.memset(selb, 0.0), writes=['selb'])
    kb.dma('pool', aggf, c.din('nsa_agg', [128, NCT, NSEL]), writes=['aggf'])
    kb.op('dve', lambda e: e.tensor_copy(agg, aggf), reads=['aggf'], writes=['agg'])
    kb.dma('pool', posTf, c.din('l%d_cmp_posT' % l, [128, 32]), writes=['posTf'])
    kb.op('dve', lambda e: e.tensor_copy(posT, posTf), reads=['posTf'], writes=['posT'])
    ef_in = c.din('nsa_E', [64, S])
    for e0 in range(0, S, 1024):
        ew = min(1024, S - e0)
        kb.dma('pool', Rt[0:64, 0:ew], ef_in[:, e0:e0 + ew], writes=['Rt'])
        kb.op('dve', lambda e, e0=e0, ew=ew: e.tensor_copy(Efull[:, e0:e0 + ew], Rt[0:64, 0:ew]), reads=['Rt'], writes=['Efull'])
    w1t = KwT[:, 0:4096].rearrange("p (t j) -> p t j", t=32) if S >= 4096 else None
    if w1t is None:
        w1t = A('w1t', [128, 32, 128], BF16)
        w1key = 'w1t'
    else:
        w1key = 'KwT'
    for kind, (w1s, w2s, srcT) in enumerate(((w1k, w2k, None), (w1v, w2v, None))):
        kb.dma('sp', w1t, w1s['all'][0][0, :, :, 0:128], reads=[('wscr', 'w1k' if kind == 0 else 'w1v', 'all', 0)], writes=[w1key])
        kb.dma('sp', w2t, w2s['all'][0][0, :, 0, 0:128], reads=[('wscr', 'w2k' if kind == 0 else 'w2v', 'all', 0)], writes=['w2t'])
        pcj, kcj = next_bank(c)
        for tau in range(32):
            kb.op('pe', lambda e, tau=tau, pcj=pcj: e.matmul(pcj[:, 0:1], w1t[:, tau, :], posT[:, tau:tau + 1],
                                                              start=(tau == 0), stop=(tau == 31)),
                  reads=[w1key, 'posT'], writes=[kcj])
        kb.op('act', lambda e, pcj=pcj, kind=kind: e.copy(cj[:, kind:kind + 1], pcj[:, 0:1]), reads=[kcj], writes=['cj'])
        for hk in range(4):
            srcD = KT[0, hk] if kind == 0 else VcT[hk]
            kb.dma('pool', KsT, srcD, writes=['KsT'])
            pp, pk = next_bank(c)
            for tau in range(32):
                kb.op('pe', lambda e, tau=tau, pp=pp: e.matmul(pp[:, 0:NC], w1t[:, tau, :], KsT[:, tau:tau + 16 * (NC - 1) + 1:16],
                                                                start=(tau == 0), stop=(tau == 31)),
                      reads=[w1key, 'KsT'], writes=[pk])
            kb.op('dve', lambda e: e.memset(hTc, 0.0), writes=['hTc'])
            kb.op('act', lambda e, pp=pp, kind=kind: e.activation(hTc[:, 0:NC], pp[:, 0:NC], AF.Silu, bias=cj[:, kind:kind + 1]),
                  reads=[pk, 'cj'], writes=['hTc'])
            if kind == 0:
                p2, k2 = next_bank(c)
                kb.op('pe', lambda e, p2=p2: e.matmul(p2[:, 0:NCP], w2t, hTc, start=True, stop=True), reads=['w2t', 'hTc'], writes=[k2])
                kb.op('act', lambda e, p2=p2, hk=hk: e.copy(kcT[:, hk, :], p2[:, 0:NCP]), reads=[k2], writes=['kcT'])
            else:
                for ct in range(NCT):
                    p2, k2 = next_bank(c)
                    kb.op('pe', lambda e, p2=p2, ct=ct: e.matmul(p2[:, 0:128], hTc[:, ct * 128:(ct + 1) * 128], w2t, start=True, stop=True),
                          reads=['w2t', 'hTc'], writes=[k2])
                    kb.op('act', lambda e, p2=p2, hk=hk, ct=ct: e.copy(vcm[:, hk, ct, :], p2[:, 0:128]), reads=[k2], writes=['vcm'])

    def exp_tile(sbanks, P):
        for hf in range(2):
            pb, pk = sbanks[hf]
            kb.op('act', lambda e, pb=pb, hf=hf, P=P: e.activation(P[0][:, hf * 512:(hf + 1) * 512], pb[:, :], AF.Exp, scale=NSA_SCALE),
                  reads=[pk], writes=[P[1]])

    def score_tile(lhsT, lkey):
        sb2 = []
        for hf in range(2):
            pb, pk = next_bank(c)
            kb.op('pe', lambda e, pb=pb, hf=hf: e.matmul(pb[:, :], lhsT, qT[:, hf * 4:(hf + 1) * 4, :], start=True, stop=True),
                  reads=[lkey, 'qT'], writes=[pk])
            sb2.append((pb, pk))
        return sb2

    def amask(P, base, cm, qstep):
        pv = P[0].rearrange("p (h q) -> p h q", h=8)
        kb.op('pool', lambda e: e.affine_select(out=pv, in_=pv, pattern=[[0, 8], [qstep, 128]], compare_op=ALU.is_ge,
                                                fill=0.0, base=base, channel_multiplier=cm),
              reads=[P[1]], writes=[P[1]])

    def gate_bc(hk, br):
        c0 = hk * 24 + br
        kb.op('dve', lambda e, c0=c0: e.tensor_tensor(
            Dg.rearrange("p (h q) -> p h q", h=8), c.ident_f[:].unsqueeze(1).to_broadcast([128, 8, 128]),
            gat[:, c0:c0 + 22:3].unsqueeze(2).to_broadcast([128, 8, 128]), ALU.mult),
            reads=['ident_f', 'gat'], writes=DGK)
        gb = []
        for hf in range(2):
            pb, pk = next_bank(c)
            kb.op('pe', lambda e, pb=pb, hf=hf: e.matmul(pb[:, :], onesf, Dg[:, hf * 512:(hf + 1) * 512], start=True, stop=True),
                  reads=['onesf'] + DGK, writes=[pk])
            gb.append((pb, pk))
        return gb

    for hk in range(4):
        kb.dma('pool', KsT, KT[1, hk], writes=['KsT'])
        kb.dma('pool', KwT, KT[2, hk], writes=['KwT'])
        kb.dma('pool', Vs, Vsw[0, :, hk * 128:(hk + 1) * 128].rearrange("(t k) d -> k t d", k=128), writes=['Vs'])
        kb.dma('pool', Vw, Vsw[1, :, hk * 128:(hk + 1) * 128].rearrange("(t k) d -> k t d", k=128), writes=['Vw'])
        for qt in range(NQT):
            qsl = slice(qt * 128, (qt + 1) * 128)
            kb.dma('pool', qT, QT[hk * 8:(hk + 1) * 8, :, qsl].rearrange("h d q -> d h q"), writes=['qT'])
            kb.dma('pool', sgt, SG[hk * 8:(hk + 1) * 8, :, qsl].rearrange("h d q -> d h q"), writes=['sgt'])
            kb.dma('pool', gat, Gs[qsl, :], writes=['gat'])
            kb.dma('pool', tabs, c.din_cached('nsa_tabs', [NQT, 128, 2, NSEL])[qt], writes=['tabs'])
            cts = []
            for ct in range(NCT):
                base = 128 * qt - 2048 * ct - 31
                if base + 127 < 0:
                    continue
                cts.append((ct, base))
            rsb_ = [reserve_bank(c), reserve_bank(c)]
            for n_, (ct, base) in enumerate(cts):
                sb2 = score_tile(kcT[:, hk, ct * 128:(ct + 1) * 128], 'kcT')
                P = (Pc[:, ct, :], ('Pc', ct))
                exp_tile(sb2, P)
                if base - 2032 < 0:
                    amask(P, base, -16, 1)
                for hf in range(2):
                    rb, rk, _ = rsb_[hf]
                    kb.op('pe', lambda e, rb=rb, hf=hf, P=P, n_=n_, nl=len(cts): e.matmul(rb[:, :], c.ones_bf[:], P[0][:, hf * 512:(hf + 1) * 512],
                                                                            start=(n_ == 0), stop=(n_ == nl - 1)),
                          reads=['ones_bf', P[1]], writes=[rk])
            if cts:
                for hf in range(2):
                    rb, rk, ridx = rsb_[hf]
                    hs = slice(hf * 512, (hf + 1) * 512)
                    kb.op('act', lambda e, rb=rb, hs=hs: e.activation(Rt[:, hs], rb[:, :], AF.Ln, bias=1e-30), reads=[rk], writes=['Rt'])
                    kb.op('act', lambda e, hs=hs: e.activation(Rt[:, hs], Rt[:, hs], AF.Exp, scale=-1.0), reads=['Rt'], writes=['Rt'])
                    for (ct, base) in cts:
                        kb.op('dve', lambda e, hs=hs, ct=ct: e.tensor_tensor(Pc[:, ct, hs], Pc[:, ct, hs], Rt[:, hs], ALU.mult),
                              reads=['Rt', ('Pc', ct)], writes=[('Pc', ct)])
            for hf in range(2):
                release_bank(c, rsb_[hf][2])
            if cts:
                ib, ik = next_bank(c)
                nmm = 8 * len(cts)
                i_ = 0
                for (ct, base) in cts:
                    for h in range(8):
                        kb.op('pe', lambda e, ib=ib, ct=ct, h=h, i_=i_, nmm=nmm: e.matmul(ib[:, 0:NSEL], Pc[:, ct, h * 128:(h + 1) * 128], agg[:, ct, :],
                                                                                 start=(i_ == 0), stop=(i_ == nmm - 1)),
                              reads=[('Pc', ct), 'agg'], writes=[ik])
                        i_ += 1
                kb.op('dve', lambda e, ib=ib: e.tensor_tensor(sc1, ib[:, 0:NSEL], tabs[:, 0, :], ALU.mult), reads=[ik, 'tabs'], writes=['sc1'])
                kb.op('dve', lambda e: e.tensor_tensor(sc1, sc1, tabs[:, 1, :], ALU.add), reads=['sc1', 'tabs'], writes=['sc1'])
            else:
                kb.op('dve', lambda e: e.tensor_copy(sc1, tabs[:, 1, :]), reads=['tabs'], writes=['sc1'])
            kb.op('dve', lambda e: e.max(out=m8[:, 0:8], in_=sc1), reads=['sc1'], writes=['m8'])
            kb.op('dve', lambda e: e.match_replace(out=sc2, in_to_replace=m8[:, 0:8], in_values=sc1, imm_value=-1e9),
                  reads=['sc1', 'm8'], writes=['sc2'])
            kb.op('dve', lambda e: e.max(out=m8[:, 8:16], in_=sc2), reads=['sc2'], writes=['m8'])
            kb.op('dve', lambda e: e.tensor_scalar(sc2, sc1, m8[:, 15:16], None, ALU.is_ge), reads=['sc1', 'm8'], writes=['sc2'])
            kb.op('dve', lambda e: e.tensor_tensor(selb[:, 0:NSEL], sc2, tabs[:, 0, :], ALU.mult), reads=['sc2', 'tabs'], writes=['selb'])
            if 'dbg_sel' in DEBUG_OUT:
                if not hasattr(c, 'dbg_sel'):
                    c.dbg_sel = c.dscr('dbg_sel', [4, NQT, 128, NSEL], F32)
                    c.dbg_sc1 = c.dscr('dbg_sc1', [4, NQT, 128, NSEL], F32)
                kb.dma('pool', c.dbg_sel[hk, qt], sc2, reads=['sc2'], writes=['dbgsel'])
                kb.dma('pool', c.dbg_sc1[hk, qt], sc1, reads=['sc1'], writes=['dbgsel'])
            tb, tk = next_bank(c)
            tbv = tb[:].bitcast(BF16)
            kb.op('pe', lambda e, tbv=tbv: e.transpose(tbv[0:64, 0:128], selb[:, 0:64], c.ident[:]), reads=['selb', 'ident'], writes=[tk])
            for r4 in range(4):
                kb.op('dve', lambda e, tbv=tbv, r4=r4: e.tensor_scalar(selT4[:, r4 * 128:(r4 + 1) * 128], tbv[0:64, 0:128], 1.0, 29952.0,
                                                                      ALU.subtract, ALU.mult), reads=[tk], writes=['selT4'])
            if cts:
                ob2 = []
                for hf in range(2):
                    pb, pk = next_bank(c)
                    for n_, (ct, base) in enumerate(cts):
                        kb.op('pe', lambda e, pb=pb, hf=hf, ct=ct, n_=n_, hk=hk, nl=len(cts): e.matmul(pb[:, :], vcm[:, hk, ct, :], Pc[:, ct, hf * 512:(hf + 1) * 512],
                                                                                  start=(n_ == 0), stop=(n_ == nl - 1)),
                              reads=['vcm', ('Pc', ct)], writes=[pk])
                    ob2.append((pb, pk))
                gb = gate_bc(hk, 0)
                for hf in range(2):
                    hs = slice(hf * 512, (hf + 1) * 512)
                    kb.op('act', lambda e, hf=hf, hs=hs, gb=gb: e.copy(Rt[:, hs], gb[hf][0][:, :]), reads=[gb[hf][1]], writes=['Rt'])
                    kb.op('dve', lambda e, hf=hf, hs=hs, ob2=ob2: e.tensor_tensor(acc[:, hs], Rt[:, hs], ob2[hf][0][:, :], ALU.mult),
                          reads=['Rt', ob2[hf][1]], writes=['acc'])
                if 0 not in NSA_BR:
                    kb.op('dve', lambda e: e.memset(acc, 0.0), writes=['acc'])
            else:
                kb.op('dve', lambda e: e.memset(acc, 0.0), writes=['acc'])
            for br in (1, 2):
                if br == 1:
                    kts = list(range(0, qt + 1))
                    Ksrc, Kkey, Vsrc, Vkey = KsT, 'KsT', Vs, 'Vs'
                else:
                    kts = list(range(max(0, qt - 4), qt + 1))
                    Ksrc, Kkey, Vsrc, Vkey = KwT, 'KwT', Vw, 'Vw'
                obk = [reserve_bank(c), reserve_bank(c)]
                rbk = [reserve_bank(c), reserve_bank(c)]
                def flush(pend, nl):
                    P, kt, n_ = pend
                    for hf in range(2):
                        hs = slice(hf * 512, (hf + 1) * 512)
                        pb, pk, _ = obk[hf]
                        kb.op('pe', lambda e, pb=pb, kt=kt, hs=hs, P=P, n_=n_, Vsrc=Vsrc, nl=nl: e.matmul(pb[:, :], Vsrc[:, kt, :], P[0][:, hs],
                                                                                               start=(n_ == 0), stop=(n_ == nl - 1)),
                              reads=[Vkey, P[1]], writes=[pk])
                        rb, rk, _ = rbk[hf]
                        kb.op('pe', lambda e, rb=rb, hs=hs, P=P, n_=n_, nl=nl: e.matmul(rb[:, :], c.ones_bf[:], P[0][:, hs],
                                                                                    start=(n_ == 0), stop=(n_ == nl - 1)),
                              reads=['ones_bf', P[1]], writes=[rk])
                pend = None
                for n_, kt in enumerate(kts):
                    ksl = slice(kt * 128, (kt + 1) * 128)
                    sb2 = []
                    for hf in range(2):
                        pb, pk = next_bank(c)
                        kb.op('pe', lambda e, pb=pb, hf=hf, ksl=ksl, Ksrc=Ksrc, br=br: e.matmul(
                            pb[:, :], Ksrc[:, ksl], qT[:, hf * 4:(hf + 1) * 4, :], start=True, stop=(br != 1)),
                            reads=[Kkey, 'qT'], writes=[pk])
                        if br == 1:
                            kb.op('pe', lambda e, pb=pb, ksl=ksl: e.matmul(pb[:, :], Efull[:, ksl], selT4, start=False, stop=True),
                                  reads=['Efull', 'selT4'], writes=[pk])
                        sb2.append((pb, pk))
                    pi_ = n_ % 2
                    P = (PT[pi_], ('PT', pi_))
                    exp_tile(sb2, P)
                    if kt == qt:
                        amask(P, 0, -1, 1)
                    if br == 2 and kt == qt - 4:
                        amask(P, -1, 1, -1)
                    if pend is not None:
                        flush(pend, len(kts))
                    pend = (P, kt, n_)
                flush(pend, len(kts))
                gb = gate_bc(hk, br)
                for hf in range(2):
                    hs = slice(hf * 512, (hf + 1) * 512)
                    pb, pk, oidx = obk[hf]
                    rb, rk, ridx = rbk[hf]
                    kb.op('act', lambda e, rb=rb, hs=hs: e.activation(Rt[:, hs], rb[:, :], AF.Ln), reads=[rk], writes=['Rt'])
                    kb.op('act', lambda e, hs=hs: e.activation(Rt[:, hs], Rt[:, hs], AF.Exp, scale=-1.0), reads=['Rt'], writes=['Rt'])
                    kb.op('dve', lambda e, hs=hs, hf=hf, gb=gb: e.tensor_tensor(Rt[:, hs], Rt[:, hs], gb[hf][0][:, :], ALU.mult),
                          reads=['Rt', gb[hf][1]], writes=['Rt'])
                    kb.op('dve', lambda e, pb=pb, hs=hs: e.tensor_tensor(Rt[:, hs], Rt[:, hs], pb[:, :], ALU.mult),
                          reads=['Rt', pk], writes=['Rt'])
                    if br in NSA_BR:
                        kb.op('dve', lambda e, hs=hs: e.tensor_tensor(acc[:, hs], acc[:, hs], Rt[:, hs], ALU.add),
                              reads=['Rt', 'acc'], writes=['acc'])
                    release_bank(c, oidx)
                    release_bank(c, ridx)
            kb.op('dve', lambda e: e.tensor_tensor(yo, acc[:, :].rearrange("p (h q) -> p h q", h=8), sgt, ALU.mult),
                  reads=['acc', 'sgt'], writes=['yo'])
            kb.dma('pool', Y[hk * 8:(hk + 1) * 8, :, qsl].rearrange("h d q -> d h q"), yo, reads=['yo'], writes=['nsaY'])

    c.areset()
    c.rows2()
    for tt in range(c.NT):
        tsl = slice(tt * TT, (tt + 1) * TT)
        if tt == 0:
            load_norm_transpose(c, x_src, tt)
        kb.dma('pool', c.yT[:, 0:32, :], Y[:, :, tsl].rearrange("h d t -> d h t"), reads=['nsaY'], writes=[('yT', f) for f in range(32)])
        mem_branch(c, l, win)
        wout_phase(c, l, wo, x_src, dst, tt, nxt=(x_src if tt + 1 < c.NT else None))


LAYER_FNS = {0: layer0, 1: layer1, 2: layer2, 3: layer3}


def _pc(v, nchunk):
    return np.ascontiguousarray(np.asarray(v, np.float32).reshape(nchunk, 128).T)


def _bc(v):
    v = np.asarray(v, np.float32).reshape(1, -1)
    return np.ascontiguousarray(np.broadcast_to(v, (128, v.shape[1])))


def make_in_map(inp, b, S, layers):
    m = {}
    m['x'] = np.ascontiguousarray(inp['x'][b, :S])
    m['mem'] = np.ascontiguousarray(inp['mem'][b])
    m['ident'] = np.eye(128, dtype=np.float32)
    m['mem_norm_pc'] = _pc(inp['mem_norm'], KD)
    for l in layers:
        p = 'l%d_' % l
        m[p + 'norm_pc'] = _pc(inp[p + 'norm'], KD)
        m[p + 'w_in'] = inp[p + 'w_in']
        m[p + 'w_out'] = inp[p + 'w_out']
        m[p + 'mem_wkv'] = inp[p + 'mem_wkv']
        m[p + 'mem_qnorm_pc'] = _pc(inp[p + 'mem_qnorm'], 2)
        m[p + 'mem_knorm_pc'] = _pc(inp[p + 'mem_knorm'], 2)
        if l == 0:
            m[p + 'pool_w'] = inp[p + 'pool_w']
            m[p + 'pool_scale_pc'] = _pc(inp[p + 'pool_scale'], 32)
            t = np.arange(TT)
            rc = np.stack([1.0 / np.minimum(t + 1, w) for w in POOL_W]).astype(np.float32)
            m['pool_rc0'] = np.ascontiguousarray(np.broadcast_to(rc[None], (128, 4, TT)))
        if l == 1:
            m.update(nsa_host_tables(S))
            m['pos_bc'] = np.ascontiguousarray(np.broadcast_to(inp['positions'][b, :S][None].astype(np.int32), (128, S)))
            m[p + 'qnorm_pc'] = _pc(inp[p + 'qnorm'], 1)
            m[p + 'knorm_pc'] = _pc(inp[p + 'knorm'], 1)
            m[p + 'cmp_posT'] = np.ascontiguousarray(inp[p + 'cmp_pos'].T)
            for nm in ('cmp_k_w1', 'cmp_k_w2', 'cmp_v_w1', 'cmp_v_w2'):
                m[p + nm] = inp[p + nm]
        if l == 2:
            m[p + 'conv_w_pc'] = np.ascontiguousarray(inp[p + 'conv_w'].reshape(4, 48, 128).transpose(2, 1, 0))
            m[p + 'conv_b_pc'] = _pc(inp[p + 'conv_b'], 48)
            m[p + 'gnorm_pc'] = _pc(inp[p + 'gnorm'], 32)
            m[p + 'dt_bias_bc'] = _bc(inp[p + 'dt_bias'])
            m[p + 'A_log_bc'] = _bc(inp[p + 'A_log'])
            m[p + 'D_bc'] = _bc(inp[p + 'D'])
            m['trilT'] = np.triu(np.ones((128, 128), np.float32))
            s8 = np.zeros((8, 8, 128), np.float32)
            for h in range(8):
                s8[h, h, :] = 1.0
            m['sel8'] = s8
        if l == 3:
            m[p + 'sgu_wT'] = np.ascontiguousarray(np.transpose(inp[p + 'sgu_w'], (2, 0, 1)))
            m[p + 'sgu_b_bc'] = np.ascontiguousarray(np.broadcast_to(inp[p + 'sgu_b'][None], (128, 8, 128)))
            m[p + 'v_norm_pc'] = _pc(inp[p + 'v_norm'], 32)
            m['trilT'] = np.triu(np.ones((128, 128), np.float32))
    return m


def nsa_host_tables(S):
    NQT, NSEL, NC = S // 128, S // 64, S // 16 - 1
    NCT = (NC + 127) // 128
    t = {}
    inv = (np.float32(10000.0) ** (-np.arange(0, 128, 2, dtype=np.float32) / np.float32(128))).astype(np.float32)
    t['rope_inv_pc'] = np.ascontiguousarray(np.concatenate([inv, inv])[:, None])
    pr = np.zeros((128, 128), np.float32)
    for d in range(64):
        pr[d + 64, d] = -1.0
        pr[d, d + 64] = 1.0
    t['rope_prot'] = pr
    cs = np.arange(NCT * 128)
    cstart, cend = cs * 16, cs * 16 + 31
    sel0 = np.arange(NSEL) * 64
    agg = ((cstart[:, None] < sel0[None, :] + 64) & (cend[:, None] >= sel0[None, :]) & (cs[:, None] < NC)).astype(np.float32)
    t['nsa_agg'] = np.ascontiguousarray(agg.reshape(NCT, 128, NSEL).transpose(1, 0, 2))
    t['nsa_E'] = (np.arange(S)[None, :] // 64 == np.arange(64)[:, None]).astype(np.float32)
    tok = np.arange(S)
    bt = tok // 64
    j = np.arange(NSEL)[None, :]
    causal = (j <= bt[:, None])
    forced = (j == 0) | (j == bt[:, None]) | (j == bt[:, None] - 1)
    bias = np.where(forced, 1000.0, np.where(causal, 0.0, -1.0)).astype(np.float32)
    tabs = np.stack([causal.astype(np.float32), bias], axis=1)
    t['nsa_tabs'] = np.ascontiguousarray(tabs.reshape(NQT, 128, 2, NSEL))
    return t


_PROG = {}


_INPUT_NAMES = (
    'x', 'mem', 'positions', 'mem_norm',
    'l0_norm', 'l0_w_in', 'l0_w_out', 'l0_mem_wkv', 'l0_mem_qnorm', 'l0_mem_knorm', 'l0_pool_w', 'l0_pool_scale',
    'l1_norm', 'l1_w_in', 'l1_w_out', 'l1_mem_wkv', 'l1_mem_qnorm', 'l1_mem_knorm', 'l1_qnorm', 'l1_knorm',
    'l1_cmp_pos', 'l1_cmp_k_w1', 'l1_cmp_k_w2', 'l1_cmp_v_w1', 'l1_cmp_v_w2',
    'l2_norm', 'l2_w_in', 'l2_w_out', 'l2_mem_wkv', 'l2_mem_qnorm', 'l2_mem_knorm', 'l2_conv_w', 'l2_conv_b',
    'l2_dt_bias', 'l2_A_log', 'l2_D', 'l2_gnorm',
    'l3_norm', 'l3_w_in', 'l3_w_out', 'l3_mem_wkv', 'l3_mem_qnorm', 'l3_mem_knorm', 'l3_v_norm', 'l3_sgu_w', 'l3_sgu_b',
)


def kernel(**inputs):
    missing = [n for n in _INPUT_NAMES if n not in inputs]
    assert not missing, missing
    S = inputs['x'].shape[1]
    B = inputs['x'].shape[0]
    layers = [0, 1, 2, 3]
    if S not in _PROG:
        _PROG[S] = build_program(S, layers)
    nc, _es = _PROG[S]
    inp = {k: np.asarray(v) for k, v in inputs.items()}
    in_maps = [make_in_map(inp, b, S, layers) for b in range(B)]
    res = run_bass_kernel_spmd(nc, in_maps, core_ids=list(range(B)))
    out = np.stack([np.asarray(r['out']) for r in res.results], axis=0)
    return out.astype(np.float32)
```

```python
import numpy as np
import ml_dtypes
from contextlib import ExitStack
import concourse.bass as bass
import concourse.mybir as mybir
from concourse.bass_utils import run_bass_kernel_spmd

F32 = mybir.dt.float32
BF16 = mybir.dt.bfloat16
AF = mybir.ActivationFunctionType
ALU = mybir.AluOpType
AX = mybir.AxisListType

D = 2048
KD = D // 128
WTOK = 4096
WMEM = 1024
MEMT = 256
EPS = 1e-6
TT = 512
NDSEM = 6
NSTG = 4
SEM_MAX = 8000

STRICT_SAME_ENGINE = True
DEBUG_OUT = set()
NSA_BR = (0, 1, 2)


class KB:
    def __init__(self, nc, es):
        self.nc = nc
        self.es = es
        self.streams = {e: [] for e in ('pe', 'act', 'dve', 'pool', 'sp')}
        self.esem = {e: es.enter_context(nc.semaphore('s_' + e)) for e in self.streams}
        self.ecount = {e: 0 for e in self.streams}
        self.dsems = {}
        for q in ('sp', 'pool', 'act'):
            self.dsems[q] = [[es.enter_context(nc.semaphore('d_%s%d' % (q, i))), 0] for i in range(NDSEM)]
        self.drr = {q: 0 for q in self.dsems}
        self.last_w = {}
        self.readers = {}
        self.seen = {e: {} for e in self.streams}
        self.tail_tokens = []
        self.nops = 0
        self.retired = []
        self.nsem = 0

    def _deps(self, eng, reads, writes):
        need = {}

        def add(tok):
            if tok is None:
                return
            sem, val, src = tok
            if (not STRICT_SAME_ENGINE or eng == 'pe') and src == eng:
                return
            k = id(sem)
            if self.seen[eng].get(k, 0) >= val:
                return
            if k not in need or need[k][1] < val:
                need[k] = (sem, val)

        for r in reads:
            add(self.last_w.get(r))
            if isinstance(r, tuple) and r[0] == 'pb':
                for t in self.readers.get(r, ()):
                    if t[2] != eng:
                        add(t)
        for w in writes:
            add(self.last_w.get(w))
            for t in self.readers.get(w, ()):
                add(t)
        waits = list(need.values())
        for sem, val in waits:
            self.seen[eng][id(sem)] = val
        return waits

    def _commit(self, tok, reads, writes):
        for r in reads:
            self.readers.setdefault(r, []).append(tok)
        for w in writes:
            self.last_w[w] = tok
            self.readers[w] = []

    def _rotate(self, eng):
        if self.ecount[eng] >= SEM_MAX:
            self.retired.append((self.esem[eng], self.ecount[eng], eng))
            self.nsem += 1
            self.esem[eng] = self.es.enter_context(self.nc.semaphore('s_%s_%d' % (eng, self.nsem)))
            self.ecount[eng] = 0

    def op(self, eng, fn, reads=(), writes=()):
        self._rotate(eng)
        waits = self._deps(eng, reads, writes)
        self.ecount[eng] += 1
        tok = (self.esem[eng], self.ecount[eng], eng)
        self.streams[eng].append((waits, fn, (self.esem[eng], 1)))
        self._commit(tok, reads, writes)
        self.nops += 1
        return tok

    def dma(self, q, out, in_, reads=(), writes=(), final=False):
        slot = self.dsems[q][self.drr[q]]
        self.drr[q] = (self.drr[q] + 1) % NDSEM
        if slot[1] >= SEM_MAX:
            self.retired.append((slot[0], slot[1], 'dma'))
            self.nsem += 1
            slot[0] = self.es.enter_context(self.nc.semaphore('d_%s_%d' % (q, self.nsem)))
            slot[1] = 0
        sem = slot[0]
        waits = self._deps(q, reads, writes)
        if slot[1] > 0 and self.seen[q].get(id(sem), 0) < slot[1]:
            waits.append((sem, slot[1]))
            self.seen[q][id(sem)] = slot[1]
        slot[1] += 16
        tok = (sem, slot[1], 'dma')
        self.streams[q].append((waits, lambda e: e.dma_start(out=out, in_=in_), (sem, 16)))
        self._commit(tok, reads, writes)
        if final:
            self.tail_tokens.append(tok)
        self.nops += 1
        return tok

    def barrier(self):
        toks = list(self.retired)
        for e2 in self.streams:
            if self.ecount[e2] > 0:
                toks.append((self.esem[e2], self.ecount[e2], e2))
        for q in self.dsems:
            for sem, val in self.dsems[q]:
                if val > 0:
                    toks.append((sem, val, 'dma'))
        for eng in self.streams:
            waits = []
            for sem, val, src in toks:
                if src == eng:
                    continue
                if self.seen[eng].get(id(sem), 0) < val:
                    waits.append((sem, val))
                    self.seen[eng][id(sem)] = val
            if waits:
                self._rotate(eng)
                self.ecount[eng] += 1
                self.streams[eng].append((waits, lambda e: e.nop(), (self.esem[eng], 1)))

    def emit(self):
        nc = self.nc
        fm = {}
        for t in self.tail_tokens:
            if id(t[0]) not in fm or fm[id(t[0])][1] < t[1]:
                fm[id(t[0])] = (t[0], t[1])
        fin = list(fm.values())
        engmap = {'pe': 'tensor', 'act': 'scalar', 'dve': 'vector', 'pool': 'gpsimd', 'sp': 'sync'}
        with nc.Block() as block:
            for ename, bname in engmap.items():
                stream = self.streams[ename]
                extra = fin if ename == 'sp' else []

                def body(e, stream=stream, extra=extra):
                    for waits, fn, inc in stream:
                        for sem, val in waits:
                            e.wait_ge(sem, val)
                        ins = fn(e)
                        ins.then_inc(inc[0], inc[1])
                    for sem, val in extra:
                        e.wait_ge(sem, val)
                getattr(block, bname)(body)


class Ctx:
    pass


def build_program(S, layers, first_layer_input='x'):
    nc = bass.Bass("TRN2", target_bir_lowering=False)
    es = ExitStack()
    kb = KB(nc, es)
    NT = S // TT
    c = Ctx()
    c.nc, c.kb, c.es, c.S, c.NT = nc, kb, es, S, NT

    _dins = {}

    def din(name, shape, dt=F32):
        if name not in _dins:
            _dins[name] = nc.dram_tensor(name, list(shape), dt, kind="ExternalInput").ap()
        return _dins[name]

    def dscr(name, shape, dt=F32):
        kind = "ExternalOutput" if name in DEBUG_OUT else "Internal"
        return nc.dram_tensor('scr_' + name, list(shape), dt, kind=kind).ap()

    c.din, c.dscr = din, dscr
    c._dinc = {}

    def din_cached(name, shape, dt=F32):
        if name not in c._dinc:
            c._dinc[name] = din(name, shape, dt)
        return c._dinc[name]
    c.din_cached = din_cached
    x_in = din('x', [S, D])
    mem_in = din('mem', [MEMT, D])
    ident_in = din('ident', [128, 128])
    out = nc.dram_tensor('out', [S, D], F32, kind="ExternalOutput").ap()
    c.mem_in = mem_in
    xbuf = [(dscr('xs0', [S, D]), 'xs0'), (dscr('xs1', [S, D]), 'xs1')]

    def sb(name, shape, dt=F32):
        return es.enter_context(nc.sbuf_tensor('sb_' + name, list(shape), dt))

    def ps(name, shape, dt=F32):
        return es.enter_context(nc.psum_tensor('ps_' + name, list(shape), dt))
    c.sb, c.ps = sb, ps

    c.ident_f = sb('ident_f', [128, 128], F32)
    c.ident = sb('ident', [128, 128], BF16)
    c.ones_bf = sb('ones_bf', [128, 128], BF16)
    c.xt = [sb('xt0', [128, D], F32)] * 2
    c.hn = [sb('hn0', [128, D], BF16)] * 2
    c.nrb = 1
    c.stat = [sb('stat0', [128, 8], F32)] * 2
    c.hT = sb('hT', [128, KD, TT], BF16)
    c.yT = sb('yT', [128, 40, TT], BF16)
    c.wt = [sb('wt%d' % i, [128, 16, 512], BF16) for i in range(2)]
    c.wrr = 0
    c.pbank = [ps('pb%d' % i, [128, 512], F32) for i in range(8)]
    c.prr = 0
    c.kT = sb('kT', [128, 4, 2, MEMT], BF16)
    c.Vm = sb('Vm', [128, 2, WMEM], BF16)
    c.mq = sb('mq', [128, 2, TT], BF16)
    c.mqs = sb('mqs', [128, 2, TT], BF16)
    c.mrs = sb('mrs', [128, TT], F32)
    c.msc = sb('msc', [128, TT], F32)
    c.mP = sb('mP', [128, 2, TT], BF16)
    c.mrd = sb('mrd', [128, TT], F32)
    c.msg = sb('msg', [128, TT], F32)
    c.mo = sb('mo', [128, TT], F32)
    c.xr = [sb('xr%d' % i, [128, 512], F32) for i in range(2)]
    c.xo = [sb('xo%d' % i, [128, 512], F32) for i in range(2)]
    c.xrr = 0
    c.gpc = sb('gpc', [128, 64], F32)
    ARENA = 74 * 1024
    c.arena = sb('arena', [128, ARENA // 4], F32)
    c.aoff = 0
    c.agen = 0

    def areset():
        kb.barrier()
        c.aoff = 0
        c.agen += 1
        c.nrb = 1
        c.xt[1], c.hn[1], c.stat[1] = c.xt[0], c.hn[0], c.stat[0]

    def rows2():
        c.xt[1] = c.aalloc('xt1', [128, D], F32)
        c.hn[1] = c.aalloc('hn1', [128, D], BF16)
        c.stat[1] = c.aalloc('stat1', [128, 8], F32)
        c.nrb = 2
    c.rows2 = rows2

    def aalloc(name, shape, dt=F32):
        esz = 2 if dt == BF16 else 4
        n = int(np.prod(shape[1:]))
        nbytes = (n * esz + 31) // 32 * 32
        assert c.aoff + nbytes <= ARENA, (name, c.aoff, nbytes)
        v = c.arena[0:shape[0], c.aoff // 4:(c.aoff + nbytes) // 4]
        c.aoff += nbytes
        if dt != F32:
            v = v.bitcast(dt)
        v = v[:, 0:n]
        if len(shape) == 3:
            v = v.rearrange("p (a b) -> p a b", a=shape[1])
        elif len(shape) == 4:
            v = v.rearrange("p (a b c) -> p a b c", a=shape[1], b=shape[2])
        return v
    c.areset, c.aalloc = areset, aalloc

    kb.dma('pool', c.ident_f[:], ident_in, writes=['ident_f'])
    kb.op('dve', lambda e: e.tensor_copy(c.ident[:], c.ident_f[:]), reads=['ident_f'], writes=['ident'])
    kb.op('dve', lambda e: e.memset(c.ones_bf[:], 1.0), writes=['ones_bf'])

    cur_in = (x_in, 'x')
    mem_prep_global(c)
    for li, l in enumerate(layers):
        dst = (out, 'out') if li == len(layers) - 1 else xbuf[li % 2]
        LAYER_FNS[l](c, l, cur_in, dst)
        cur_in = dst
    kb.emit()
    return nc, es


def next_bank(c):
    if not hasattr(c, 'rot'):
        c.rot = list(range(8))
    i = c.rot.pop(0)
    c.rot.append(i)
    return c.pbank[i], ('pb', i)


def reserve_bank(c):
    if not hasattr(c, 'rot'):
        c.rot = list(range(8))
    i = c.rot.pop(0)
    return c.pbank[i], ('pb', i), i


def release_bank(c, i):
    c.rot.append(i)


def norm_rows(c, i, src_ap, srckeys):
    kb = c.kb
    i = i % c.nrb
    xt, hn, st = c.xt[i], c.hn[i], c.stat[i]
    kb.dma('pool', xt[:], src_ap, reads=srckeys, writes=[('xt', i)])
    kb.op('act', lambda e: e.activation(hn[:], xt[:], AF.Square, accum_out=st[:, 0:1]),
          reads=[('xt', i)], writes=[('hn', i), ('st', i)])
    kb.op('dve', lambda e: e.tensor_scalar(st[:, 1:2], st[:, 0:1], 1.0 / D, EPS, ALU.mult, ALU.add),
          reads=[('st', i)], writes=[('st', i)])
    kb.op('act', lambda e: e.activation(st[:, 2:3], st[:, 1:2], AF.Sqrt),
          reads=[('st', i)], writes=[('st', i)])
    kb.op('dve', lambda e: e.reciprocal(st[:, 3:4], st[:, 2:3]),
          reads=[('st', i)], writes=[('st', i)])
    kb.op('dve', lambda e: e.tensor_scalar(hn[:], xt[:], st[:, 3:4], None, ALU.mult),
          reads=[('xt', i), ('st', i)], writes=[('hn', i)])


def transpose_rows(c, i, dst, dkey, s):
    kb = c.kb
    i = i % c.nrb
    hn = c.hn[i]
    for g in range(KD // 4):
        pb, pk = next_bank(c)
        pbv = pb[:].bitcast(BF16)
        for j in range(4):
            k = g * 4 + j
            kb.op('pe', lambda e, k=k, j=j, pbv=pbv: e.transpose(
                pbv[:, j * 128:(j + 1) * 128], hn[:, k * 128:(k + 1) * 128], c.ident[:]),
                reads=[('hn', i), 'ident'], writes=[pk])
        src = pbv[:, 0:512].rearrange("p (j t) -> p j t", j=4)
        o = dst[:, g * 4:(g + 1) * 4, s * 128:(s + 1) * 128]
        if g % 2 == 0:
            kb.op('act', lambda e, o=o, src=src: e.copy(o, src), reads=[pk], writes=[dkey])
        else:
            kb.op('dve', lambda e, o=o, src=src: e.tensor_copy(o, src), reads=[pk], writes=[dkey])


def load_norm_transpose(c, x_src, tt):
    for s in range(TT // 128):
        i = (tt * 4 + s) % 2
        r0 = tt * TT + s * 128
        norm_rows(c, i, x_src[0][r0:r0 + 128, :], [('xd', x_src[1], tt)])
        transpose_rows(c, i, c.hT, ('hT', s), s)


def load_pc(c, col0, name, n):
    c.kb.dma('pool', c.gpc[:, col0:col0 + n], c.din(name, [128, n]), writes=['gpc'])
    return c.gpc[:, col0:col0 + n]


def prep_w(c, name, w_ap, K, segs, rowscale=None):
    kb = c.kb
    KC = K // 128
    wv = w_ap.rearrange("(k p) n -> p k n", p=128)
    res = {}
    for (sname, col0, ncols) in segs:
        nb = (ncols + 511) // 512
        scr = c.dscr('w_%s_%s' % (name, sname), [nb, 128, KC, 512], BF16)
        res[sname] = (scr, ncols)
        for b in range(nb):
            n = min(512, ncols - b * 512)
            for k0 in range(0, KC, 2):
                kn = min(2, KC - k0)
                i = c.srr
                c.srr = (c.srr + 1) % NSTG
                stg, stb = c.stg[i], c.stb[i]
                kb.dma('sp', stg[:, 0:kn, 0:n], wv[:, k0:k0 + kn, col0 + b * 512: col0 + b * 512 + n],
                       writes=[('stg', i)])
                if rowscale is None:
                    if i % 2 == 0:
                        kb.op('dve', lambda e, stg=stg, stb=stb, kn=kn, n=n: e.tensor_copy(stb[:, 0:kn, 0:n], stg[:, 0:kn, 0:n]),
                              reads=[('stg', i)], writes=[('stb', i)])
                    else:
                        kb.op('act', lambda e, stg=stg, stb=stb, kn=kn, n=n: e.copy(stb[:, 0:kn, 0:n], stg[:, 0:kn, 0:n]),
                              reads=[('stg', i)], writes=[('stb', i)])
                else:
                    for kk in range(kn):
                        sc = rowscale[:, k0 + kk:k0 + kk + 1]
                        if kk == 0:
                            kb.op('dve', lambda e, stg=stg, stb=stb, kk=kk, n=n, sc=sc: e.tensor_scalar(
                                stb[:, kk, 0:n], stg[:, kk, 0:n], sc, None, ALU.mult),
                                reads=[('stg', i), 'gpc'], writes=[('stb', i)])
                        else:
                            kb.op('act', lambda e, stg=stg, stb=stb, kk=kk, n=n, sc=sc: e.activation(
                                stb[:, kk, 0:n], stg[:, kk, 0:n], AF.Copy, scale=sc),
                                reads=[('stg', i), 'gpc'], writes=[('stb', i)])
                kb.dma('pool', scr[b, :, k0:k0 + kn, 0:n], stb[:, 0:kn, 0:n],
                       reads=[('stb', i)], writes=[('wscr', name, sname, b)])
    return res


def prep_begin(c):
    c.areset()
    c.stg = [c.aalloc('stg%d' % i, [128, 2, 512], F32) for i in range(NSTG)]
    c.stb = [c.aalloc('stb%d' % i, [128, 2, 512], BF16) for i in range(NSTG)]
    c.srr = 0


def load_w_dep(c, src_ap, srckey):
    i = c.wrr
    c.wrr = (c.wrr + 1) % 2
    wt = c.wt[i]
    shp = src_ap.shape
    c.kb.dma('sp', wt[:, 0:shp[1], 0:shp[2]], src_ap, reads=[srckey], writes=[('wt', i)])
    return wt, ('wt', i)


def proj_fm(c, wseg, name, sname, kc=KD, rhs_of=None, ntok=TT, b0=0, nblk=None):
    kb = c.kb
    scr, ncols = wseg[sname]
    nb = (ncols + 511) // 512
    if nblk is None:
        nblk = nb - b0
    if rhs_of is None:
        rhs_of = lambda k: (c.hT[:, k, :], [('hT', s) for s in range(4)])
    for b in range(b0, b0 + nblk):
        n = min(512, ncols - b * 512)
        wt, wk = load_w_dep(c, scr[b, :, 0:kc, 0:n], ('wscr', name, sname, b))
        for sub in range((n + 127) // 128):
            m = min(128, n - sub * 128)
            pb, pk = next_bank(c)
            for k in range(kc):
                rhs, rk = rhs_of(k)
                kb.op('pe', lambda e, pb=pb, wt=wt, k=k, sub=sub, m=m, rhs=rhs: e.matmul(
                    pb[0:m, 0:ntok], wt[:, k, sub * 128: sub * 128 + m], rhs, start=(k == 0), stop=(k == kc - 1)),
                    reads=[wk] + rk, writes=[pk])
            yield (b - b0) * 4 + sub, m, pb, pk


def wout_phase(c, l, wo, x_src, dst, tt, nxt=None):
    kb = c.kb
    scr, _ = wo['all']
    for cb in range(4):
        if nxt is not None:
            r0n = (tt + 1) * TT + cb * 128
            norm_rows(c, ((tt + 1) * 4 + cb) % 2, nxt[0][r0n:r0n + 128, :], [('xd', nxt[1], tt + 1)])
        banks = [next_bank(c) for _ in range(4)]
        for kg, (k0, kn) in enumerate(((0, 16), (16, 16), (32, 8))):
            wt, wk = load_w_dep(c, scr[cb, :, k0:k0 + kn, :], ('wscr', 'wo%d' % l, 'all', cb))
            for s in range(4):
                pb, pk = banks[s]
                for kk in range(kn):
                    f = k0 + kk
                    kb.op('pe', lambda e, pb=pb, wt=wt, kk=kk, f=f, s=s: e.matmul(
                        pb[:, :], c.yT[:, f, s * 128:(s + 1) * 128], wt[:, kk, :], start=(f == 0), stop=(f == 39)),
                        reads=[wk, ('yT', f)], writes=[pk])
        for s in range(4):
            pb, pk = banks[s]
            i = c.xrr
            c.xrr = (c.xrr + 1) % 2
            xr, xo = c.xr[i], c.xo[i]
            r0 = tt * TT + s * 128
            kb.dma('pool', xr[:], x_src[0][r0:r0 + 128, cb * 512:(cb + 1) * 512], reads=[('xd', x_src[1], tt)],
                   writes=[('xr', i)])
            kb.op('dve', lambda e, xo=xo, xr=xr, pb=pb: e.tensor_tensor(xo[:], pb[:, :], xr[:], ALU.add),
                  reads=[pk, ('xr', i)], writes=[('xo', i)])
            kb.dma('pool', dst[0][r0:r0 + 128, cb * 512:(cb + 1) * 512], xo[:], reads=[('xo', i)],
                   writes=[('xd', dst[1], tt)], final=True)
        if nxt is not None:
            transpose_rows(c, ((tt + 1) * 4 + cb) % 2, c.hT, ('hT', cb), cb)


def mem_prep_global(c):
    kb = c.kb
    c.areset()
    memnT = c.aalloc('memnT', [128, KD, MEMT], BF16)
    c.memnT_scr = c.dscr('memnT', [128, KD, MEMT], BF16)
    for s in range(2):
        norm_rows(c, s, c.mem_in[s * 128:(s + 1) * 128, :], [])
        transpose_rows(c, s, memnT, 'memnT', s)
    kb.dma('pool', c.memnT_scr, memnT, reads=['memnT'], writes=['memnT_scr'])


def mem_prep_layer(c, l):
    kb = c.kb
    gm = load_pc(c, 32, 'mem_norm_pc', KD)
    wkv = prep_w(c, 'wkv%d' % l, c.din('l%d_mem_wkv' % l, [D, 2 * WMEM]), D, [('k', 0, WMEM), ('v', WMEM, WMEM)],
                 rowscale=gm)
    memnT = c.aalloc('memnT', [128, KD, MEMT], BF16)
    kraw = c.aalloc('kraw', [128, 2, MEMT], F32)
    ksq = c.aalloc('ksq', [128, 2, MEMT], BF16)
    krs = c.aalloc('krs', [128, MEMT], F32)
    kb.dma('pool', memnT, c.memnT_scr, reads=['memnT_scr'], writes=['memnT'])
    qkg = c.gpc[:, 48:54]
    load_pc(c, 48, 'l%d_mem_qnorm_pc' % l, 2)
    load_pc(c, 50, 'l%d_mem_knorm_pc' % l, 2)
    kb.op('dve', lambda e: e.tensor_tensor(qkg[:, 4:6], qkg[:, 0:2], qkg[:, 2:4], ALU.mult),
          reads=['gpc'], writes=['gpc2'])
    memrhs = lambda k: (memnT[:, k, :], ['memnT'])
    for ci, m, pb, pk in proj_fm(c, wkv, 'wkv%d' % l, 'k', rhs_of=memrhs, ntok=MEMT):
        hd, dc = ci // 2, ci % 2
        kb.op('act', lambda e, pb=pb, dc=dc: e.copy(kraw[:, dc, :], pb[:, 0:MEMT]),
              reads=[pk], writes=[('kraw', dc)])
        kb.op('act', lambda e, pb=pb, dc=dc: e.activation(ksq[:, dc, :], pb[:, 0:MEMT], AF.Square),
              reads=[pk], writes=[('ksq', dc)])
        if dc == 1:
            pb2, pk2 = next_bank(c)
            for d2 in range(2):
                kb.op('pe', lambda e, pb2=pb2, d2=d2: e.matmul(pb2[:, 0:MEMT], c.ones_bf[:], ksq[:, d2, :],
                                                                start=(d2 == 0), stop=(d2 == 1)),
                      reads=['ones_bf', ('ksq', d2)], writes=[pk2])
            kb.op('dve', lambda e, pb2=pb2: e.tensor_scalar(krs, pb2[:, 0:MEMT], 1.0 / 256, EPS, ALU.mult, ALU.add),
                  reads=[pk2], writes=['krs'])
            kb.op('act', lambda e: e.activation(krs, krs, AF.Sqrt), reads=['krs'], writes=['krs'])
            kb.op('dve', lambda e: e.reciprocal(krs, krs), reads=['krs'], writes=['krs'])
            for d2 in range(2):
                kb.op('dve', lambda e, hd=hd, d2=d2: e.scalar_tensor_tensor(
                    c.kT[:, hd, d2, :], kraw[:, d2, :], qkg[:, 4 + d2:5 + d2], krs, ALU.mult, ALU.mult),
                    reads=[('kraw', d2), 'gpc2', 'krs'], writes=['kT'])
    scr, _ = wkv['v']
    for b in range(2):
        wt, wk = load_w_dep(c, scr[b, :, :, :], ('wscr', 'wkv%d' % l, 'v', b))
        for mt in range(2):
            pb, pk = next_bank(c)
            for k in range(KD):
                kb.op('pe', lambda e, pb=pb, wt=wt, k=k, mt=mt: e.matmul(
                    pb[:, :], memnT[:, k, mt * 128:(mt + 1) * 128], wt[:, k, :], start=(k == 0), stop=(k == KD - 1)),
                    reads=[wk, 'memnT'], writes=[pk])
            kb.op('act', lambda e, pb=pb, mt=mt, b=b: e.copy(c.Vm[:, mt, b * 512:(b + 1) * 512], pb[:, :]),
                  reads=[pk], writes=['Vm'])


def mem_branch(c, l, win):
    kb = c.kb
    gate_it = proj_fm(c, win, 'win%d' % l, 'memg')
    q_it = proj_fm(c, win, 'win%d' % l, 'memq')
    G = [(c.msg, 'msg'), (c.mo, 'mo')]

    def gate_chunk(i):
        gci, gm_, gpb, gpk = next(gate_it)
        kb.op('act', lambda e, gpb=gpb, i=i: e.activation(G[i][0][:], gpb[:, :], AF.Silu), reads=[gpk], writes=[G[i][1]])

    for hd in range(4):
        for dc in range(2):
            ci, m, pb, pk = next(q_it)
            kb.op('act', lambda e, pb=pb, dc=dc: e.copy(c.mq[:, dc, :], pb[:, :]), reads=[pk], writes=[('mq', dc)])
            kb.op('act', lambda e, pb=pb, dc=dc: e.activation(c.mqs[:, dc, :], pb[:, :], AF.Square),
                  reads=[pk], writes=[('mqs', dc)])
        gate_chunk(0)
        pb2, pk2 = next_bank(c)
        for d2 in range(2):
            kb.op('pe', lambda e, pb2=pb2, d2=d2: e.matmul(pb2[:, :], c.ones_bf[:], c.mqs[:, d2, :],
                                                            start=(d2 == 0), stop=(d2 == 1)),
                  reads=['ones_bf', ('mqs', d2)], writes=[pk2])
        kb.op('act', lambda e, pb2=pb2: e.activation(c.mrs[:], pb2[:, :], AF.Ln, scale=1.0 / 256, bias=EPS),
              reads=[pk2], writes=['mrs'])
        kb.op('act', lambda e: e.activation(c.mrs[:], c.mrs[:], AF.Exp, scale=-0.5), reads=['mrs'], writes=['mrs'])
        gate_chunk(1)
        for mt in range(2):
            pb3, pk3 = next_bank(c)
            for d2 in range(2):
                kb.op('pe', lambda e, pb3=pb3, d2=d2, mt=mt, hd=hd: e.matmul(
                    pb3[:, :], c.kT[:, hd, d2, mt * 128:(mt + 1) * 128], c.mq[:, d2, :], start=(d2 == 0), stop=(d2 == 1)),
                    reads=['kT', ('mq', d2)], writes=[pk3])
            kb.op('dve', lambda e, pb3=pb3: e.tensor_tensor(c.msc[:], pb3[:, :], c.mrs[:], ALU.mult),
                  reads=[pk3, 'mrs'], writes=['msc'])
            kb.op('act', lambda e, mt=mt: e.activation(c.mP[:, mt, :], c.msc[:], AF.Exp, scale=1.0 / 16),
                  reads=['msc'], writes=[('mP', mt)])
        pb4, pk4 = next_bank(c)
        for mt in range(2):
            kb.op('pe', lambda e, pb4=pb4, mt=mt: e.matmul(pb4[:, :], c.ones_bf[:], c.mP[:, mt, :],
                                                            start=(mt == 0), stop=(mt == 1)),
                  reads=['ones_bf', ('mP', mt)], writes=[pk4])
        pvb = []
        for d2 in range(2):
            pb5, pk5 = next_bank(c)
            for mt in range(2):
                kb.op('pe', lambda e, pb5=pb5, mt=mt, d2=d2, hd=hd: e.matmul(
                    pb5[:, :], c.Vm[:, mt, hd * 256 + d2 * 128: hd * 256 + (d2 + 1) * 128], c.mP[:, mt, :],
                    start=(mt == 0), stop=(mt == 1)),
                    reads=['Vm', ('mP', mt)], writes=[pk5])
            pvb.append((pb5, pk5))
        kb.op('act', lambda e, pb4=pb4: e.activation(c.mrd[:], pb4[:, :], AF.Ln), reads=[pk4], writes=['mrd'])
        kb.op('act', lambda e: e.activation(c.mrd[:], c.mrd[:], AF.Exp, scale=-1.0), reads=['mrd'], writes=['mrd'])
        for d2 in range(2):
            pb5, pk5 = pvb[d2]
            f = hd * 2 + d2
            kb.op('dve', lambda e, pb5=pb5: e.tensor_tensor(c.msc[:], pb5[:, :], c.mrd[:], ALU.mult),
                  reads=[pk5, 'mrd'], writes=['msc'])
            kb.op('dve', lambda e, f=f, d2=d2: e.tensor_tensor(c.yT[:, 32 + f, :], c.msc[:], G[d2][0][:], ALU.mult),
                  reads=['msc', G[d2][1]], writes=[('yT', 32 + f)])


POOL_W = (2, 4, 8, 16)


def layer0(c, l, x_src, dst):
    kb = c.kb
    prep_begin(c)
    gain = load_pc(c, 0, 'l%d_norm_pc' % l, KD)
    win = prep_w(c, 'win%d' % l, c.din('l%d_w_in' % l, [D, 10240]), D,
                 [('v', 0, 4096), ('gate', 4096, 4096), ('memq', 8192, 1024), ('memg', 9216, 1024)], rowscale=gain)
    wo = prep_w(c, 'wo%d' % l, c.din('l%d_w_out' % l, [5120, D]), 5120, [('all', 0, D)])
    pw_in = c.din('l%d_pool_w' % l, [4, 1024, 1024])
    pws = [prep_w(c, 'pw%d_%d' % (l, g), pw_in[g], 1024, [('all', 0, 1024)]) for g in range(4)]
    mem_prep_layer(c, l)
    c.areset()
    c.rows2()
    psc = c.aalloc('pool_sc', [128, 32], F32)
    kb.dma('pool', psc, c.din('l%d_pool_scale_pc' % l, [128, 32]), writes=['psc'])
    rc0 = c.aalloc('pool_rc0', [128, 4, TT], F32)
    kb.dma('pool', rc0, c.din('pool_rc0', [128, 4, TT]), writes=['rc0'])
    carry = c.aalloc('pool_carry', [128, 32, 16], F32)
    kb.op('dve', lambda e: e.memset(carry, 0.0), writes=['carry'])
    va = c.aalloc('pool_va', [128, 16 + TT], F32)
    vb = c.aalloc('pool_vb', [128, 16 + TT], F32)
    vc = c.aalloc('pool_vc', [128, 16 + TT], F32)
    mixT = c.aalloc('pool_mixT', [128, 8, TT], BF16)
    sg = c.aalloc('pool_sg', [128, TT], F32)
    for tt in range(c.NT):
        if tt == 0:
            load_norm_transpose(c, x_src, tt)
        for g in range(4):
            w = POOL_W[g]
            for ci, m, pb, pk in proj_fm(c, win, 'win%d' % l, 'v', b0=2 * g, nblk=2):
                ch = g * 8 + ci
                kb.op('act', lambda e, pb=pb: e.copy(va[:, 16:16 + TT], pb[:, :]), reads=[pk], writes=['va'])
                kb.op('pool', lambda e, ch=ch: e.tensor_copy(va[:, 0:16], carry[:, ch, :]),
                      reads=['carry'], writes=['va'])
                kb.op('pool', lambda e, ch=ch: e.tensor_copy(carry[:, ch, :], va[:, TT:TT + 16]),
                      reads=['va'], writes=['carry'])
                src, srck = va, 'va'
                bufs = [(vb, 'vb'), (vc, 'vc')]
                sh = 1
                bi = 0
                while sh < w:
                    dstb, dk = bufs[bi]
                    bi ^= 1
                    kb.op('dve', lambda e, src=src, dstb=dstb, sh=sh: e.tensor_tensor(
                        dstb[:, sh:16 + TT], src[:, sh:16 + TT], src[:, 0:16 + TT - sh], ALU.add),
                        reads=[srck], writes=[dk])
                    src, srck = dstb, dk
                    sh *= 2
                if tt == 0:
                    dstb, dk = bufs[bi]
                    kb.op('dve', lambda e, src=src, dstb=dstb, g=g: e.tensor_tensor(
                        dstb[:, 16:16 + TT], src[:, 16:16 + TT], rc0[:, g, :], ALU.mult),
                        reads=[srck, 'rc0'], writes=[dk])
                    kb.op('dve', lambda e, dstb=dstb, ci=ci: e.tensor_tensor(
                        mixT[:, ci, :], dstb[:, 16:16 + TT], va[:, 16:16 + TT], ALU.subtract),
                        reads=[dk, 'va'], writes=[('mixT', ci)])
                else:
                    kb.op('dve', lambda e, src=src, ci=ci, w=w: e.scalar_tensor_tensor(
                        mixT[:, ci, :], src[:, 16:16 + TT], 1.0 / w, va[:, 16:16 + TT], ALU.mult, ALU.subtract),
                        reads=[srck, 'va'], writes=[('mixT', ci)])
            gate_it = proj_fm(c, win, 'win%d' % l, 'gate', b0=2 * g, nblk=2)
            mixrhs = lambda k: (mixT[:, k, :], [('mixT', k)])
            for ci, m, pb, pk in proj_fm(c, pws[g], 'pw%d_%d' % (l, g), 'all', kc=8, rhs_of=mixrhs):
                gci, gm, gpb, gpk = next(gate_it)
                ch = g * 8 + ci
                kb.op('act', lambda e, gpb=gpb: e.activation(sg, gpb[:, :], AF.Silu), reads=[gpk], writes=['sg'])
                kb.op('dve', lambda e, pb=pb, ch=ch: e.scalar_tensor_tensor(
                    c.yT[:, ch, :], pb[:, :], psc[:, ch:ch + 1], sg, ALU.mult, ALU.mult),
                    reads=[pk, 'psc', 'sg'], writes=[('yT', ch)])
        mem_branch(c, l, win)
        wout_phase(c, l, wo, x_src, dst, tt, nxt=(x_src if tt + 1 < c.NT else None))


def layer3(c, l, x_src, dst):
    kb = c.kb
    prep_begin(c)
    gain = load_pc(c, 0, 'l%d_norm_pc' % l, KD)
    win = prep_w(c, 'win%d' % l, c.din('l%d_w_in' % l, [D, 14336]), D,
                 [('u', 0, 4096), ('v', 4096, 4096), ('gate', 8192, 4096), ('memq', 12288, 1024),
                  ('memg', 13312, 1024)], rowscale=gain)
    wo = prep_w(c, 'wo%d' % l, c.din('l%d_w_out' % l, [5120, D]), 5120, [('all', 0, D)])
    mem_prep_layer(c, l)
    c.areset()
    c.rows2()
    vg = c.aalloc('sgu_vg', [128, 4, 4096], BF16)
    WTf = c.aalloc('sgu_WTf', [128, 8, 128], F32)
    WT = c.aalloc('sgu_WT', [128, 8, 128], BF16)
    tril = c.aalloc('sgu_tril', [128, 128], F32)
    bbc = c.aalloc('sgu_bbc', [128, 8, 128], F32)
    vng = c.aalloc('sgu_vng', [128, 32], F32)
    ssq = c.aalloc('sgu_ssq', [128, 4, 8], F32)
    rst = c.aalloc('sgu_rst', [128, 8], F32)
    t1 = c.aalloc('sgu_t1', [128, TT], F32)
    gu = c.aalloc('sgu_gu', [128, TT], F32)
    sg = c.aalloc('sgu_sg', [128, TT], F32)
    kb.dma('pool', WTf, c.din('l%d_sgu_wT' % l, [128, 8, 128]), writes=['WTf'])
    kb.dma('pool', tril, c.din('trilT', [128, 128]), writes=['tril'])
    kb.dma('pool', bbc, c.din('l%d_sgu_b_bc' % l, [128, 8, 128]), writes=['bbc'])
    kb.dma('pool', vng, c.din('l%d_v_norm_pc' % l, [128, 32]), writes=['vng'])
    for g in range(8):
        kb.op('dve', lambda e, g=g: e.tensor_tensor(WT[:, g, :], WTf[:, g, :], tril, ALU.mult),
              reads=['WTf', 'tril'], writes=['WT'])
    scr_v, _ = win['v']
    for tt in range(c.NT):
        if tt == 0:
            load_norm_transpose(c, x_src, tt)
        for b in range(8):
            wt, wk = load_w_dep(c, scr_v[b, :, :, :], ('wscr', 'win%d' % l, 'v', b))
            for s in range(4):
                pb, pk = next_bank(c)
                for k in range(KD):
                    kb.op('pe', lambda e, pb=pb, wt=wt, k=k, s=s: e.matmul(
                        pb[:, :], c.hT[:, k, s * 128:(s + 1) * 128], wt[:, k, :], start=(k == 0), stop=(k == KD - 1)),
                        reads=[wk, ('hT', s)], writes=[pk])
                kb.op('act', lambda e, pb=pb, s=s, b=b: e.activation(vg[:, s, b * 512:(b + 1) * 512], pb[:, :], AF.Gelu),
                      reads=[pk], writes=[('vg', s)])
                kb.op('act', lambda e, s=s, b=b: e.activation(t1, vg[:, s, b * 512:(b + 1) * 512], AF.Square,
                                                              accum_out=ssq[:, s, b:b + 1]),
                      reads=[('vg', s)], writes=['t1', ('ssq', s)])
        for s in range(4):
            kb.op('dve', lambda e, s=s: e.tensor_reduce(rst[:, s:s + 1], ssq[:, s, :], AX.X, ALU.add),
                  reads=[('ssq', s)], writes=[('rst', s)])
            kb.op('dve', lambda e, s=s: e.tensor_scalar(rst[:, s:s + 1], rst[:, s:s + 1], 1.0 / 4096, EPS, ALU.mult, ALU.add),
                  reads=[('rst', s)], writes=[('rst', s)])
            kb.op('act', lambda e, s=s: e.activation(rst[:, s:s + 1], rst[:, s:s + 1], AF.Sqrt),
                  reads=[('rst', s)], writes=[('rst', s)])
            kb.op('dve', lambda e, s=s: e.reciprocal(rst[:, s:s + 1], rst[:, s:s + 1]),
                  reads=[('rst', s)], writes=[('rst', s)])
            kb.op('dve', lambda e, s=s: e.tensor_scalar(vg[:, s, :], vg[:, s, :], rst[:, s:s + 1], None, ALU.mult),
                  reads=[('rst', s), ('vg', s)], writes=[('vg', s)])
        u_it = proj_fm(c, win, 'win%d' % l, 'u')
        g_it = proj_fm(c, win, 'win%d' % l, 'gate')
        for fc in range(32):
            g = fc // 4
            _, _, upb, upk = next(u_it)
            kb.op('act', lambda e, upb=upb: e.activation(gu, upb[:, :], AF.Gelu), reads=[upk], writes=['gu'])
            _, _, gpb, gpk = next(g_it)
            kb.op('act', lambda e, gpb=gpb: e.activation(sg, gpb[:, :], AF.Silu), reads=[gpk], writes=['sg'])
            pb, pk = next_bank(c)
            for s in range(4):
                kb.op('pe', lambda e, pb=pb, s=s, fc=fc, g=g: e.matmul(
                    pb[:, s * 128:(s + 1) * 128], vg[:, s, fc * 128:(fc + 1) * 128], WT[:, g, :], start=True, stop=True),
                    reads=[('vg', s), 'WT'], writes=[pk])
            for s in range(4):
                kb.op('dve', lambda e, pb=pb, s=s, fc=fc, g=g: e.scalar_tensor_tensor(
                    t1[:, s * 128:(s + 1) * 128], pb[:, s * 128:(s + 1) * 128], vng[:, fc:fc + 1], bbc[:, g, :],
                    ALU.mult, ALU.add), reads=[pk, 'vng', 'bbc'], writes=['t1'])
            kb.op('pool', lambda e: e.tensor_tensor(gu, gu, sg, ALU.mult), reads=['gu', 'sg'], writes=['gu'])
            kb.op('dve', lambda e, fc=fc: e.tensor_tensor(c.yT[:, fc, :], t1, gu, ALU.mult),
                  reads=['t1', 'gu'], writes=[('yT', fc)])
        mem_branch(c, l, win)
        wout_phase(c, l, wo, x_src, dst, tt, nxt=(x_src if tt + 1 < c.NT else None))


def layer2(c, l, x_src, dst):
    kb = c.kb
    HT = 256
    prep_begin(c)
    gain = load_pc(c, 0, 'l%d_norm_pc' % l, KD)
    win = prep_w(c, 'win%d' % l, c.din('l%d_w_in' % l, [D, 12352]), D,
                 [('z', 0, 4096), ('xbc', 4096, 6144), ('dt', 10240, 64), ('memq', 10304, 1024),
                  ('memg', 11328, 1024)], rowscale=gain)
    wo = prep_w(c, 'wo%d' % l, c.din('l%d_w_out' % l, [5120, D]), 5120, [('all', 0, D)])
    mem_prep_layer(c, l)
    c.areset()
    A = c.aalloc
    xtok = A('xtok', [128, 2, 4096], BF16)
    BT = A('BT', [128, 8, HT], BF16)
    CT = A('CT', [128, 8, HT], BF16)
    Btok = A('Btok', [128, 2, 1024], BF16)
    state = A('state', [128, 4096], F32)
    stbf = A('stbf', [128, 512], BF16)
    xw = A('xw', [128, 512], BF16)
    cvb = [A('cv%d' % i, [128, 3 + HT], F32) for i in range(2)]
    acc = [A('acc%d' % i, [128, HT], F32) for i in range(2)]
    xss = [A('xs%d' % i, [128, HT], BF16) for i in range(3)]
    carry = A('carry', [128, 48, 3], F32)
    cw = A('cw', [128, 48, 4], F32)
    cb = A('cb', [128, 48], F32)
    gn = A('gn', [128, 32], F32)
    wdt = A('wdt', [128, KD, 64], BF16)
    U = A('U', [128, 128], F32)
    onesf = A('onesf', [128, 128], F32)
    tab = A('tab', [128, 4, 64], F32)
    dtt = A('dtt', [128, 2, 64], F32)
    dtA = A('dtA', [128, 2, 64], F32)
    atok = A('atok', [128, 2, 64], F32)
    sel8 = A('sel8', [8, 8, 128], F32)
    aTg = A('aTg', [8, 128], F32)
    eAe = A('eAe', [128, 2, 64], F32)
    wtok = A('wtok', [128, 2, 64], F32)
    Gm = A('Gm', [128, 128], F32)
    szg = A('szg', [128, 4, HT], BF16)
    ty = A('ty', [128, 128], F32)
    ysq = A('ysq', [128, 128], BF16)
    rbc = A('rbc', [128, HT], F32)
    kb.dma('pool', cw, c.din('l%d_conv_w_pc' % l, [128, 48, 4]), writes=['cw'])
    kb.dma('pool', cb, c.din('l%d_conv_b_pc' % l, [128, 48]), writes=['cb'])
    kb.dma('pool', gn, c.din('l%d_gnorm_pc' % l, [128, 32]), writes=['gn'])
    kb.dma('pool', U, c.din('trilT', [128, 128]), writes=['U'])
    kb.dma('pool', sel8, c.din('sel8', [8, 8, 128]), writes=['sel8'])
    kb.dma('pool', tab[:, 0, :], c.din('l%d_dt_bias_bc' % l, [128, 64]), writes=['tab'])
    kb.dma('pool', tab[:, 1, :], c.din('l%d_A_log_bc' % l, [128, 64]), writes=['tab'])
    kb.dma('pool', tab[:, 2, :], c.din('l%d_D_bc' % l, [128, 64]), writes=['tab'])
    kb.op('act', lambda e: e.activation(tab[:, 3, :], tab[:, 1, :], AF.Exp), reads=['tab'], writes=['tab3'])
    kb.op('dve', lambda e: e.tensor_scalar(tab[:, 1, :], tab[:, 3, :], -1.0, None, ALU.mult), reads=['tab3', 'tab'], writes=['tab'])
    kb.op('dve', lambda e: e.memset(onesf, 1.0), writes=['onesf'])
    kb.op('dve', lambda e: e.memset(state, 0.0), writes=['state'])
    kb.op('dve', lambda e: e.memset(carry, 0.0), writes=['carry'])
    kb.dma('sp', wdt, win['dt'][0][0, :, :, 0:64], reads=[('wscr', 'win%d' % l, 'dt', 0)], writes=['wdt'])
    hkeys = [('hT', s) for s in range(4)]
    for tt in range(c.NT):
        if tt == 0:
            load_norm_transpose(c, x_src, tt)
        for half in range(2):
            t0 = half * HT
            rhs_half = lambda k, t0=t0: (c.hT[:, k, t0:t0 + HT], hkeys)
            pend_tr = []
            for ci, m, pb, pk in proj_fm(c, win, 'win%d' % l, 'xbc', rhs_of=rhs_half, ntok=HT):
                i = ci % 2
                cv, ac = cvb[i], acc[i]
                kb.op('pool', lambda e, cv=cv, ci=ci: e.tensor_copy(cv[:, 0:3], carry[:, ci, :]),
                      reads=['carry'], writes=[('cv', i)])
                kb.op('act', lambda e, cv=cv, pb=pb: e.copy(cv[:, 3:3 + HT], pb[:, 0:HT]), reads=[pk], writes=[('cv', i)])
                kb.op('pool', lambda e, cv=cv, ci=ci: e.tensor_copy(carry[:, ci, :], cv[:, HT:HT + 3]),
                      reads=[('cv', i)], writes=['carry'])
                kb.op('dve', lambda e, cv=cv, ac=ac, ci=ci: e.tensor_scalar(
                    ac, cv[:, 3:3 + HT], cw[:, ci, 3:4], cb[:, ci:ci + 1], ALU.mult, ALU.add),
                    reads=[('cv', i), 'cw', 'cb'], writes=[('acc', i)])
                for j in range(3):
                    kb.op('dve', lambda e, cv=cv, ac=ac, ci=ci, j=j: e.scalar_tensor_tensor(
                        ac, cv[:, j:j + HT], cw[:, ci, j:j + 1], ac, ALU.mult, ALU.add),
                        reads=[('cv', i), 'cw', ('acc', i)], writes=[('acc', i)])
                if ci < 32:
                    o, ok = xss[ci % 3], ('xs', ci % 3)
                elif ci < 40:
                    o, ok = BT[:, ci - 32, :], 'BT'
                else:
                    o, ok = CT[:, ci - 40, :], 'CT'
                kb.op('act', lambda e, o=o, ac=ac: e.activation(o, ac, AF.Silu), reads=[('acc', i)], writes=[ok])
                if ci < 40:
                    pend_tr.append((ci, o, ok))
                while pend_tr and (pend_tr[0][0] <= ci - 2 or ci == 47):
                    ci_t, o, ok = pend_tr.pop(0)
                    tb, tk = next_bank(c)
                    tbv = tb[:].bitcast(BF16)
                    for s2 in range(2):
                        kb.op('pe', lambda e, o=o, s2=s2, tbv=tbv: e.transpose(
                            tbv[:, s2 * 128:(s2 + 1) * 128], o[:, s2 * 128:(s2 + 1) * 128], c.ident[:]),
                            reads=[ok, 'ident'], writes=[tk])
                    srcv = tbv[:, 0:256].rearrange("p (s f) -> p s f", s=2)
                    if ci_t < 32:
                        kb.op('dve', lambda e, srcv=srcv, ci_t=ci_t: e.tensor_copy(xtok[:, :, ci_t * 128:(ci_t + 1) * 128], srcv),
                              reads=[tk], writes=['xtok'])
                    else:
                        g = ci_t - 32
                        kb.op('dve', lambda e, srcv=srcv, g=g: e.tensor_copy(Btok[:, :, g * 128:(g + 1) * 128], srcv),
                              reads=[tk], writes=['Btok'])
            for s2 in range(2):
                pb, pk = next_bank(c)
                for k in range(KD):
                    kb.op('pe', lambda e, pb=pb, k=k, s2=s2, t0=t0: e.matmul(
                        pb[:, 0:64], c.hT[:, k, t0 + s2 * 128:t0 + (s2 + 1) * 128], wdt[:, k, :],
                        start=(k == 0), stop=(k == KD - 1)), reads=['wdt'] + hkeys, writes=[pk])
                kb.op('dve', lambda e, pb=pb, s2=s2: e.tensor_tensor(dtt[:, s2, :], pb[:, 0:64], tab[:, 0, :], ALU.add),
                      reads=[pk, 'tab'], writes=[('dtt', s2)])
                kb.op('act', lambda e, s2=s2: e.activation(dtt[:, s2, :], dtt[:, s2, :], AF.Exp),
                      reads=[('dtt', s2)], writes=[('dtt', s2)])
                kb.op('act', lambda e, s2=s2: e.activation(dtt[:, s2, :], dtt[:, s2, :], AF.Ln, bias=1.0),
                      reads=[('dtt', s2)], writes=[('dtt', s2)])
                kb.op('dve', lambda e, s2=s2: e.tensor_tensor(dtA[:, s2, :], dtt[:, s2, :], tab[:, 1, :], ALU.mult),
                      reads=[('dtt', s2), 'tab'], writes=[('dtA', s2)])
                pb1, pk1 = next_bank(c)
                kb.op('pe', lambda e, pb1=pb1, s2=s2: e.matmul(pb1[:, 0:64], U, dtA[:, s2, :], start=True, stop=True),
                      reads=['U', ('dtA', s2)], writes=[pk1])
                kb.op('act', lambda e, pb1=pb1, s2=s2: e.copy(atok[:, s2, :], pb1[:, 0:64]), reads=[pk1], writes=[('atok', s2)])
                pb3, pk3 = next_bank(c)
                kb.op('pe', lambda e, pb3=pb3, s2=s2: e.matmul(pb3[:, 0:64], onesf, dtA[:, s2, :], start=True, stop=True),
                      reads=['onesf', ('dtA', s2)], writes=[pk3])
                kb.op('act', lambda e, pb3=pb3, s2=s2: e.activation(eAe[:, s2, :], pb3[:, 0:64], AF.Exp),
                      reads=[pk3], writes=[('eAe', s2)])
                kb.op('dve', lambda e, pb3=pb3, s2=s2: e.tensor_tensor(wtok[:, s2, :], pb3[:, 0:64], atok[:, s2, :], ALU.subtract),
                      reads=[pk3, ('atok', s2)], writes=[('wtok', s2)])
                kb.op('act', lambda e, s2=s2: e.activation(wtok[:, s2, :], wtok[:, s2, :], AF.Exp),
                      reads=[('wtok', s2)], writes=[('wtok', s2)])
                kb.op('dve', lambda e, s2=s2: e.tensor_tensor(wtok[:, s2, :], wtok[:, s2, :], dtt[:, s2, :], ALU.mult),
                      reads=[('wtok', s2), ('dtt', s2)], writes=[('wtok', s2)])
            sres = [reserve_bank(c), reserve_bank(c)]
            z_it = proj_fm(c, win, 'win%d' % l, 'z', rhs_of=rhs_half, ntok=HT)
            tmpF = [(c.msc, 'msc'), (c.mrs, 'mrs')]
            DIb = [(c.mrd, 'mrd'), (c.msg, 'msg')]
            E4 = [(c.mq[:, j, :], ('mq', j)) for j in range(2)]
            M4 = [(c.mqs[:, j, :], ('mqs', j)) for j in range(2)]
            C4 = [(c.mP[:, j, :], ('mP', j)) for j in range(2)]
            v4 = lambda ap: ap.rearrange("p (h l) -> p h l", h=4)
            for g in range(8):
                for q4 in range(4):
                    _, _, zpb, zpk = next(z_it)
                    kb.op('act', lambda e, zpb=zpb, q4=q4: e.activation(szg[:, q4, :], zpb[:, 0:HT], AF.Silu),
                          reads=[zpk], writes=[('szg', q4)])
                for j in range(2):
                    h0 = g * 8 + j * 4
                    kb.op('dve', lambda e, j=j, h0=h0: e.tensor_tensor(
                        v4(DIb[j][0][:, :]), c.ident_f[:].unsqueeze(1).to_broadcast([128, 4, 128]),
                        tab[:, 2, h0:h0 + 4].unsqueeze(2).to_broadcast([128, 4, 128]), ALU.mult),
                        reads=['ident_f', 'tab'], writes=[DIb[j][1]])
                for s2 in range(2):
                    tc = slice(s2 * 128, (s2 + 1) * 128)
                    gb, gk = next_bank(c)
                    kb.op('pe', lambda e, gb=gb, g=g, tc=tc: e.matmul(gb[:, 0:128], BT[:, g, tc], CT[:, g, tc], start=True, stop=True),
                          reads=['BT', 'CT'], writes=[gk])
                    kb.op('dve', lambda e, gb=gb: e.tensor_tensor(Gm, gb[:, 0:128], U, ALU.mult), reads=[gk, 'U'], writes=['Gm'])
                    kb.op('pool', lambda e, g=g: e.tensor_copy(stbf, state[:, g * 512:(g + 1) * 512]),
                          reads=[('state', g)], writes=['stbf'])
                    atb, atk = next_bank(c)
                    kb.op('pe', lambda e, atb=atb, g=g, s2=s2: e.matmul(atb[0:8, 0:128], dtA[:, s2, g * 8:(g + 1) * 8], U, start=True, stop=True),
                          reads=['U', ('dtA', s2)], writes=[atk])
                    kb.op('act', lambda e, atb=atb: e.copy(aTg, atb[0:8, 0:128]), reads=[atk], writes=['aTg'])
                    abk = []
                    for j in range(2):
                        ab, ak = next_bank(c)
                        for hl in range(4):
                            kb.op('pe', lambda e, ab=ab, hl=hl, j=j: e.matmul(ab[:, hl * 128:(hl + 1) * 128], sel8[:, j * 4 + hl, :], aTg,
                                                                              start=True, stop=True),
                                  reads=['sel8', 'aTg'], writes=[ak])
                        abk.append((ab, ak))
                    for j in range(2):
                        h0 = g * 8 + j * 4
                        ab, ak = abk[j]
                        F, Fk = tmpF[j]
                        kb.op('dve', lambda e, ab=ab, F=F, h0=h0, s2=s2: e.tensor_tensor(
                            v4(F[:, :]), v4(ab[:, :]), atok[:, s2, h0:h0 + 4].unsqueeze(2).to_broadcast([128, 4, 128]), ALU.subtract),
                            reads=[ak, ('atok', s2)], writes=[Fk])
                        kb.op('dve', lambda e, F=F: e.tensor_scalar_min(F[:, :], F[:, :], 0.0), reads=[Fk], writes=[Fk])
                        kb.op('act', lambda e, F=F, j=j: e.activation(E4[j][0], F[:, :], AF.Exp), reads=[Fk], writes=[E4[j][1]])
                        kb.op('act', lambda e, ab=ab, j=j: e.activation(C4[j][0], ab[:, :], AF.Exp), reads=[ak], writes=[C4[j][1]])
                        kb.op('dve', lambda e, F=F, h0=h0, s2=s2: e.tensor_tensor(
                            v4(F[:, :]), Gm.unsqueeze(1).to_broadcast([128, 4, 128]),
                            dtt[:, s2, h0:h0 + 4].unsqueeze(2).to_broadcast([128, 4, 128]), ALU.mult),
                            reads=['Gm', ('dtt', s2), E4[j][1]], writes=[Fk])
                        kb.op('dve', lambda e, F=F, j=j: e.tensor_tensor(F[:, :], F[:, :], DIb[j][0][:, :], ALU.add),
                              reads=[Fk, DIb[j][1]], writes=[Fk])
                        kb.op('dve', lambda e, F=F, j=j: e.tensor_tensor(M4[j][0], E4[j][0], F[:, :], ALU.mult),
                              reads=[Fk, E4[j][1]], writes=[M4[j][1]])
                        kb.op('dve', lambda e, j=j, g=g, tc=tc: e.tensor_tensor(
                            v4(C4[j][0]), v4(C4[j][0]), CT[:, g, tc].unsqueeze(1).to_broadcast([128, 4, 128]), ALU.mult),
                            reads=['CT', C4[j][1]], writes=[C4[j][1]])
                    ybank = [None, None]
                    for hh in range(8):
                        h = g * 8 + hh
                        j, hl = hh // 4, hh % 4
                        yb, yk = next_bank(c)
                        ybank[hh % 2] = (yb, yk)
                        pc = h // 2
                        kb.op('pe', lambda e, yb=yb, s2=s2, pc=pc, j=j, hl=hl: e.matmul(
                            yb[:, 0:128], xtok[:, s2, pc * 128:(pc + 1) * 128], M4[j][0][:, hl * 128:(hl + 1) * 128], start=True, stop=False),
                            reads=['xtok', M4[j][1]], writes=[yk])
                        pl = hh // 2
                        kb.op('pe', lambda e, yb=yb, pl=pl, j=j, hl=hl: e.matmul(
                            yb[:, 0:128], stbf[:, pl * 128:(pl + 1) * 128], C4[j][0][:, hl * 128:(hl + 1) * 128], start=False, stop=True),
                            reads=['stbf', C4[j][1]], writes=[yk])
                        if hh % 2 == 1:
                            (ya, yak), (yb2, ybk) = ybank
                            kb.op('dve', lambda e, ya=ya, pl=pl, tc=tc: e.tensor_tensor(
                                ty[0:64, :], ya[0:64, 0:128], szg[0:64, pl, tc], ALU.mult),
                                reads=[yak, ('szg', pl)], writes=['ty'])
                            kb.op('dve', lambda e, yb2=yb2, pl=pl, tc=tc: e.tensor_tensor(
                                ty[64:128, :], yb2[64:128, 0:128], szg[64:128, pl, tc], ALU.mult),
                                reads=[ybk, ('szg', pl)], writes=['ty'])
                            col = slice(t0 + s2 * 128, t0 + (s2 + 1) * 128)
                            kb.op('act', lambda e, pc=pc, col=col: e.copy(c.yT[:, pc, col], ty), reads=['ty'], writes=[('yT', pc)])
                            kb.op('act', lambda e: e.activation(ysq, ty, AF.Square), reads=['ty'], writes=['ysq'])
                            sb_, sk_, _ = sres[s2]
                            kb.op('pe', lambda e, sb_=sb_, pc=pc: e.matmul(sb_[:, 0:128], c.ones_bf[:], ysq, start=(pc == 0), stop=(pc == 31)),
                                  reads=['ones_bf', 'ysq'], writes=[sk_])
                    kb.op('dve', lambda e, g=g, s2=s2: e.tensor_tensor(
                        xw.rearrange("p (h j) -> p h j", h=8), xtok[:, s2, g * 512:(g + 1) * 512].rearrange("p (h j) -> p h j", h=8),
                        wtok[:, s2, g * 8:(g + 1) * 8].unsqueeze(2).to_broadcast([128, 8, 64]), ALU.mult),
                        reads=['xtok', ('wtok', s2)], writes=['xw'])
                    nb_, nk_ = next_bank(c)
                    kb.op('pe', lambda e, nb_=nb_, s2=s2, g=g: e.matmul(nb_[:, :], Btok[:, s2, g * 128:(g + 1) * 128], xw, start=True, stop=True),
                          reads=['Btok', 'xw'], writes=[nk_])
                    for hh in range(8):
                        h = g * 8 + hh
                        kb.op('dve', lambda e, nb_=nb_, hh=hh, h=h, s2=s2: e.scalar_tensor_tensor(
                            state[:, h * 64:(h + 1) * 64], state[:, h * 64:(h + 1) * 64], eAe[:, s2, h:h + 1],
                            nb_[:, hh * 64:(hh + 1) * 64], ALU.mult, ALU.add),
                            reads=[nk_, ('eAe', s2), ('state', g)], writes=[('state', g)])
            for s2 in range(2):
                sb_, sk_, sidx = sres[s2]
                kb.op('dve', lambda e, sb_=sb_, s2=s2: e.tensor_scalar(rbc[:, s2 * 128:(s2 + 1) * 128], sb_[:, 0:128], 1.0 / 4096, EPS,
                                                                       ALU.mult, ALU.add), reads=[sk_], writes=['rbc'])
                release_bank(c, sidx)
            kb.op('act', lambda e: e.activation(rbc, rbc, AF.Sqrt), reads=['rbc'], writes=['rbc'])
            kb.op('dve', lambda e: e.reciprocal(rbc, rbc), reads=['rbc'], writes=['rbc'])
            for pc in range(32):
                kb.op('dve', lambda e, pc=pc, t0=t0: e.scalar_tensor_tensor(
                    c.yT[:, pc, t0:t0 + HT], c.yT[:, pc, t0:t0 + HT], gn[:, pc:pc + 1], rbc, ALU.mult, ALU.mult),
                    reads=[('yT', pc), 'gn', 'rbc'], writes=[('yT', pc)])
        mem_branch(c, l, win)
        wout_phase(c, l, wo, x_src, dst, tt, nxt=(x_src if tt + 1 < c.NT else None))


I32 = mybir.dt.int32
NSA_SCALE = 128 ** -0.5


def layer1(c, l, x_src, dst):
    kb = c.kb
    S = c.S
    NQT = S // 128
    NSEL = S // 64
    NC = S // 16 - 1
    NCT = (NC + 127) // 128
    NCP = NCT * 128
    prep_begin(c)
    gain = load_pc(c, 0, 'l%d_norm_pc' % l, KD)
    segs = [('q', 0, 4096), ('kc', 4096, 512), ('vc', 4608, 512), ('ks', 5120, 512), ('vs', 5632, 512),
            ('kw', 6144, 512), ('vw', 6656, 512), ('gl', 7168, 96), ('gate', 7264, 4096),
            ('memq', 11360, 1024), ('memg', 12384, 1024)]
    wname = 'win%d' % l
    win = prep_w(c, wname, c.din('l%d_w_in' % l, [D, 13408]), D, segs, rowscale=gain)
    wo = prep_w(c, 'wo%d' % l, c.din('l%d_w_out' % l, [5120, D]), 5120, [('all', 0, D)])
    w1k = prep_w(c, 'w1k', c.din('l%d_cmp_k_w1' % l, [4096, 128]), 4096, [('all', 0, 128)])
    w1v = prep_w(c, 'w1v', c.din('l%d_cmp_v_w1' % l, [4096, 128]), 4096, [('all', 0, 128)])
    w2k = prep_w(c, 'w2k', c.din('l%d_cmp_k_w2' % l, [128, 128]), 128, [('all', 0, 128)])
    w2v = prep_w(c, 'w2v', c.din('l%d_cmp_v_w2' % l, [128, 128]), 128, [('all', 0, 128)])
    mem_prep_layer(c, l)
    QT = c.dscr('nsa_QT', [32, 128, S], BF16)
    KT = c.dscr('nsa_KT', [3, 4, 128, S], BF16)
    VcT = c.dscr('nsa_VcT', [4, 128, S], BF16)
    Vsw = c.dscr('nsa_Vsw', [2, S, 512], BF16)
    Gs = c.dscr('nsa_Gs', [S, 96], F32)
    SG = c.dscr('nsa_SG', [32, 128, S], BF16)
    Y = c.dscr('nsa_Y', [32, 128, S], BF16)
    cosD = c.dscr('nsa_cos', [128, S], F32)
    sinD = c.dscr('nsa_sin', [128, S], F32)

    c.areset()
    A = c.aalloc
    PW = min(1024, S)
    pi = A('pi', [128, PW], I32)
    ang = A('ang', [128, PW], F32)
    uu = A('uu', [128, PW], F32)
    nn_i = A('nn_i', [128, PW], I32)
    nn_f = A('nn_f', [128, PW], F32)
    inv = A('inv', [128, 1], F32)
    kb.dma('pool', inv, c.din('rope_inv_pc', [128, 1]), writes=['inv'])
    pos_in = c.din('pos_bc', [128, S], I32)
    for p0 in range(0, S, PW):
        kb.dma('pool', pi, pos_in[:, p0:p0 + PW], writes=['pi'])
        kb.op('dve', lambda e: e.tensor_copy(ang, pi), reads=['pi'], writes=['ang'])
        kb.op('dve', lambda e: e.tensor_scalar(ang, ang, inv[:, 0:1], None, ALU.mult), reads=['ang', 'inv'], writes=['ang'])
        for which, off, dstD in (('sin', 0.5, sinD), ('cos', 0.75, cosD)):
            kb.op('dve', lambda e, off=off: e.tensor_scalar(uu, ang, 1.0 / (2 * np.pi), off, ALU.mult, ALU.add),
                  reads=['ang'], writes=['uu'])
            kb.op('dve', lambda e: e.tensor_copy(nn_i, uu), reads=['uu'], writes=['nn_i'])
            kb.op('dve', lambda e: e.tensor_copy(nn_f, nn_i), reads=['nn_i'], writes=['nn_f'])
            kb.op('dve', lambda e: e.tensor_tensor(uu, uu, nn_f, ALU.subtract), reads=['uu', 'nn_f'], writes=['uu'])
            kb.op('dve', lambda e: e.tensor_scalar(nn_f, uu, 0.0, None, ALU.is_lt), reads=['uu'], writes=['nn_f'])
            kb.op('dve', lambda e: e.tensor_tensor(uu, uu, nn_f, ALU.add), reads=['uu', 'nn_f'], writes=['uu'])
            kb.op('act', lambda e: e.activation(uu, uu, AF.Sin, scale=2 * np.pi, bias=-np.pi), reads=['uu'], writes=['uu'])
            kb.dma('pool', dstD[:, p0:p0 + PW], uu, reads=['uu'], writes=['ropeD'])

    c.areset()
    c.rows2()
    cosT = A('cosT', [128, TT], F32)
    sinT = A('sinT', [128, TT], F32)
    NRB = 3
    xsqs = [A('xsq%d' % i, [128, TT], BF16) for i in range(NRB)]
    rsbs = [A('rsb%d' % i, [128, TT], F32) for i in range(NRB)]
    xns = [A('xn%d' % i, [128, TT], BF16) for i in range(NRB)]
    t1s = [A('t1_%d' % i, [128, TT], F32) for i in range(NRB)]
    t2s = [A('t2_%d' % i, [128, TT], F32) for i in range(NRB)]
    ob = [A('ob%d' % i, [128, TT], BF16) for i in range(4)]
    obr = [0]
    gsb = [A('gsb%d' % i, [128, 96], F32) for i in range(2)]
    prot = A('prot', [128, 128], BF16)
    protf = A('protf', [128, 128], F32)
    qkn = A('qkn', [128, 2], F32)
    wgl = A('wgl', [128, KD, 96], BF16)
    kb.dma('pool', protf, c.din('rope_prot', [128, 128]), writes=['protf'])
    kb.op('dve', lambda e: e.tensor_copy(prot, protf), reads=['protf'], writes=['prot'])
    kb.dma('pool', qkn[:, 0:1], c.din('l%d_qnorm_pc' % l, [128, 1]), writes=['qkn'])
    kb.dma('pool', qkn[:, 1:2], c.din('l%d_knorm_pc' % l, [128, 1]), writes=['qkn'])
    kb.dma('sp', wgl, win['gl'][0][0, :, :, 0:96], reads=[('wscr', wname, 'gl', 0)], writes=['wgl'])

    def next_ob():
        i = obr[0]
        obr[0] = (i + 1) % 4
        return ob[i], ('ob', i)

    def nr_stage1(job, bi):
        pb, pk, gcol, dst_ap = job
        xsq = xsqs[bi]
        kb.op('act', lambda e: e.activation(xsq, pb[:, :], AF.Square), reads=[pk], writes=[('xsq', bi)])

    def nr_stage2(job, bi):
        pb, pk, gcol, dst_ap = job
        xsq, rsb, xn = xsqs[bi], rsbs[bi], xns[bi]
        p2, k2 = next_bank(c)
        kb.op('pe', lambda e: e.matmul(p2[:, :], c.ones_bf[:], xsq, start=True, stop=True), reads=['ones_bf', ('xsq', bi)], writes=[k2])
        kb.op('act', lambda e: e.activation(rsb, p2[:, :], AF.Ln, scale=1.0 / 128, bias=EPS), reads=[k2], writes=[('rsb', bi)])
        kb.op('act', lambda e: e.activation(rsb, rsb, AF.Exp, scale=-0.5), reads=[('rsb', bi)], writes=[('rsb', bi)])
        kb.op('dve', lambda e: e.scalar_tensor_tensor(xn, pb[:, :], qkn[:, gcol:gcol + 1], rsb, ALU.mult, ALU.mult),
              reads=[pk, 'qkn', ('rsb', bi)], writes=[('xn', bi)])

    def nr_stage3(job, bi):
        pb, pk, gcol, dst_ap = job
        xn, t1, t2 = xns[bi], t1s[bi], t2s[bi]
        p3, k3 = next_bank(c)
        kb.op('pe', lambda e: e.matmul(p3[:, :], prot, xn, start=True, stop=True), reads=['prot', ('xn', bi)], writes=[k3])
        kb.op('dve', lambda e: e.tensor_tensor(t1, xn, cosT, ALU.mult), reads=[('xn', bi), 'cosT'], writes=[('t1', bi)])
        kb.op('dve', lambda e: e.tensor_tensor(t2, p3[:, :], sinT, ALU.mult), reads=[k3, 'sinT'], writes=[('t2', bi)])
        o, ok = next_ob()
        kb.op('pool', lambda e: e.tensor_tensor(o, t1, t2, ALU.add), reads=[('t1', bi), ('t2', bi)], writes=[ok])
        kb.dma('pool', dst_ap, o, reads=[ok], writes=['nsaD'])

    def norm_rope_pipeline(jobgen):
        jobs = []
        done = False
        i = 0
        while True:
            if not done:
                try:
                    jobs.append(next(jobgen))
                    nr_stage1(jobs[i], i % NRB)
                except StopIteration:
                    done = True
            if 0 <= i - 1 < len(jobs):
                nr_stage2(jobs[i - 1], (i - 1) % NRB)
            if 0 <= i - 2 < len(jobs):
                nr_stage3(jobs[i - 2], (i - 2) % NRB)
            i += 1
            if done and i - 2 >= len(jobs):
                break

    for tt in range(c.NT):
        tsl = slice(tt * TT, (tt + 1) * TT)
        load_norm_transpose(c, x_src, tt)
        kb.dma('pool', cosT, cosD[:, tsl], reads=['ropeD'], writes=['cosT'])
        kb.dma('pool', sinT, sinD[:, tsl], reads=['ropeD'], writes=['sinT'])
        def jobgen(tsl=tsl):
            for ci, m, pb, pk in proj_fm(c, win, wname, 'q'):
                yield (pb, pk, 0, QT[ci, :, tsl])
            for ki, sname in enumerate(('kc', 'ks', 'kw')):
                for ci, m, pb, pk in proj_fm(c, win, wname, sname):
                    yield (pb, pk, 1, KT[ki, ci, :, tsl])
        norm_rope_pipeline(jobgen())
        for ci, m, pb, pk in proj_fm(c, win, wname, 'vc'):
            o, ok = next_ob()
            kb.op('act', lambda e, o=o, pb=pb: e.copy(o, pb[:, :]), reads=[pk], writes=[ok])
            kb.dma('pool', VcT[ci, :, tsl], o, reads=[ok], writes=['nsaD'])
        for ci, m, pb, pk in proj_fm(c, win, wname, 'gate'):
            o, ok = next_ob()
            kb.op('act', lambda e, o=o, pb=pb: e.activation(o, pb[:, :], AF.Silu), reads=[pk], writes=[ok])
            kb.dma('pool', SG[ci, :, tsl], o, reads=[ok], writes=['nsaD'])
        for vi, sname in enumerate(('vs', 'vw')):
            wt, wk = load_w_dep(c, win[sname][0][0, :, :, :], ('wscr', wname, sname, 0))
            for s4 in range(4):
                pb, pk = next_bank(c)
                for k in range(KD):
                    kb.op('pe', lambda e, pb=pb, wt=wt, k=k, s4=s4: e.matmul(
                        pb[:, :], c.hT[:, k, s4 * 128:(s4 + 1) * 128], wt[:, k, :], start=(k == 0), stop=(k == KD - 1)),
                        reads=[wk, ('hT', s4)], writes=[pk])
                o, ok = next_ob()
                kb.op('act', lambda e, o=o, pb=pb: e.copy(o, pb[:, :]), reads=[pk], writes=[ok])
                r0 = tt * TT + s4 * 128
                kb.dma('pool', Vsw[vi, r0:r0 + 128, :], o, reads=[ok], writes=['nsaD'])
        for s4 in range(4):
            pb, pk = next_bank(c)
            for k in range(KD):
                kb.op('pe', lambda e, pb=pb, k=k, s4=s4: e.matmul(
                    pb[:, 0:96], c.hT[:, k, s4 * 128:(s4 + 1) * 128], wgl[:, k, :], start=(k == 0), stop=(k == KD - 1)),
                    reads=['wgl', ('hT', s4)], writes=[pk])
            gi = s4 % 2
            kb.op('act', lambda e, pb=pb, gi=gi: e.activation(gsb[gi], pb[:, 0:96], AF.Sigmoid), reads=[pk], writes=[('gsb', gi)])
            r0 = tt * TT + s4 * 128
            kb.dma('pool', Gs[r0:r0 + 128, :], gsb[gi], reads=[('gsb', gi)], writes=['nsaD'])

    c.areset()
    kcT = A('kcT', [128, 4, NCP], BF16)
    vcm = A('vcm', [128, 4, NCT, 128], BF16)
    KsT = A('KsT', [128, S], BF16)
    KwT = A('KwT', [128, S], BF16)
    Vs = A('Vs', [128, NQT, 128], BF16)
    Vw = A('Vw', [128, NQT, 128], BF16)
    Efull = A('Efull', [64, S], BF16)
    Eff = None
    aggf = A('aggf', [128, NCT, NSEL], F32)
    agg = A('agg', [128, NCT, NSEL], BF16)
    onesf = A('onesf', [128, 128], F32)
    qT = A('qT', [128, 8, 128], BF16)
    PT = [A('PT%d' % i, [128, 1024], BF16) for i in range(2)]
    PcDg = A('PcDg', [128, 1024], F32)
    Pc = PcDg.bitcast(BF16).rearrange("p (a b) -> p a b", a=2)
    Rt = A('Rt', [128, 1024], F32)
    acc = A('acc', [128, 1024], F32)
    Dg = PcDg.bitcast(BF16)[:, 0:1024]
    DGK = [('Pc', 0)]
    sgt = A('sgt', [128, 8, 128], BF16)
    yo = A('yo', [128, 8, 128], BF16)
    gat = A('gat', [128, 96], F32)
    tabs = A('tabs', [128, 2, NSEL], F32)
    sc1 = A('sc1', [128, NSEL], F32)
    sc2 = A('sc2', [128, NSEL], F32)
    m8 = A('m8', [128, 16], F32)
    selb = A('selb', [128, 64], BF16)
    selT4 = A('selT4', [64, 512], BF16)
    hTc = A('hTc', [128, NCP], BF16)
    cj = A('cj', [128, 2], F32)
    posT = A('posT', [128, 32], BF16)
    posTf = A('posTf', [128, 32], F32)
    w2t = A('w2t', [128, 128], BF16)
    kb.op('dve', lambda e: e.memset(onesf, 1.0), writes=['onesf'])
    kb.op('dve', lambda e: e.memset(selb, 0.0), writes=['selb'])
    kb.dma('pool', aggf, c.din('nsa_agg', [128, NCT, NSEL]), writes=['aggf'])
    kb.op('dve', lambda e: e.tensor_copy(agg, aggf), reads=['aggf'], writes=['agg'])
    kb.dma('pool', posTf, c.din('l%d_cmp_posT' % l, [128, 32]), writes=['posTf'])
    kb.op('dve', lambda e: e.tensor_copy(posT, posTf), reads=['posTf'], writes=['posT'])
    ef_in = c.din('nsa_E', [64, S])
    for e0 in range(0, S, 1024):
        ew = min(1024, S - e0)
        kb.dma('pool', Rt[0:64, 0:ew], ef_in[:, e0:e0 + ew], writes=['Rt'])
        kb.op('dve', lambda e, e0=e0, ew=ew: e.tensor_copy(Efull[:, e0:e0 + ew], Rt[0:64, 0:ew]), reads=['Rt'], writes=['Efull'])
    w1t = KwT[:, 0:4096].rearrange("p (t j) -> p t j", t=32) if S >= 4096 else None
    if w1t is None:
        w1t = A('w1t', [128, 32, 128], BF16)
        w1key = 'w1t'
    else:
        w1key = 'KwT'
    for kind, (w1s, w2s, srcT) in enumerate(((w1k, w2k, None), (w1v, w2v, None))):
        kb.dma('sp', w1t, w1s['all'][0][0, :, :, 0:128], reads=[('wscr', 'w1k' if kind == 0 else 'w1v', 'all', 0)], writes=[w1key])
        kb.dma('sp', w2t, w2s['all'][0][0, :, 0, 0:128], reads=[('wscr', 'w2k' if kind == 0 else 'w2v', 'all', 0)], writes=['w2t'])
        pcj, kcj = next_bank(c)
        for tau in range(32):
            kb.op('pe', lambda e, tau=tau, pcj=pcj: e.matmul(pcj[:, 0:1], w1t[:, tau, :], posT[:, tau:tau + 1],
                                                              start=(tau == 0), stop=(tau == 31)),
                  reads=[w1key, 'posT'], writes=[kcj])
        kb.op('act', lambda e, pcj=pcj, kind=kind: e.copy(cj[:, kind:kind + 1], pcj[:, 0:1]), reads=[kcj], writes=['cj'])
        for hk in range(4):
            srcD = KT[0, hk] if kind == 0 else VcT[hk]
            kb.dma('pool', KsT, srcD, writes=['KsT'])
            pp, pk = next_bank(c)
            for tau in range(32):
                kb.op('pe', lambda e, tau=tau, pp=pp: e.matmul(pp[:, 0:NC], w1t[:, tau, :], KsT[:, tau:tau + 16 * (NC - 1) + 1:16],
                                                                start=(tau == 0), stop=(tau == 31)),
                      reads=[w1key, 'KsT'], writes=[pk])
            kb.op('dve', lambda e: e.memset(hTc, 0.0), writes=['hTc'])
            kb.op('act', lambda e, pp=pp, kind=kind: e.activation(hTc[:, 0:NC], pp[:, 0:NC], AF.Silu, bias=cj[:, kind:kind + 1]),
                  reads=[pk, 'cj'], writes=['hTc'])
            if kind == 0:
                p2, k2 = next_bank(c)
                kb.op('pe', lambda e, p2=p2: e.matmul(p2[:, 0:NCP], w2t, hTc, start=True, stop=True), reads=['w2t', 'hTc'], writes=[k2])
                kb.op('act', lambda e, p2=p2, hk=hk: e.copy(kcT[:, hk, :], p2[:, 0:NCP]), reads=[k2], writes=['kcT'])
            else:
                for ct in range(NCT):
                    p2, k2 = next_bank(c)
                    kb.op('pe', lambda e, p2=p2, ct=ct: e.matmul(p2[:, 0:128], hTc[:, ct * 128:(ct + 1) * 128], w2t, start=True, stop=True),
                          reads=['w2t', 'hTc'], writes=[k2])
                    kb.op('act', lambda e, p2=p2, hk=hk, ct=ct: e.copy(vcm[:, hk, ct, :], p2[:, 0:128]), reads=[k2], writes=['vcm'])

    def exp_tile(sbanks, P):
        for hf in range(2):
            pb, pk = sbanks[hf]
            kb.op('act', lambda e, pb=pb, hf=hf, P=P: e.activation(P[0][:, hf * 512:(hf + 1) * 512], pb[:, :], AF.Exp, scale=NSA_SCALE),
                  reads=[pk], writes=[P[1]])

    def score_tile(lhsT, lkey):
        sb2 = []
        for hf in range(2):
            pb, pk = next_bank(c)
            kb.op('pe', lambda e, pb=pb, hf=hf: e.matmul(pb[:, :], lhsT, qT[:, hf * 4:(hf + 1) * 4, :], start=True, stop=True),
                  reads=[lkey, 'qT'], writes=[pk])
            sb2.append((pb, pk))
        return sb2

    def amask(P, base, cm, qstep):
        pv = P[0].rearrange("p (h q) -> p h q", h=8)
        kb.op('pool', lambda e: e.affine_select(out=pv, in_=pv, pattern=[[0, 8], [qstep, 128]], compare_op=ALU.is_ge,
                                                fill=0.0, base=base, channel_multiplier=cm),
              reads=[P[1]], writes=[P[1]])

    def gate_bc(hk, br):
        c0 = hk * 24 + br
        kb.op('dve', lambda e, c0=c0: e.tensor_tensor(
            Dg.rearrange("p (h q) -> p h q", h=8), c.ident_f[:].unsqueeze(1).to_broadcast([128, 8, 128]),
            gat[:, c0:c0 + 22:3].unsqueeze(2).to_broadcast([128, 8, 128]), ALU.mult),
            reads=['ident_f', 'gat'], writes=DGK)
        gb = []
        for hf in range(2):
            pb, pk = next_bank(c)
            kb.op('pe', lambda e, pb=pb, hf=hf: e.matmul(pb[:, :], c.ones_bf[:], Dg[:, hf * 512:(hf + 1) * 512], start=True, stop=True),
                  reads=['ones_bf'] + DGK, writes=[pk])
            gb.append((pb, pk))
        return gb

    for hk in range(4):
        kb.dma('pool', KsT, KT[1, hk], writes=['KsT'])
        kb.dma('pool', KwT, KT[2, hk], writes=['KwT'])
        kb.dma('pool', Vs, Vsw[0, :, hk * 128:(hk + 1) * 128].rearrange("(t k) d -> k t d", k=128), writes=['Vs'])
        kb.dma('pool', Vw, Vsw[1, :, hk * 128:(hk + 1) * 128].rearrange("(t k) d -> k t d", k=128), writes=['Vw'])
        for qt in range(NQT):
            qsl = slice(qt * 128, (qt + 1) * 128)
            kb.dma('pool', qT, QT[hk * 8:(hk + 1) * 8, :, qsl].rearrange("h d q -> d h q"), writes=['qT'])
            kb.dma('pool', sgt, SG[hk * 8:(hk + 1) * 8, :, qsl].rearrange("h d q -> d h q"), writes=['sgt'])
            kb.dma('pool', gat, Gs[qsl, :], writes=['gat'])
            kb.dma('pool', tabs, c.din_cached('nsa_tabs', [NQT, 128, 2, NSEL])[qt], writes=['tabs'])
            cts = []
            for ct in range(NCT):
                base = 128 * qt - 2048 * ct - 31
                if base + 127 < 0:
                    continue
                cts.append((ct, base))
            rsb_ = [reserve_bank(c), reserve_bank(c)]
            for n_, (ct, base) in enumerate(cts):
                sb2 = score_tile(kcT[:, hk, ct * 128:(ct + 1) * 128], 'kcT')
                P = (Pc[:, ct, :], ('Pc', ct))
                exp_tile(sb2, P)
                if base - 2032 < 0:
                    amask(P, base, -16, 1)
                for hf in range(2):
                    rb, rk, _ = rsb_[hf]
                    kb.op('pe', lambda e, rb=rb, hf=hf, P=P, n_=n_, nl=len(cts): e.matmul(rb[:, :], c.ones_bf[:], P[0][:, hf * 512:(hf + 1) * 512],
                                                                            start=(n_ == 0), stop=(n_ == nl - 1)),
                          reads=['ones_bf', P[1]], writes=[rk])
            if cts:
                for hf in range(2):
                    rb, rk, ridx = rsb_[hf]
                    hs = slice(hf * 512, (hf + 1) * 512)
                    kb.op('act', lambda e, rb=rb, hs=hs: e.activation(Rt[:, hs], rb[:, :], AF.Ln, bias=1e-30), reads=[rk], writes=['Rt'])
                    kb.op('act', lambda e, hs=hs: e.activation(Rt[:, hs], Rt[:, hs], AF.Exp, scale=-1.0), reads=['Rt'], writes=['Rt'])
                    for (ct, base) in cts:
                        kb.op('dve', lambda e, hs=hs, ct=ct: e.tensor_tensor(Pc[:, ct, hs], Pc[:, ct, hs], Rt[:, hs], ALU.mult),
                              reads=['Rt', ('Pc', ct)], writes=[('Pc', ct)])
            for hf in range(2):
                release_bank(c, rsb_[hf][2])
            if cts:
                ib, ik = next_bank(c)
                nmm = 8 * len(cts)
                i_ = 0
                for (ct, base) in cts:
                    for h in range(8):
                        kb.op('pe', lambda e, ib=ib, ct=ct, h=h, i_=i_, nmm=nmm: e.matmul(ib[:, 0:NSEL], Pc[:, ct, h * 128:(h + 1) * 128], agg[:, ct, :],
                                                                                 start=(i_ == 0), stop=(i_ == nmm - 1)),
                              reads=[('Pc', ct), 'agg'], writes=[ik])
                        i_ += 1
                kb.op('dve', lambda e, ib=ib: e.tensor_tensor(sc1, ib[:, 0:NSEL], tabs[:, 0, :], ALU.mult), reads=[ik, 'tabs'], writes=['sc1'])
                kb.op('dve', lambda e: e.tensor_tensor(sc1, sc1, tabs[:, 1, :], ALU.add), reads=['sc1', 'tabs'], writes=['sc1'])
            else:
                kb.op('dve', lambda e: e.tensor_copy(sc1, tabs[:, 1, :]), reads=['tabs'], writes=['sc1'])
            kb.op('dve', lambda e: e.max(out=m8[:, 0:8], in_=sc1), reads=['sc1'], writes=['m8'])
            kb.op('dve', lambda e: e.match_replace(out=sc2, in_to_replace=m8[:, 0:8], in_values=sc1, imm_value=-1e9),
                  reads=['sc1', 'm8'], writes=['sc2'])
            kb.op('dve', lambda e: e.max(out=m8[:, 8:16], in_=sc2), reads=['sc2'], writes=['m8'])
            kb.op('dve', lambda e: e.tensor_scalar(sc2, sc1, m8[:, 15:16], None, ALU.is_ge), reads=['sc1', 'm8'], writes=['sc2'])
            kb.op('dve', lambda e: e.tensor_tensor(selb[:, 0:NSEL], sc2, tabs[:, 0, :], ALU.mult), reads=['sc2', 'tabs'], writes=['selb'])
            if 'dbg_sel' in DEBUG_OUT:
                if not hasattr(c, 'dbg_sel'):
                    c.dbg_sel = c.dscr('dbg_sel', [4, NQT, 128, NSEL], F32)
                    c.dbg_sc1 = c.dscr('dbg_sc1', [4, NQT, 128, NSEL], F32)
                kb.dma('pool', c.dbg_sel[hk, qt], sc2, reads=['sc2'], writes=['dbgsel'])
                kb.dma('pool', c.dbg_sc1[hk, qt], sc1, reads=['sc1'], writes=['dbgsel'])
            tb, tk = next_bank(c)
            tbv = tb[:].bitcast(BF16)
            kb.op('pe', lambda e, tbv=tbv: e.transpose(tbv[0:64, 0:128], selb[:, 0:64], c.ident[:]), reads=['selb', 'ident'], writes=[tk])
            for r4 in range(4):
                kb.op('dve', lambda e, tbv=tbv, r4=r4: e.tensor_scalar(selT4[:, r4 * 128:(r4 + 1) * 128], tbv[0:64, 0:128], 1.0, 29952.0,
                                                                      ALU.subtract, ALU.mult), reads=[tk], writes=['selT4'])
            if cts:
                ob2 = []
                for hf in range(2):
                    pb, pk = next_bank(c)
                    for n_, (ct, base) in enumerate(cts):
                        kb.op('pe', lambda e, pb=pb, hf=hf, ct=ct, n_=n_, hk=hk, nl=len(cts): e.matmul(pb[:, :], vcm[:, hk, ct, :], Pc[:, ct, hf * 512:(hf + 1) * 512],
                                                                                  start=(n_ == 0), stop=(n_ == nl - 1)),
                              reads=['vcm', ('Pc', ct)], writes=[pk])
                    ob2.append((pb, pk))
                gb = gate_bc(hk, 0)
                for hf in range(2):
                    hs = slice(hf * 512, (hf + 1) * 512)
                    kb.op('act', lambda e, hf=hf, hs=hs, gb=gb: e.copy(Rt[:, hs], gb[hf][0][:, :]), reads=[gb[hf][1]], writes=['Rt'])
                    kb.op('dve', lambda e, hf=hf, hs=hs, ob2=ob2: e.tensor_tensor(acc[:, hs], Rt[:, hs], ob2[hf][0][:, :], ALU.mult),
                          reads=['Rt', ob2[hf][1]], writes=['acc'])
                if 0 not in NSA_BR:
                    kb.op('dve', lambda e: e.memset(acc, 0.0), writes=['acc'])
            else:
                kb.op('dve', lambda e: e.memset(acc, 0.0), writes=['acc'])
            for br in (1, 2):
                if br == 1:
                    kts = list(range(0, qt + 1))
                    Ksrc, Kkey, Vsrc, Vkey = KsT, 'KsT', Vs, 'Vs'
                else:
                    kts = list(range(max(0, qt - 4), qt + 1))
                    Ksrc, Kkey, Vsrc, Vkey = KwT, 'KwT', Vw, 'Vw'
                obk = [reserve_bank(c), reserve_bank(c)]
                rbk = [reserve_bank(c), reserve_bank(c)]
                def flush(pend, nl):
                    P, kt, n_ = pend
                    for hf in range(2):
                        hs = slice(hf * 512, (hf + 1) * 512)
                        pb, pk, _ = obk[hf]
                        kb.op('pe', lambda e, pb=pb, kt=kt, hs=hs, P=P, n_=n_, Vsrc=Vsrc, nl=nl: e.matmul(pb[:, :], Vsrc[:, kt, :], P[0][:, hs],
                                                                                               start=(n_ == 0), stop=(n_ == nl - 1)),
                              reads=[Vkey, P[1]], writes=[pk])
                        rb, rk, _ = rbk[hf]
                        kb.op('pe', lambda e, rb=rb, hs=hs, P=P, n_=n_, nl=nl: e.matmul(rb[:, :], c.ones_bf[:], P[0][:, hs],
                                                                                    start=(n_ == 0), stop=(n_ == nl - 1)),
                              reads=['ones_bf', P[1]], writes=[rk])
                pend = None
                for n_, kt in enumerate(kts):
                    ksl = slice(kt * 128, (kt + 1) * 128)
                    sb2 = []
                    for hf in range(2):
                        pb, pk = next_bank(c)
                        kb.op('pe', lambda e, pb=pb, hf=hf, ksl=ksl, Ksrc=Ksrc, br=br: e.matmul(
                            pb[:, :], Ksrc[:, ksl], qT[:, hf * 4:(hf + 1) * 4, :], start=True, stop=(br != 1)),
                            reads=[Kkey, 'qT'], writes=[pk])
                        if br == 1:
                            kb.op('pe', lambda e, pb=pb, ksl=ksl: e.matmul(pb[:, :], Efull[:, ksl], selT4, start=False, stop=True),
                                  reads=['Efull', 'selT4'], writes=[pk])
                        sb2.append((pb, pk))
                    pi_ = n_ % 2
                    P = (PT[pi_], ('PT', pi_))
                    exp_tile(sb2, P)
                    if kt == qt:
                        amask(P, 0, -1, 1)
                    if br == 2 and kt == qt - 4:
                        amask(P, -1, 1, -1)
                    if pend is not None:
                        flush(pend, len(kts))
                    pend = (P, kt, n_)
                flush(pend, len(kts))
                gb = gate_bc(hk, br)
                for hf in range(2):
                    hs = slice(hf * 512, (hf + 1) * 512)
                    pb, pk, oidx = obk[hf]
                    rb, rk, ridx = rbk[hf]
                    kb.op('act', lambda e, rb=rb, hs=hs: e.activation(Rt[:, hs], rb[:, :], AF.Ln), reads=[rk], writes=['Rt'])
                    kb.op('act', lambda e, hs=hs: e.activation(Rt[:, hs], Rt[:, hs], AF.Exp, scale=-1.0), reads=['Rt'], writes=['Rt'])
                    kb.op('dve', lambda e, hs=hs, hf=hf, gb=gb: e.tensor_tensor(Rt[:, hs], Rt[:, hs], gb[hf][0][:, :], ALU.mult),
                          reads=['Rt', gb[hf][1]], writes=['Rt'])
                    kb.op('dve', lambda e, pb=pb, hs=hs: e.tensor_tensor(Rt[:, hs], Rt[:, hs], pb[:, :], ALU.mult),
                          reads=['Rt', pk], writes=['Rt'])
                    if br in NSA_BR:
                        kb.op('dve', lambda e, hs=hs: e.tensor_tensor(acc[:, hs], acc[:, hs], Rt[:, hs], ALU.add),
                              reads=['Rt', 'acc'], writes=['acc'])
                    release_bank(c, oidx)
                    release_bank(c, ridx)
            kb.op('dve', lambda e: e.tensor_tensor(yo, acc[:, :].rearrange("p (h q) -> p h q", h=8), sgt, ALU.mult),
                  reads=['acc', 'sgt'], writes=['yo'])
            kb.dma('pool', Y[hk * 8:(hk + 1) * 8, :, qsl].rearrange("h d q -> d h q"), yo, reads=['yo'], writes=['nsaY'])

    c.areset()
    c.rows2()
    for tt in range(c.NT):
        tsl = slice(tt * TT, (tt + 1) * TT)
        if tt == 0:
            load_norm_transpose(c, x_src, tt)
        kb.dma('pool', c.yT[:, 0:32, :], Y[:, :, tsl].rearrange("h d t -> d h t"), reads=['nsaY'], writes=[('yT', f) for f in range(32)])
        mem_branch(c, l, win)
        wout_phase(c, l, wo, x_src, dst, tt, nxt=(x_src if tt + 1 < c.NT else None))


LAYER_FNS = {0: layer0, 1: layer1, 2: layer2, 3: layer3}


def _pc(v, nchunk):
    return np.ascontiguousarray(np.asarray(v, np.float32).reshape(nchunk, 128).T)


def _bc(v):
    v = np.asarray(v, np.float32).reshape(1, -1)
    return np.ascontiguousarray(np.broadcast_to(v, (128, v.shape[1])))


def make_in_map(inp, b, S, layers):
    m = {}
    m['x'] = np.ascontiguousarray(inp['x'][b, :S])
    m['mem'] = np.ascontiguousarray(inp['mem'][b])
    m['ident'] = np.eye(128, dtype=np.float32)
    m['mem_norm_pc'] = _pc(inp['mem_norm'], KD)
    for l in layers:
        p = 'l%d_' % l
        m[p + 'norm_pc'] = _pc(inp[p + 'norm'], KD)
        m[p + 'w_in'] = inp[p + 'w_in']
        m[p + 'w_out'] = inp[p + 'w_out']
        m[p + 'mem_wkv'] = inp[p + 'mem_wkv']
        m[p + 'mem_qnorm_pc'] = _pc(inp[p + 'mem_qnorm'], 2)
        m[p + 'mem_knorm_pc'] = _pc(inp[p + 'mem_knorm'], 2)
        if l == 0:
            m[p + 'pool_w'] = inp[p + 'pool_w']
            m[p + 'pool_scale_pc'] = _pc(inp[p + 'pool_scale'], 32)
            t = np.arange(TT)
            rc = np.stack([1.0 / np.minimum(t + 1, w) for w in POOL_W]).astype(np.float32)
            m['pool_rc0'] = np.ascontiguousarray(np.broadcast_to(rc[None], (128, 4, TT)))
        if l == 1:
            m.update(nsa_host_tables(S))
            m['pos_bc'] = np.ascontiguousarray(np.broadcast_to(inp['positions'][b, :S][None].astype(np.int32), (128, S)))
            m[p + 'qnorm_pc'] = _pc(inp[p + 'qnorm'], 1)
            m[p + 'knorm_pc'] = _pc(inp[p + 'knorm'], 1)
            m[p + 'cmp_posT'] = np.ascontiguousarray(inp[p + 'cmp_pos'].T)
            for nm in ('cmp_k_w1', 'cmp_k_w2', 'cmp_v_w1', 'cmp_v_w2'):
                m[p + nm] = inp[p + nm]
        if l == 2:
            m[p + 'conv_w_pc'] = np.ascontiguousarray(inp[p + 'conv_w'].reshape(4, 48, 128).transpose(2, 1, 0))
            m[p + 'conv_b_pc'] = _pc(inp[p + 'conv_b'], 48)
            m[p + 'gnorm_pc'] = _pc(inp[p + 'gnorm'], 32)
            m[p + 'dt_bias_bc'] = _bc(inp[p + 'dt_bias'])
            m[p + 'A_log_bc'] = _bc(inp[p + 'A_log'])
            m[p + 'D_bc'] = _bc(inp[p + 'D'])
            m['trilT'] = np.triu(np.ones((128, 128), np.float32))
            s8 = np.zeros((8, 8, 128), np.float32)
            for h in range(8):
                s8[h, h, :] = 1.0
            m['sel8'] = s8
        if l == 3:
            m[p + 'sgu_wT'] = np.ascontiguousarray(np.transpose(inp[p + 'sgu_w'], (2, 0, 1)))
            m[p + 'sgu_b_bc'] = np.ascontiguousarray(np.broadcast_to(inp[p + 'sgu_b'][None], (128, 8, 128)))
            m[p + 'v_norm_pc'] = _pc(inp[p + 'v_norm'], 32)
            m['trilT'] = np.triu(np.ones((128, 128), np.float32))
    return m


def nsa_host_tables(S):
    NQT, NSEL, NC = S // 128, S // 64, S // 16 - 1
    NCT = (NC + 127) // 128
    t = {}
    inv = (np.float32(10000.0) ** (-np.arange(0, 128, 2, dtype=np.float32) / np.float32(128))).astype(np.float32)
    t['rope_inv_pc'] = np.ascontiguousarray(np.concatenate([inv, inv])[:, None])
    pr = np.zeros((128, 128), np.float32)
    for d in range(64):
        pr[d + 64, d] = -1.0
        pr[d, d + 64] = 1.0
    t['rope_prot'] = pr
    cs = np.arange(NCT * 128)
    cstart, cend = cs * 16, cs * 16 + 31
    sel0 = np.arange(NSEL) * 64
    agg = ((cstart[:, None] < sel0[None, :] + 64) & (cend[:, None] >= sel0[None, :]) & (cs[:, None] < NC)).astype(np.float32)
    t['nsa_agg'] = np.ascontiguousarray(agg.reshape(NCT, 128, NSEL).transpose(1, 0, 2))
    t['nsa_E'] = (np.arange(S)[None, :] // 64 == np.arange(64)[:, None]).astype(np.float32)
    tok = np.arange(S)
    bt = tok // 64
    j = np.arange(NSEL)[None, :]
    causal = (j <= bt[:, None])
    forced = (j == 0) | (j == bt[:, None]) | (j == bt[:, None] - 1)
    bias = np.where(forced, 1000.0, np.where(causal, 0.0, -1.0)).astype(np.float32)
    tabs = np.stack([causal.astype(np.float32), bias], axis=1)
    t['nsa_tabs'] = np.ascontiguousarray(tabs.reshape(NQT, 128, 2, NSEL))
    return t


_PROG = {}


_INPUT_NAMES = (
    'x', 'mem', 'positions', 'mem_norm',
    'l0_norm', 'l0_w_in', 'l0_w_out', 'l0_mem_wkv', 'l0_mem_qnorm', 'l0_mem_knorm', 'l0_pool_w', 'l0_pool_scale',
    'l1_norm', 'l1_w_in', 'l1_w_out', 'l1_mem_wkv', 'l1_mem_qnorm', 'l1_mem_knorm', 'l1_qnorm', 'l1_knorm',
    'l1_cmp_pos', 'l1_cmp_k_w1', 'l1_cmp_k_w2', 'l1_cmp_v_w1', 'l1_cmp_v_w2',
    'l2_norm', 'l2_w_in', 'l2_w_out', 'l2_mem_wkv', 'l2_mem_qnorm', 'l2_mem_knorm', 'l2_conv_w', 'l2_conv_b',
    'l2_dt_bias', 'l2_A_log', 'l2_D', 'l2_gnorm',
    'l3_norm', 'l3_w_in', 'l3_w_out', 'l3_mem_wkv', 'l3_mem_qnorm', 'l3_mem_knorm', 'l3_v_norm', 'l3_sgu_w', 'l3_sgu_b',
)


def kernel(**inputs):
    missing = [n for n in _INPUT_NAMES if n not in inputs]
    assert not missing, missing
    S = inputs['x'].shape[1]
    B = inputs['x'].shape[0]
    layers = [0, 1, 2, 3]
    if S not in _PROG:
        _PROG[S] = build_program(S, layers)
    nc, _es = _PROG[S]
    inp = {k: np.asarray(v) for k, v in inputs.items()}
    in_maps = [make_in_map(inp, b, S, layers) for b in range(B)]
    res = run_bass_kernel_spmd(nc, in_maps, core_ids=list(range(B)))
    out = np.stack([np.asarray(r['out']) for r in res.results], axis=0)
    return out.astype(np.float32)
```
